# Optimizing a Trainium2 kernel written in Bass

```python
import math
import jax
import jax.numpy as jnp
from jax import lax
import numpy as np

D_MODEL = 1024
BATCH = 4
SEQ = 8192
DEPTH = 4

GRID_W = 64
CTX_LEN = 256
N_MIXERS = 3
EPS = 1e-6
N_MOD = 9
D_FF = 2816
MACARON_W = 0.5
D_RNN = 1280
LRU_BLOCKS = 8
LRU_BLK = D_RNN // LRU_BLOCKS
CONV_W = 4
CONV_PAD_LEFT = 1
LRU_C = 8.0
LRU_A_MIN = 0.9
LRU_A_MAX = 0.999
RET_HEADS = 4
RET_DK = D_MODEL // RET_HEADS
RET_DV = 2 * RET_DK
RET_CHUNK = 128
RET_THETA_BASE = 10000.0
DIFF_HEADS = 8
DIFF_DH = D_MODEL // DIFF_HEADS // 2
DIFF_DV = 2 * DIFF_DH
Q_BLOCK = 128
ROPE_BASE = 10000.0

kernel_name = 'hybrid_rglru_retention_diffattn_macaron_dit'


def _rms(x):
    xf = x.astype(jnp.float32)
    return xf * lax.rsqrt(jnp.mean(xf * xf, axis=-1, keepdims=True) + EPS)


def rms_norm(x, g):
    return (_rms(x) * g.astype(jnp.float32)).astype(x.dtype)


def modulate(h, shift, scale):
    return h * (1 + scale) + shift


def ada_mod(cond, w, b):
    m = jax.nn.silu(cond) @ w + b
    return jnp.split(m[..., None, :], N_MOD, axis=-1)


def swiglu(h, w_in, w_out):
    a, g = jnp.split(h @ w_in, 2, axis=-1)
    return (jax.nn.silu(a) * g) @ w_out


def apply_rope(x, cos, sin):
    x1, x2 = jnp.split(x.astype(jnp.float32), 2, axis=-1)
    c = cos[:, None, :]
    s = sin[:, None, :]
    return jnp.concatenate([x1 * c - x2 * s, x1 * s + x2 * c], axis=-1).astype(x.dtype)


def axial_rope_tables(rows, head_dim):
    n_freq = head_dim // 4
    inv = ROPE_BASE ** (-jnp.arange(n_freq, dtype=jnp.float32) / n_freq)
    row = jnp.broadcast_to(jnp.arange(rows, dtype=jnp.float32)[:, None], (rows, GRID_W)).reshape(-1)
    col = jnp.broadcast_to(jnp.arange(GRID_W, dtype=jnp.float32)[None, :], (rows, GRID_W)).reshape(-1)
    ang = jnp.concatenate([row[:, None] * inv, col[:, None] * inv], axis=-1)
    return jnp.cos(ang), jnp.sin(ang)


def retention_rope_tables(n_tokens):
    theta = RET_THETA_BASE ** (-jnp.linspace(0.0, 1.0, RET_DK // 2, dtype=jnp.float32))
    ang = jnp.arange(n_tokens, dtype=jnp.float32)[:, None] * theta
    return jnp.cos(ang), jnp.sin(ang)


def block_diag(x, w, b):
    bsz, t, _ = x.shape
    xg = x.reshape(bsz, t, LRU_BLOCKS, LRU_BLK)
    return (jnp.einsum('btgi,gij->btgj', xg, w) + b).reshape(bsz, t, D_RNN)


def lru_conv_branch(h, w_x, conv_w, conv_b):
    xb = h @ w_x
    xc = lax.conv_general_dilated(
        xb, conv_w[:, None, :].astype(xb.dtype), window_strides=(1,),
        padding=[(CONV_PAD_LEFT, CONV_W - 1 - CONV_PAD_LEFT)],
        dimension_numbers=('NWC', 'WIO', 'NWC'), feature_group_count=D_RNN)
    return xc + conv_b


def lru_coeffs(xc, gate_w, gate_b, lam):
    xf = xc.astype(jnp.float32)
    r = jax.nn.sigmoid(block_diag(xf, gate_w[0], gate_b[0]))
    ig = jax.nn.sigmoid(block_diag(xf, gate_w[1], gate_b[1]))
    log_a = -LRU_C * r * jax.nn.softplus(-lam.astype(jnp.float32))
    a = jnp.exp(log_a)
    b = jnp.sqrt(-jnp.expm1(2.0 * log_a)) * (ig * xf)
    return a, b


def _affine_combine(left, right):
    a_l, b_l = left
    a_r, b_r = right
    return a_l * a_r, a_r * b_l + b_r


def linear_scan(a, b, h0, reverse):
    if reverse:
        a, b = jnp.flip(a, 1), jnp.flip(b, 1)
    b = b.at[:, 0].add(a[:, 0] * h0)
    _, h = lax.associative_scan(_affine_combine, (a, b), axis=1)
    return jnp.flip(h, 1) if reverse else h


def mixer_rglru(hc, hx, w_in, conv_w, conv_b, gate_w, gate_b, lam, w_out, need_ctx_out):
    w_x, w_g = w_in[:, :D_RNN], w_in[:, D_RNN:]
    xc_c = lru_conv_branch(hc, w_x, conv_w, conv_b)
    xc_x = lru_conv_branch(hx, w_x, conv_w, conv_b)
    h0 = jnp.zeros((hc.shape[0], D_RNN), jnp.float32)
    hs_c, hs_x = [], []
    for d, rev in enumerate((False, True)):
        a_c, b_c = lru_coeffs(xc_c, gate_w[d], gate_b[d], lam[d])
        h_c = linear_scan(a_c, b_c, h0, rev)
        h_end = h_c[:, 0] if rev else h_c[:, -1]
        a_x, b_x = lru_coeffs(xc_x, gate_w[d], gate_b[d], lam[d])
        hs_x.append(linear_scan(a_x, b_x, h_end, rev))
        hs_c.append(h_c)

    def finish(h, hsum):
        gate = jax.nn.gelu((h @ w_g).astype(jnp.float32))
        return (hsum * gate).astype(h.dtype) @ w_out

    yx = finish(hx, hs_x[0] + hs_x[1])
    yc = finish(hc, hs_c[0] + hs_c[1]) if need_ctx_out else None
    return yc, yx


def retention_scan(q, k, v, log_gamma, s0, include_diag):
    bsz, t, heads, _ = q.shape
    dv = v.shape[-1]
    n = t // RET_CHUNK
    idx = jnp.arange(RET_CHUNK, dtype=jnp.float32)
    dist = idx[:, None] - idx[None, :]
    mask = (dist >= 0) if include_diag else (dist > 0)
    intra = jnp.where(mask[None], jnp.exp(log_gamma[:, None, None] * jnp.maximum(dist, 0.0)[None]), 0.0)
    q_decay = jnp.exp(log_gamma[:, None] * (idx[None] + 1.0)).T[None, :, :, None]
    k_decay = jnp.exp(log_gamma[:, None] * (RET_CHUNK - 1.0 - idx[None])).T[None, :, :, None]
    chunk_decay = jnp.exp(log_gamma * RET_CHUNK)[None, :, None, None]

    def to_chunks(z):
        return jnp.moveaxis(z.reshape(bsz, n, RET_CHUNK, heads, z.shape[-1]), 1, 0)

    def step(s, qkv):
        qc, kc, vc = (z.astype(jnp.float32) for z in qkv)
        scores = jnp.einsum('bihd,bjhd->bhij', qc, kc) * intra
        o = jnp.einsum('bhij,bjhe->bihe', scores, vc)
        o = o + jnp.einsum('bihd,bhde->bihe', qc * q_decay, s)
        s = s * chunk_decay + jnp.einsum('bjhd,bjhe->bhde', kc * k_decay, vc)
        return s, o

    s_fin, o = lax.scan(step, s0, (to_chunks(q), to_chunks(k), to_chunks(v)))
    return jnp.moveaxis(o, 0, 1).reshape(bsz, t, heads, dv), s_fin


def bi_retention(q, k, v, log_gamma, s0_f, s0_b):
    o_f, s_f = retention_scan(q, k, v, log_gamma, s0_f, True)
    flip = lambda z: jnp.flip(z, axis=1)
    o_b, s_b = retention_scan(flip(q), flip(k), flip(v), log_gamma, s0_b, False)
    return o_f + flip(o_b), s_f, s_b


def mixer_retention(hc, hx, w_in, w_out, cos_r, sin_r, need_ctx_out):
    log_gamma = jnp.log1p(-jnp.exp2(-5.0 - jnp.arange(RET_HEADS, dtype=jnp.float32)))
    hdk = RET_HEADS * RET_DK
    hdv = RET_HEADS * RET_DV

    def project(h):
        bsz, t, _ = h.shape
        q, k, v, g = jnp.split(h @ w_in, [hdk, 2 * hdk, 2 * hdk + hdv], axis=-1)
        q = q.reshape(bsz, t, RET_HEADS, RET_DK)
        k = k.reshape(bsz, t, RET_HEADS, RET_DK) * (RET_DK ** -0.5)
        v = v.reshape(bsz, t, RET_HEADS, RET_DV)
        return q, k, v, g

    def finish(o, g):
        bsz, t = g.shape[:2]
        y = _rms(o).reshape(bsz, t, hdv) * jax.nn.silu(g.astype(jnp.float32))
        return y.astype(g.dtype) @ w_out

    qc, kc, vc, gc = project(hc)
    qx, kx, vx, gx = project(hx)
    qx = apply_rope(qx, cos_r, sin_r)
    kx = apply_rope(kx, cos_r, sin_r)
    zero = jnp.zeros((hc.shape[0], RET_HEADS, RET_DK, RET_DV), jnp.float32)
    oc, s_f, s_b = bi_retention(qc, kc, vc, log_gamma, zero, zero)
    ox, _, _ = bi_retention(qx, kx, vx, log_gamma, s_f, s_b)
    yx = finish(ox, gx)
    yc = finish(oc, gc) if need_ctx_out else None
    return yc, yx


def diff_attend(q, k, v, lam):
    s = jnp.einsum('bqhmd,bkhmd->bhmqk', q.astype(jnp.float32), k.astype(jnp.float32)) * (DIFF_DH ** -0.5)
    p = jax.nn.softmax(s, axis=-1)
    w = p[:, :, 0] - lam * p[:, :, 1]
    return jnp.einsum('bhqk,bkhe->bqhe', w, v.astype(jnp.float32))


def mixer_diff(hc, hx, w_in, lam_p, subln_g, w_out, lam_init, cos_d, sin_d, need_ctx_out):
    lp = lam_p.astype(jnp.float32)
    lam = jnp.exp(jnp.sum(lp[0] * lp[1])) - jnp.exp(jnp.sum(lp[2] * lp[3])) + lam_init

    def project(h, rotate):
        bsz, t, _ = h.shape
        q, k, v = jnp.split(h @ w_in, 3, axis=-1)
        q = q.reshape(bsz, t, 2 * DIFF_HEADS, DIFF_DH)
        k = k.reshape(bsz, t, 2 * DIFF_HEADS, DIFF_DH)
        if rotate:
            q = apply_rope(q, cos_d, sin_d)
            k = apply_rope(k, cos_d, sin_d)
        return (q.reshape(bsz, t, DIFF_HEADS, 2, DIFF_DH), k.reshape(bsz, t, DIFF_HEADS, 2, DIFF_DH),
                v.reshape(bsz, t, DIFF_HEADS, DIFF_DV))

    def finish(o, dtype):
        bsz, t = o.shape[:2]
        y = rms_norm(o, subln_g) * (1.0 - lam_init)
        return y.reshape(bsz, t, DIFF_HEADS * DIFF_DV).astype(dtype) @ w_out

    qc, kc, vc = project(hc, False)
    qx, kx, vx = project(hx, True)
    k_all = jnp.concatenate([kc, kx], axis=1)
    v_all = jnp.concatenate([vc, vx], axis=1)
    bsz, t = qx.shape[:2]
    n_blk = t // Q_BLOCK
    q_blocks = jnp.moveaxis(qx.reshape(bsz, n_blk, Q_BLOCK, DIFF_HEADS, 2, DIFF_DH), 1, 0)
    ox = lax.map(lambda qb: diff_attend(qb, k_all, v_all, lam), q_blocks)
    ox = jnp.moveaxis(ox, 0, 1).reshape(bsz, t, DIFF_HEADS, DIFF_DV)
    yx = finish(ox, hx.dtype)
    yc = finish(diff_attend(qc, kc, vc, lam), hc.dtype) if need_ctx_out else None
    return yc, yx


def setup_inputs(seed: int = 0) -> dict:
    key = jax.random.key(seed)
    keys = iter(jax.random.split(key, 32))
    f32 = jnp.float32

    def normal(shape, scale):
        return jax.random.normal(next(keys), shape, f32) * scale

    n_a = len(range(0, DEPTH, N_MIXERS))
    n_b = len(range(1, DEPTH, N_MIXERS))
    n_c = len(range(2, DEPTH, N_MIXERS))
    u = jax.random.uniform(next(keys), (n_a, 2, D_RNN), f32, LRU_A_MIN ** 2, LRU_A_MAX ** 2)
    a_base = u ** (1.0 / LRU_C)
    lru_lam = jnp.log(a_base) - jnp.log1p(-a_base)
    return {
        'x': normal((BATCH, SEQ, D_MODEL), 1.0),
        'c': normal((BATCH, D_MODEL), 1.0),
        'ctx': normal((BATCH, CTX_LEN, D_MODEL), 1.0),
        'c_ctx': normal((D_MODEL,), 1.0),
        'norm_g': 1.0 + normal((DEPTH, 3, D_MODEL), 0.05),
        'mod_w': normal((DEPTH, D_MODEL, N_MOD * D_MODEL), 0.5 * D_MODEL ** -0.5),
        'mod_b': normal((DEPTH, N_MOD * D_MODEL), 0.02),
        'ffn_w_in': normal((DEPTH, 2, D_MODEL, 2 * D_FF), D_MODEL ** -0.5),
        'ffn_w_out': normal((DEPTH, 2, D_FF, D_MODEL), D_FF ** -0.5),
        'lru_w_in': normal((n_a, D_MODEL, 2 * D_RNN), D_MODEL ** -0.5),
        'lru_conv_w': normal((n_a, CONV_W, D_RNN), CONV_W ** -0.5),
        'lru_conv_b': normal((n_a, D_RNN), 0.02),
        'lru_gate_w': normal((n_a, 2, 2, LRU_BLOCKS, LRU_BLK, LRU_BLK), LRU_BLK ** -0.5),
        'lru_gate_b': normal((n_a, 2, 2, LRU_BLOCKS, LRU_BLK), 0.02),
        'lru_lam': lru_lam,
        'lru_w_out': normal((n_a, D_RNN, D_MODEL), D_RNN ** -0.5),
        'ret_w_in': normal((n_b, D_MODEL, 2 * RET_HEADS * (RET_DK + RET_DV)), D_MODEL ** -0.5),
        'ret_w_out': normal((n_b, RET_HEADS * RET_DV, D_MODEL), (RET_HEADS * RET_DV) ** -0.5),
        'dif_w_in': normal((n_c, D_MODEL, 3 * D_MODEL), D_MODEL ** -0.5),
        'dif_lam': normal((n_c, 4, DIFF_DH), 0.1),
        'dif_subln': 1.0 + normal((n_c, DIFF_DV), 0.05),
        'dif_w_out': normal((n_c, DIFF_HEADS * DIFF_DV, D_MODEL), (DIFF_HEADS * DIFF_DV) ** -0.5),
        'final_g': 1.0 + normal((D_MODEL,), 0.05),
    }


def reference(x, c, ctx, c_ctx, norm_g, mod_w, mod_b, ffn_w_in, ffn_w_out,
              lru_w_in, lru_conv_w, lru_conv_b, lru_gate_w, lru_gate_b, lru_lam, lru_w_out,
              ret_w_in, ret_w_out, dif_w_in, dif_lam, dif_subln, dif_w_out, final_g):
    n_lat = x.shape[1]
    rows = n_lat // GRID_W
    cos_d, sin_d = axial_rope_tables(rows, DIFF_DH)
    cos_r, sin_r = retention_rope_tables(n_lat)
    h_ctx = ctx
    for i in range(DEPTH):
        last = i == DEPTH - 1
        mx = ada_mod(c, mod_w[i], mod_b[i])
        mc = ada_mod(c_ctx, mod_w[i], mod_b[i])
        x = x + MACARON_W * mx[2] * swiglu(modulate(rms_norm(x, norm_g[i, 0]), mx[0], mx[1]),
                                           ffn_w_in[i, 0], ffn_w_out[i, 0])
        h_ctx = h_ctx + MACARON_W * mc[2] * swiglu(modulate(rms_norm(h_ctx, norm_g[i, 0]), mc[0], mc[1]),
                                                   ffn_w_in[i, 0], ffn_w_out[i, 0])
        hx = modulate(rms_norm(x, norm_g[i, 1]), mx[3], mx[4])
        hc = modulate(rms_norm(h_ctx, norm_g[i, 1]), mc[3], mc[4])
        kind, j = i % N_MIXERS, i // N_MIXERS
        if kind == 0:
            yc, yx = mixer_rglru(hc, hx, lru_w_in[j], lru_conv_w[j], lru_conv_b[j], lru_gate_w[j],
                                 lru_gate_b[j], lru_lam[j], lru_w_out[j], not last)
        elif kind == 1:
            yc, yx = mixer_retention(hc, hx, ret_w_in[j], ret_w_out[j], cos_r, sin_r, not last)
        else:
            lam_init = 0.8 - 0.6 * math.exp(-0.3 * i)
            yc, yx = mixer_diff(hc, hx, dif_w_in[j], dif_lam[j], dif_subln[j], dif_w_out[j],
                                lam_init, cos_d, sin_d, not last)
        x = x + mx[5] * yx
        x = x + MACARON_W * mx[8] * swiglu(modulate(rms_norm(x, norm_g[i, 2]), mx[6], mx[7]),
                                           ffn_w_in[i, 1], ffn_w_out[i, 1])
        if not last:
            h_ctx = h_ctx + mc[5] * yc
            h_ctx = h_ctx + MACARON_W * mc[8] * swiglu(modulate(rms_norm(h_ctx, norm_g[i, 2]), mc[6], mc[7]),
                                                       ffn_w_in[i, 1], ffn_w_out[i, 1])
    return rms_norm(x, final_g)
```

```python
import math
import os
import numpy as np
import concourse.bass as bass
import concourse.mybir as mybir
from concourse.bass_utils import run_bass_kernel_spmd

F32 = mybir.dt.float32
BF16 = mybir.dt.bfloat16
AF = mybir.ActivationFunctionType
ALU = mybir.AluOpType

D = 1024
SEQ = 8192
CTX = 256
TT = SEQ + CTX
DEPTH = 4
DFF = 2816
NFC = 22
DRNN = 1280
EPS = 1e-6
NCORES = 4
EPOCH = 60000
NENGSEM = 6
ARENA = 52000
DEBUG_LINES = bool(os.environ.get("KB_DUMP"))


class Buf:
    __slots__ = ("name", "w", "rd", "dsem")

    def __init__(self, name):
        self.name = name
        self.w = None
        self.rd = {}
        self.dsem = {}


class Ins:
    __slots__ = ("fn", "waits", "mark", "dma", "cnt", "real", "line")

    def __init__(self, fn):
        self.fn = fn
        self.line = 0
        self.waits = []
        self.mark = False
        self.dma = None
        self.cnt = 0
        self.real = fn is not None


class T:
    __slots__ = ("ap", "b")

    def __init__(self, ap, b):
        self.ap = ap
        self.b = b

    def __getitem__(self, idx):
        return self.ap[idx]


COMPUTE = ("pe", "act", "dve", "pool")
ENGS = ("pe", "act", "dve", "pool", "sp")


class KB:
    def __init__(self, nc):
        self.nc = nc
        self.streams = {e: [] for e in ENGS}
        self.seen = {e: {} for e in ENGS}
        self.last_real = {e: -1 for e in ENGS}
        self.ndsem = 100 - 4 * NENGSEM - 2
        self.dsem_vals = [0] * self.ndsem
        self.nsw = 30
        self.free_dsems = {"sw": list(range(self.nsw)), "hw": list(range(self.nsw, self.ndsem))}
        self.bg_sems = set()
        self.live_bufs = []
        self.arena_top = 0
        self.arena_base = 0
        self.persist = []

    def buf(self, name):
        b = Buf(name)
        self.live_bufs.append(b)
        return b

    def sb(self, name, shape, dtype=F32, parts=128):
        n = 1
        for s_ in shape:
            n *= s_
        words = n if dtype == F32 else (n + 1) // 2
        words += words & 1
        off = self.arena_top
        self.arena_top += words
        assert self.arena_top <= ARENA, (name, self.arena_top)
        if dtype == F32:
            ap = self.arena_f[0:parts, off:off + n]
        else:
            ap = self.arena_b[0:parts, 2 * off:2 * off + n]
        if len(shape) == 2:
            ap = ap.rearrange("p (a b) -> p a b", a=shape[0])
        elif len(shape) == 3:
            ap = ap.rearrange("p (a b c) -> p a b c", a=shape[0], b=shape[1])
        return T(ap, self.buf(name))

    def phase_reset(self):
        self.arena_top = self.arena_base

    def _wait_list(self, eng, reads, writes):
        waits = []
        raw = []
        war = []
        for b in reads:
            if b.w is not None:
                raw.append(b.w)
        for b in writes:
            if b.w is not None:
                raw.append(b.w)
            war.extend(b.rd.values())
        seen = self.seen[eng]
        for lst, is_war in ((raw, False), (war, True)):
            for t in lst:
                if t[0] == "e":
                    src, idx = t[1], t[2]
                    if src == eng and (eng == "pe" or is_war):
                        continue
                    if seen.get(src, -1) >= idx:
                        continue
                    seen[src] = idx
                    self.streams[src][idx].mark = True
                    waits.append(t)
                else:
                    key = ("d", t[1])
                    if seen.get(key, -1) >= t[2]:
                        continue
                    seen[key] = t[2]
                    waits.append(t)
        return waits

    def op(self, eng, fn, reads=(), writes=()):
        ins = Ins(fn)
        if DEBUG_LINES:
            import sys
            ins.line = sys._getframe(2).f_lineno
        ins.waits = self._wait_list(eng, reads, writes)
        idx = len(self.streams[eng])
        self.streams[eng].append(ins)
        self.last_real[eng] = idx
        tok = ("e", eng, idx)
        for b in reads:
            b.rd[eng] = tok
        for b in writes:
            b.w = tok
            b.rd = {}
        return ins

    def dma(self, q, out, in_, reads=(), writes=(), owner=None):
        ins = Ins(lambda e: e.dma_start(out=out, in_=in_))
        if DEBUG_LINES:
            import sys
            ins.line = -sys._getframe(1).f_lineno
        ins.waits = self._wait_list(q, reads, writes)
        if owner is None:
            owner = writes[0] if writes else reads[0]
        cls = "sw" if q == "pool" else "hw"
        if cls not in owner.dsem:
            owner.dsem[cls] = self.free_dsems[cls].pop()
        sem = owner.dsem[cls]
        self.dsem_vals[sem] += 16
        tok = ("d", sem, self.dsem_vals[sem])
        ins.dma = sem
        self.streams[q].append(ins)
        for b in reads:
            b.rd[("d", sem)] = tok
        for b in writes:
            b.w = tok
            b.rd = {}
        return tok

    def bg_dma(self, q, out, in_, gbuf):
        ins = Ins(lambda e: e.dma_start(out=out, in_=in_))
        if "sw" not in gbuf.dsem:
            gbuf.dsem["sw"] = self.free_dsems["sw"].pop()
            self.bg_sems.add(gbuf.dsem["sw"])
        sem = gbuf.dsem["sw"]
        self.dsem_vals[sem] += 16
        ins.dma = sem
        self.streams[q].append(ins)
        gbuf.w = ("d", sem, self.dsem_vals[sem])

    def barrier(self, final=False, soft=False):
        last = dict(self.last_real)
        for eng in ENGS:
            ins = Ins(None)
            seen = self.seen[eng]
            for src in COMPUTE:
                idx = last[src]
                if src == eng or idx < 0:
                    continue
                if seen.get(src, -1) >= idx:
                    continue
                seen[src] = idx
                self.streams[src][idx].mark = True
                ins.waits.append(("e", src, idx))
            for sem in range(self.ndsem):
                val = self.dsem_vals[sem]
                if val == 0 or (sem in self.bg_sems and not final):
                    continue
                key = ("d", sem)
                if seen.get(key, -1) >= val:
                    continue
                seen[key] = val
                ins.waits.append(("d", sem, val))
            self.streams[eng].append(ins)
        for b in self.live_bufs:
            for cls, sem in list(b.dsem.items()):
                if sem in self.bg_sems:
                    continue
                if sem not in self.free_dsems[cls]:
                    self.free_dsems[cls].append(sem)
                del b.dsem[cls]
        if not soft:
            self.live_bufs = list(self.persist)
            self.phase_reset()

    def emit(self, esems, dsems):
        nc = self.nc
        for eng in COMPUTE:
            c = 0
            for ins in self.streams[eng]:
                if ins.mark:
                    c += 1
                    ins.cnt = c
            assert c <= EPOCH * NENGSEM, (eng, c)
            if os.environ.get("KB_STATS"):
                print("KB", eng, "n_ins", len(self.streams[eng]), "marked", c, "waits", sum(len(i.waits) for i in self.streams[eng]))
        if os.environ.get("KB_STATS"):
            print("KB sp n_ins", len(self.streams["sp"]), "max dsem", max(self.dsem_vals), self.dsem_vals)
        streams = self.streams
        if os.environ.get("KB_DUMP"):
            with open(os.environ["KB_DUMP"], "w") as f:
                for eng in ENGS:
                    f.write("=== %s\n" % eng)
                    for i, ins in enumerate(streams[eng]):
                        ws = []
                        for t in ins.waits:
                            if t[0] == "e":
                                ws.append("%s#%d(c%d)" % (t[1], t[2], streams[t[1]][t[2]].cnt))
                            else:
                                ws.append("d%d>=%d" % (t[1], t[2]))
                        f.write("%d L%d %s%s waits=[%s]\n" % (i, ins.line, "M%d " % ins.cnt if ins.mark else "", "DMA%d " % ins.dma if ins.dma is not None else "", ",".join(ws)))

        def semval(t):
            if t[0] == "e":
                cnt = streams[t[1]][t[2]].cnt
                assert cnt > 0
                return esems[t[1]][(cnt - 1) // EPOCH], (cnt - 1) % EPOCH + 1
            return dsems[t[1]], t[2]

        def run(eng, e):
            for ins in streams[eng]:
                for t in ins.waits:
                    sm, v = semval(t)
                    e.wait_ge(sm, v)
                if ins.fn is None:
                    continue
                r = ins.fn(e)
                if ins.dma is not None:
                    r.then_inc(dsems[ins.dma], 16)
                elif ins.mark:
                    r.then_inc(esems[eng][(ins.cnt - 1) // EPOCH], 1)

        with nc.Block() as block:
            @block.tensor
            def _(e):
                run("pe", e)

            @block.scalar
            def _(e):
                run("act", e)

            @block.vector
            def _(e):
                run("dve", e)

            @block.gpsimd
            def _(e):
                run("pool", e)

            @block.sync
            def _(e):
                run("sp", e)

    def mm(self, out, lhsT, rhs, start=True, stop=True, R=(), W=(), **kw):
        return self.op("pe", lambda e: e.matmul(out, lhsT, rhs, start=start, stop=stop, **kw), R, W)

    def tr(self, out, in_, ident, R=(), W=()):
        return self.op("pe", lambda e: e.transpose(out, in_, ident), R, W)

    def act(self, out, in_, func, R=(), W=(), bias=None, scale=None, accum_out=None):
        kw = {}
        if accum_out is not None:
            kw["accum_out"] = accum_out
        if bias is not None:
            kw["bias"] = bias
        if scale is not None:
            kw["scale"] = scale
        return self.op("act", lambda e: e.activation(out=out, in_=in_, func=func, **kw), R, W)

    def tt(self, eng, out, in0, in1, op, R=(), W=()):
        return self.op(eng, lambda e: e.tensor_tensor(out=out, in0=in0, in1=in1, op=op), R, W)

    def ts(self, eng, out, in0, s1, s2, op0, op1=None, R=(), W=()):
        if op1 is None:
            return self.op(eng, lambda e: e.tensor_scalar(out=out, in0=in0, scalar1=s1, scalar2=None, op0=op0), R, W)
        return self.op(eng, lambda e: e.tensor_scalar(out=out, in0=in0, scalar1=s1, scalar2=s2, op0=op0, op1=op1), R, W)

    def stt(self, eng, out, in0, scalar, in1, op0, op1, R=(), W=()):
        return self.op(eng, lambda e: e.scalar_tensor_tensor(out=out, in0=in0, scalar=scalar, in1=in1, op0=op0, op1=op1), R, W)

    def copy(self, eng, out, in_, R=(), W=()):
        if eng == "act":
            return self.op("act", lambda e: e.copy(out=out, in_=in_), R, W)
        return self.op(eng, lambda e: e.tensor_copy(out=out, in_=in_), R, W)

    def rstd(self, out, outb, in_, inb, scale):
        self.act(out, in_, AF.Sqrt, bias=EPS, scale=scale, R=[inb], W=[outb])
        self.op("dve", lambda e: e.reciprocal(out=out, in_=out), [outb], [outb])

    def scan(self, out, a, b, init, R=(), W=()):
        return self.op("dve", lambda e: e.tensor_tensor_scan(out=out, data0=a, data1=b, initial=init,
                                                             op0=ALU.mult, op1=ALU.add), R, W)

    def memset(self, eng, ap, val, W=()):
        return self.op(eng, lambda e: e.memset(ap, val), (), W)


class Rot:
    def __init__(self, tiles):
        self.t = tiles
        self.i = 0

    def next(self):
        r = self.t[self.i % len(self.t)]
        self.i += 1
        return r


def pack_w(W, kp, gw):
    K, M = W.shape
    nk, ng = K // kp, M // gw
    a = W.reshape(nk, kp, ng, gw).transpose(2, 1, 0, 3)
    return np.ascontiguousarray(a).reshape(ng, kp, nk * gw)


def col128(v):
    v = np.asarray(v, np.float32)
    lead = v.shape[:-1]
    n = v.shape[-1] // 128
    a = v.reshape(*lead, n, 128)
    a = np.moveaxis(a, -1, 0)
    return np.ascontiguousarray(a).reshape(128, -1)


def col80(v):
    v = np.asarray(v, np.float32)
    lead = v.shape[:-1]
    n = v.shape[-1] // 80
    a = v.reshape(*lead, n, 80)
    a = np.moveaxis(a, -1, 0)
    out = np.zeros((128, a.reshape(80, -1).shape[1]), np.float32)
    out[:80] = a.reshape(80, -1)
    return out


class Smalls:
    def __init__(self):
        self.cols = []
        self.off = {}
        self.n = 0

    def add(self, name, arr):
        arr = np.asarray(arr, np.float32)
        assert arr.shape[0] == 128
        self.off[name] = (self.n, arr.shape[1])
        self.cols.append(arr)
        self.n += arr.shape[1]

    def build(self):
        return np.ascontiguousarray(np.concatenate(self.cols, axis=1))


RET_H = 4
RET_DK = 256
RET_DV = 512


def ret_consts():
    lg = np.log1p(-np.exp2(-5.0 - np.arange(RET_H, dtype=np.float32))).astype(np.float32)
    idx = np.arange(128, dtype=np.float32)
    dist = np.abs(idx[None, :] - idx[:, None])
    maskT = np.exp(lg[:, None, None] * dist[None]).astype(np.float32)
    qdf = np.exp(lg[:, None] * (idx[None] + 1.0)).astype(np.float32)
    qdb = np.exp(lg[:, None] * (128.0 - idx[None])).astype(np.float32)
    kdf = np.exp(lg[:, None] * (127.0 - idx[None])).astype(np.float32)
    kdb = np.exp(lg[:, None] * idx[None]).astype(np.float32)
    cd = np.exp(lg * 128.0).astype(np.float32)
    return maskT, qdf, qdb, kdf, kdb, cd


def prep_static(inp):
    sm = Smalls()
    sm.add("norm_g", col128(inp["norm_g"]))
    sm.add("final_g", col128(inp["final_g"]))
    sm.add("mod_b", col128(inp["mod_b"]))
    sm.add("c_ctx", col128(inp["c_ctx"]))
    sm.add("conv_w", col80(inp["lru_conv_w"]))
    sm.add("conv_b", col80(inp["lru_conv_b"]))
    sm.add("gate_b", col80(inp["lru_gate_b"].reshape(2, 2, 2, DRNN)))
    sm.add("lam", col80(inp["lru_lam"]))
    maskT, qdf, qdb, kdf, kdb, cd = ret_consts()
    sm.add("r_mask", np.ascontiguousarray(maskT.transpose(1, 0, 2)).reshape(128, RET_H * 128))
    sm.add("r_kdf", np.ascontiguousarray(kdf.T))
    sm.add("r_kdb", np.ascontiguousarray(kdb.T))
    rqtab = np.concatenate([np.tile(qdf, (1, 4)).reshape(1, RET_H * 512), np.tile(qdb, (1, 4)).reshape(1, RET_H * 512)], axis=1)
    sm.add("r_cd", np.broadcast_to(cd.reshape(1, RET_H), (128, RET_H)))
    sm.add("dif_lam", np.pad(np.ascontiguousarray(inp["dif_lam"][0].T), ((0, 64), (0, 0))))
    sm.add("dif_subln", np.broadcast_to(inp["dif_subln"][0].reshape(1, 128), (128, 128)))
    sm.add("ident", np.eye(128, dtype=np.float32))
    sh = {"smalls": sm.build()}
    sh["rqtab"] = np.ascontiguousarray(np.broadcast_to(rqtab, (128, 2 * RET_H * 512)), dtype=np.float32)
    W = {}
    for l in range(DEPTH):
        for s in range(2):
            wi = inp["ffn_w_in"][l, s]
            a = wi.reshape(8, 128, 2, NFC, 128).transpose(3, 1, 0, 2, 4)
            W[f"win{l}{s}"] = np.ascontiguousarray(a).reshape(NFC * 128, 2048)
            wo = inp["ffn_w_out"][l, s]
            b = wo.reshape(NFC, 128, 8, 128).transpose(2, 1, 0, 3)
            W[f"wout{l}{s}"] = np.ascontiguousarray(b).reshape(8 * 128, NFC * 128)
    for n, l in enumerate((0, 3)):
        w_in = inp["lru_w_in"][n]
        W[f"lwx{l}"] = pack_w(w_in[:, :DRNN], 128, 80).reshape(16 * 128, 640)
        W[f"lwg{l}"] = pack_w(w_in[:, DRNN:], 128, 80).reshape(16 * 128, 640)
        gw = inp["lru_gate_w"][n]
        a = gw.reshape(2, 2, 8, 2, 80, 2, 80).transpose(4, 0, 1, 2, 3, 5, 6)
        W[f"lgw{l}"] = np.ascontiguousarray(a).reshape(80, 128 * 80)
        W[f"lwo{l}"] = pack_w(inp["lru_w_out"][n], 80, 128).reshape(8 * 80, 16 * 128)
    rw = inp["ret_w_in"][0]
    W["rwq"] = pack_w(rw[:, 0:1024], 128, 128).reshape(8 * 128, 1024)
    W["rwk"] = pack_w(rw[:, 1024:2048], 128, 128).reshape(8 * 128, 1024)
    W["rwv"] = pack_w(rw[:, 2048:4096], 128, 512).reshape(4 * 128, 4096)
    W["rwg"] = pack_w(rw[:, 4096:6144], 128, 128).reshape(16 * 128, 1024)
    W["rwo"] = pack_w(inp["ret_w_out"][0], 128, 128).reshape(8 * 128, 16 * 128)
    dw = inp["dif_w_in"][0]
    perm = np.arange(1024).reshape(16, 2, 32)[:, ::-1, :].reshape(-1)
    W["dwq"] = pack_w(dw[:, 0:1024], 128, 128).reshape(8 * 128, 1024)
    W["dwqs"] = pack_w(dw[:, 0:1024][:, perm], 128, 128).reshape(8 * 128, 1024)
    W["dwk"] = pack_w(dw[:, 1024:2048], 128, 128).reshape(8 * 128, 1024)
    W["dwks"] = pack_w(dw[:, 1024:2048][:, perm], 128, 128).reshape(8 * 128, 1024)
    W["dwv"] = pack_w(dw[:, 2048:3072], 128, 512).reshape(2 * 128, 4096)
    W["dwo"] = pack_w(inp["dif_w_out"][0], 128, 128).reshape(8 * 128, 1024)
    for k_, v_ in W.items():
        sh[k_] = np.ascontiguousarray(v_, dtype=np.float32)
    sh["mod_w"] = np.ascontiguousarray(inp["mod_w"], dtype=np.float32).reshape(DEPTH * D, 9 * D)
    theta = (np.float32(10000.0) ** (-np.linspace(0.0, 1.0, 128, dtype=np.float32))).astype(np.float32)
    ang = np.arange(SEQ, dtype=np.float32)[None, :] * theta[:, None]
    sh["r_cos"] = np.cos(ang).astype(np.float32)
    sh["r_sin"] = np.sin(ang).astype(np.float32)
    nfreq = 16
    inv = (np.float32(10000.0) ** (-np.arange(nfreq, dtype=np.float32) / nfreq)).astype(np.float32)
    rows = SEQ // 64
    row = np.broadcast_to(np.arange(rows, dtype=np.float32)[:, None], (rows, 64)).reshape(-1)
    col = np.broadcast_to(np.arange(64, dtype=np.float32)[None, :], (rows, 64)).reshape(-1)
    angd = np.concatenate([row[:, None] * inv, col[:, None] * inv], axis=-1).astype(np.float32)
    cosd, sind = np.cos(angd).astype(np.float32), np.sin(angd).astype(np.float32)
    ii = (np.arange(128) % 64) % 32
    sgn = np.where((np.arange(128) % 64) < 32, -1.0, 1.0).astype(np.float32)
    sh["d_cos"] = np.ascontiguousarray(cosd[:, ii].T)
    sh["d_sin"] = np.ascontiguousarray((sind[:, ii] * sgn[None, :]).T)
    return sh, sm.off


from contextlib import ExitStack
import os

GELU_C0 = math.sqrt(2.0 / math.pi)
GELU_C1 = GELU_C0 * 0.044715
LAM_INIT2 = 0.8 - 0.6 * math.exp(-0.3 * 2)

SCRATCH = {
    "xs": ([D, TT], F32),
    "xb": ([16, 80, TT], F32), "gl": ([16, 80, TT], F32), "hf": ([16, 80, TT], F32),
    "ym": ([16, 80, TT], BF16),
    "rq": ([8, 128, TT], BF16), "rk": ([8, 128, TT], BF16),
    "rqdf": ([8, 128, TT], BF16), "rqdb": ([8, 128, TT], BF16),
    "rkdf": ([TT, 1024], BF16), "rkdb": ([TT, 1024], BF16),
    "rv": ([TT, 2048], BF16), "rsg": ([16, 128, TT], BF16),
    "ro": ([16, 128, TT], F32), "ryr": ([16, 128, TT], BF16),
    "dq": ([8, 128, TT], BF16), "dk": ([8, 128, TT], BF16), "dv": ([TT, 1024], BF16),
    "dya": ([8, 128, TT], BF16),
}


def wgroup(name):
    if name.startswith("win") or name.startswith("wout"):
        return (int(name[-2]), "f" + name[-1])
    if name.startswith("l"):
        return (int(name[-1]), "m")
    if name.startswith("r"):
        return (1, "m")
    return (2, "m")


def build_program(soff, ns, wshapes, stop_after=None, debug_out=()):
    nc = bass.Bass("TRN2", target_bir_lowering=False)
    k = KB(nc)
    dr = {}

    def din(name, shape, dt=F32):
        dr[name] = nc.dram_tensor(name, list(shape), dt, kind="ExternalInput").ap()

    din("xin", [D, TT])
    din("cond", [128, 8])
    din("smalls", [128, ns])
    din("mod_w", [DEPTH * D, 9 * D])
    for nm in ("r_cos", "r_sin", "d_cos", "d_sin"):
        din(nm, [128, SEQ])
    din("rqtab", [128, 2 * RET_H * 512])
    wbf = {}
    for nm, shp in wshapes.items():
        din(nm, shp)
        wbf[nm] = nc.dram_tensor("b_" + nm, list(shp), BF16, kind="Internal").ap()
    for nm, (shp, dt) in SCRATCH.items():
        kind = "ExternalOutput" if nm in debug_out else "Internal"
        dr[nm] = nc.dram_tensor(nm, list(shp), dt, kind=kind).ap()
    dr["out"] = nc.dram_tensor("out", [D, SEQ], F32, kind="ExternalOutput").ap()

    es = ExitStack()
    with es:
        arena = es.enter_context(nc.sbuf_tensor("arena", [128, ARENA], F32))
        k.arena_f = arena
        k.arena_b = arena.bitcast(BF16)
        psb = [es.enter_context(nc.psum_tensor(f"ps{i}", [128, 512], F32)) for i in range(8)]
        esems = {e: [es.enter_context(nc.semaphore(f"s_{e}{i}")) for i in range(NENGSEM)] for e in COMPUTE}
        dsems = [es.enter_context(nc.semaphore(f"d{i}")) for i in range(k.ndsem)]
        ps = [T(psb[i][:, :], Buf(f"ps{i}")) for i in range(8)]
        psbf = {ps[i]: psb[i].bitcast(BF16)[:, :] for i in range(8)}

        smt = k.sb("smalls", [ns])
        modt = k.sb("modt", [DEPTH * 2 * 72])
        tabA = k.sb("tabA", [DEPTH * 2 * 3 * 8])
        tabG = k.sb("tabG", [DEPTH * 2 * 3 * 8])
        ones_f = k.sb("ones_f", [128])
        ident_b = k.sb("ident_b", [128], BF16)
        zeros_b = k.sb("zeros_b", [512], BF16)
        lamt = k.sb("lamt", [8])
        gsub = k.sb("gsub", [128])
        k.persist = [smt.b, modt.b, tabA.b, tabG.b, ones_f.b, ident_b.b, zeros_b.b, lamt.b, gsub.b] + [p.b for p in ps]
        k.arena_base = k.arena_top

        def S(name, lo=0, n=None):
            off, w = soff[name]
            if n is None:
                n = w - lo
            return smt.ap[:, off + lo:off + lo + n]

        def S80(name, lo=0, n=1):
            off, w = soff[name]
            return smt.ap[0:80, off + lo:off + lo + n]

        def A_(l, kind, s, dc):
            c = ((l * 2 + kind) * 3 + s) * 8 + dc
            return tabA.ap[:, c:c + 1]

        def G_(l, kind, s, dc):
            c = ((l * 2 + kind) * 3 + s) * 8 + dc
            return tabG.ap[:, c:c + 1]

        def B_(l, kind, s, dc):
            c = (l * 2 + kind) * 72 + 3 * s * 8 + dc
            return modt.ap[:, c:c + 1]

        gbufs = {}
        order = sorted(wshapes.keys(), key=lambda n_: (wgroup(n_)[0], {"f0": 0, "m": 1, "f1": 2}[wgroup(n_)[1]]))
        for nm in order:
            if os.environ.get("SIM_ONLY", "") == "rk" and nm != "rwk":
                continue
            g = wgroup(nm)
            if g not in gbufs:
                gbufs[g] = Buf("wg%s%s" % g)
            R_, C_ = wshapes[nm]
            rp = max(1, (1 << 21) // C_)
            for r0 in range(0, R_, rp):
                r1 = min(R_, r0 + rp)
                k.bg_dma("pool", wbf[nm][r0:r1, :], dr[nm][r0:r1, :], gbufs[g])

        def GB(nm):
            return gbufs[wgroup(nm)]

        k.dma("sp", smt.ap, dr["smalls"], writes=[smt.b])
        k.memset("dve", ones_f.ap, 1.0, W=[ones_f.b])
        k.memset("dve", zeros_b.ap, 0.0, W=[zeros_b.b])
        k.copy("dve", ident_b.ap, S("ident"), R=[smt.b], W=[ident_b.b])
        cond = k.sb("cond", [16])
        sc = k.sb("sc", [16])
        k.dma("sp", cond.ap[:, 0:8], dr["cond"], writes=[cond.b])
        k.copy("dve", cond.ap[:, 8:16], S("c_ctx"), R=[smt.b, cond.b], W=[cond.b])
        k.act(sc.ap, cond.ap, AF.Silu, R=[cond.b], W=[sc.b])
        mwr = Rot([k.sb(f"mw{i}", [8, 1152]) for i in range(2)])
        for l in range(DEPTH):
            mps = ps[l % 2]
            for cg in range(8):
                wt = mwr.next()
                k.dma("sp", wt.ap, dr["mod_w"][l * D:(l + 1) * D, cg * 1152:(cg + 1) * 1152].rearrange("(kc p) n -> p kc n", p=128), writes=[wt.b])
                for cc in range(9):
                    j = cg * 9 + cc
                    for kc in range(8):
                        k.mm(mps.ap[:, 2 * j:2 * j + 2], wt.ap[:, kc, cc * 128:(cc + 1) * 128], sc.ap[:, kc:16:8],
                             start=(kc == 0), stop=(kc == 7), R=[wt.b, sc.b], W=[mps.b])
            for kind in range(2):
                base = (l * 2 + kind) * 72
                k.tt("dve", modt.ap[:, base:base + 72], mps.ap[:, kind:144:2], S("mod_b", l * 72, 72), ALU.add,
                     R=[mps.b, smt.b], W=[modt.b])
                for s in range(3):
                    col = ((l * 2 + kind) * 3 + s) * 8
                    k.stt("dve", tabA.ap[:, col:col + 8], modt.ap[:, base + (3 * s + 1) * 8:base + (3 * s + 2) * 8], 1.0,
                          S("norm_g", (l * 3 + s) * 8, 8), ALU.add, ALU.mult, R=[modt.b, smt.b], W=[tabA.b])
                    k.ts("dve", tabG.ap[:, col:col + 8], modt.ap[:, base + (3 * s + 2) * 8:base + (3 * s + 3) * 8],
                         (1.0 if s == 1 else 0.5), None, ALU.mult, R=[modt.b], W=[tabG.b])
        pr = k.sb("lamprod", [2])
        k.tt("dve", pr.ap[0:64, 0:1], S("dif_lam")[0:64, 0:1], S("dif_lam")[0:64, 1:2], ALU.mult, R=[smt.b], W=[pr.b])
        k.tt("dve", pr.ap[0:64, 1:2], S("dif_lam")[0:64, 2:3], S("dif_lam")[0:64, 3:4], ALU.mult, R=[smt.b, pr.b], W=[pr.b])
        k.mm(ps[2].ap[:, 0:2], ones_f.ap[0:64, :], pr.ap[0:64, 0:2], R=[pr.b, ones_f.b], W=[ps[2].b])
        k.act(lamt.ap[:, 0:2], ps[2].ap[:, 0:2], AF.Exp, R=[ps[2].b], W=[lamt.b])
        k.tt("dve", lamt.ap[:, 2:3], lamt.ap[:, 0:1], lamt.ap[:, 1:2], ALU.subtract, R=[lamt.b], W=[lamt.b])
        k.ts("dve", lamt.ap[:, 3:4], lamt.ap[:, 2:3], -1.0, -LAM_INIT2, ALU.mult, ALU.add, R=[lamt.b], W=[lamt.b])
        k.ts("dve", gsub.ap, S("dif_subln"), 1.0 - LAM_INIT2, None, ALU.mult, R=[smt.b], W=[gsub.b])
        k.barrier()
        nph = [0]

        def done():
            k.barrier()
            nph[0] += 1
            return stop_after is not None and nph[0] >= stop_after

        def load_norm(src, t0, n, kind, l, sl, x, h, sq, rs, tmp, pss, nrm=True):
            k.dma("sp", x.ap[:, :, 0:n], src[:, t0:t0 + n].rearrange("(c p) t -> p c t", p=128), writes=[x.b])
            if not nrm:
                return
            for sub in range(0, n, 512):
                w = min(512, n - sub)
                for c in range(8):
                    q = sq.next()
                    k.act(q.ap[:, 0:w], x.ap[:, c, sub:sub + w], AF.Square, R=[x.b], W=[q.b])
                    k.mm(pss.ap[:, 0:w], ones_f.ap, q.ap[:, 0:w], start=(c == 0), stop=(c == 7), R=[q.b, ones_f.b], W=[pss.b])
                k.rstd(rs.ap[:, sub:sub + w], rs.b, pss.ap[:, 0:w], pss.b, 1.0 / D)
                for c in range(8):
                    t = tmp.next()
                    k.stt("dve", t.ap[:, 0:w], x.ap[:, c, sub:sub + w], A_(l, kind, sl, c), rs.ap[:, sub:sub + w],
                          ALU.mult, ALU.mult, R=[x.b, rs.b, tabA.b], W=[t.b])
                    k.act(h.ap[:, c, sub:sub + w], t.ap[:, 0:w], AF.Identity, bias=B_(l, kind, sl, c), scale=1.0,
                          R=[t.b, modt.b], W=[h.b])

        NBLK = int(os.environ.get("NBLK", "0"))
        SIM_ONLY = os.environ.get("SIM_ONLY", "")

        def blocks(bs):
            r = [(0, CTX, 1)] + [(CTX + bs * i, bs, 0) for i in range(SEQ // bs)]
            return r[:NBLK] if NBLK else r

        def xsrc(l, first):
            return dr["xin"] if (l == 0 and first) else dr["xs"]

        def ffn_phase(l, s):
            sl = 0 if s == 0 else 2
            src = xsrc(l, s == 0)
            gb = GB(f"win{l}{s}")
            xt = Rot([k.sb(f"x{i}", [8, 1024]) for i in range(2)])
            hr = Rot([k.sb(f"h{i}", [8, 1024], BF16) for i in range(2)])
            u = [k.sb(f"u{f}", [1024], BF16) for f in range(NFC)]
            wi = Rot([k.sb(f"wi{i}", [8, 256], BF16) for i in range(3)])
            wo = Rot([k.sb(f"wo{i}", [NFC, 128], BF16) for i in range(2)])
            sq = Rot([k.sb(f"sq{i}", [512]) for i in range(2)])
            tmp = Rot([k.sb(f"tmp{i}", [512]) for i in range(2)])
            sa = Rot([k.sb(f"sa{i}", [512]) for i in range(2)])
            rs = k.sb("rs", [1024])
            pss = ps[0]
            pa = Rot([ps[1], ps[2]])
            pg = Rot([ps[3], ps[4]])
            py = Rot([ps[5], ps[6]])
            bl = blocks(1024)
            xs_ = [None] * len(bl)
            hs_ = [None] * len(bl)
            xs_[0], hs_[0] = xt.next(), hr.next()
            load_norm(src, bl[0][0], bl[0][1], bl[0][2], l, sl, xs_[0], hs_[0], sq, rs, tmp, pss)
            for bi, (t0, n, kind) in enumerate(bl):
                x, h = xs_[bi], hs_[bi]
                for f in range(NFC):
                    w_ = wi.next()
                    k.dma("sp", w_.ap, wbf[f"win{l}{s}"][f * 128:(f + 1) * 128, :].rearrange("p (kc c) -> p kc c", kc=8),
                          reads=[gb], writes=[w_.b])
                    for sub in range(0, n, 512):
                        wd = min(512, n - sub)
                        a, g = pa.next(), pg.next()
                        for kc in range(8):
                            k.mm(a.ap[:, 0:wd], w_.ap[:, kc, 0:128], h.ap[:, kc, sub:sub + wd], start=(kc == 0), stop=(kc == 7),
                                 R=[w_.b, h.b], W=[a.b])
                        for kc in range(8):
                            k.mm(g.ap[:, 0:wd], w_.ap[:, kc, 128:256], h.ap[:, kc, sub:sub + wd], start=(kc == 0), stop=(kc == 7),
                                 R=[w_.b, h.b], W=[g.b])
                        s_ = sa.next()
                        k.act(s_.ap[:, 0:wd], a.ap[:, 0:wd], AF.Silu, R=[a.b], W=[s_.b])
                        k.tt("dve", u[f].ap[:, sub:sub + wd], s_.ap[:, 0:wd], g.ap[:, 0:wd], ALU.mult, R=[s_.b, g.b], W=[u[f].b])
                if bi + 1 < len(bl):
                    xs_[bi + 1], hs_[bi + 1] = xt.next(), hr.next()
                    nb = bl[bi + 1]
                    load_norm(src, nb[0], nb[1], nb[2], l, sl, xs_[bi + 1], hs_[bi + 1], sq, rs, tmp, pss)
                for dc in range(8):
                    w_ = wo.next()
                    k.dma("sp", w_.ap, wbf[f"wout{l}{s}"][dc * 128:(dc + 1) * 128, :].rearrange("p (fc c) -> p fc c", fc=NFC),
                          reads=[gb], writes=[w_.b])
                    for sub in range(0, n, 512):
                        wd = min(512, n - sub)
                        y = py.next()
                        for f in range(NFC):
                            k.mm(y.ap[:, 0:wd], w_.ap[:, f, :], u[f].ap[:, sub:sub + wd], start=(f == 0), stop=(f == NFC - 1),
                                 R=[w_.b, u[f].b], W=[y.b])
                        k.stt("dve", x.ap[:, dc, sub:sub + wd], y.ap[:, 0:wd], G_(l, kind, sl, dc), x.ap[:, dc, sub:sub + wd],
                              ALU.mult, ALU.add, R=[y.b, x.b, tabG.b], W=[x.b])
                k.dma("pool", dr["xs"][:, t0:t0 + n].rearrange("(c p) t -> p c t", p=128), x.ap[:, :, 0:n], reads=[x.b])

        def outproj_phase(l, ysrc, wname, kp, nkc):
            gb = GB(wname)
            wt = k.sb("wo", [8, nkc, 128], BF16, parts=kp)
            k.dma("sp", wt.ap, wbf[wname].rearrange("(dc p) (c m) -> p dc c m", p=kp, c=nkc), reads=[gb], writes=[wt.b])
            xt = Rot([k.sb(f"x{i}", [8, 512]) for i in range(2)])
            yt = Rot([k.sb(f"y{i}", [nkc, 512], BF16, parts=kp) for i in range(2)])
            py = Rot([ps[0], ps[1], ps[2]])
            for (t0, n, kind) in blocks(512):
                x, y = xt.next(), yt.next()
                k.dma("sp", x.ap[:, :, 0:n], dr["xs"][:, t0:t0 + n].rearrange("(c p) t -> p c t", p=128), writes=[x.b])
                k.dma("sp", y.ap[:, :, 0:n], ysrc[:, :, t0:t0 + n].rearrange("c p t -> p c t"), writes=[y.b])
                for dc in range(8):
                    p_ = py.next()
                    for c in range(nkc):
                        k.mm(p_.ap[:, 0:n], wt.ap[:, dc, c, :], y.ap[:, c, 0:n], start=(c == 0), stop=(c == nkc - 1),
                             R=[wt.b, y.b], W=[p_.b])
                    k.stt("dve", x.ap[:, dc, 0:n], p_.ap[:, 0:n], G_(l, kind, 1, dc), x.ap[:, dc, 0:n], ALU.mult, ALU.add,
                          R=[p_.b, x.b, tabG.b], W=[x.b])
                k.dma("pool", dr["xs"][:, t0:t0 + n].rearrange("(c p) t -> p c t", p=128), x.ap[:, :, 0:n], reads=[x.b])

        PH = {}

        def lru_phase(l):
            n_ = 0 if l == 0 else 1
            gb = GB(f"lwx{l}")
            wx = k.sb("wx", [16, 8, 80], BF16)
            wg = k.sb("wg", [16, 8, 80], BF16)
            k.dma("sp", wx.ap, wbf[f"lwx{l}"].rearrange("(g p) (kc c) -> p g kc c", p=128, kc=8), reads=[gb], writes=[wx.b])
            k.dma("sp", wg.ap, wbf[f"lwg{l}"].rearrange("(g p) (kc c) -> p g kc c", p=128, kc=8), reads=[gb], writes=[wg.b])
            xt = Rot([k.sb(f"x{i}", [8, 512]) for i in range(2)])
            hr = Rot([k.sb(f"h{i}", [8, 512], BF16) for i in range(2)])
            sq = Rot([k.sb(f"sq{i}", [512]) for i in range(2)])
            tmp = Rot([k.sb(f"tmp{i}", [512]) for i in range(2)])
            rs = k.sb("rs", [512])
            xbr = Rot([k.sb(f"xbt{i}", [8, 512], parts=80) for i in range(2)])
            glr = Rot([k.sb(f"glt{i}", [8, 512], parts=80) for i in range(2)])
            pp = Rot([ps[1], ps[2], ps[3], ps[4]])
            for (t0, n, kind) in blocks(512):
                x, h = xt.next(), hr.next()
                load_norm(dr["xs"], t0, n, kind, l, 1, x, h, sq, rs, tmp, ps[0])
                for half in range(2):
                    xbt, glt = xbr.next(), glr.next()
                    for c8 in range(8):
                        c = half * 8 + c8
                        p1 = pp.next()
                        for kc in range(8):
                            k.mm(p1.ap[0:80, 0:n], wx.ap[:, c, kc, :], h.ap[:, kc, 0:n], start=(kc == 0), stop=(kc == 7),
                                 R=[wx.b, h.b], W=[p1.b])
                        k.copy("dve", xbt.ap[:, c8, 0:n], p1.ap[0:80, 0:n], R=[p1.b], W=[xbt.b])
                        p2 = pp.next()
                        for kc in range(8):
                            k.mm(p2.ap[0:80, 0:n], wg.ap[:, c, kc, :], h.ap[:, kc, 0:n], start=(kc == 0), stop=(kc == 7),
                                 R=[wg.b, h.b], W=[p2.b])
                        k.act(glt.ap[:, c8, 0:n], p2.ap[0:80, 0:n], AF.Gelu_apprx_tanh, R=[p2.b], W=[glt.b])
                    k.dma("pool", dr["xb"][half * 8:half * 8 + 8, :, t0:t0 + n].rearrange("c p t -> p c t"), xbt.ap[:, :, 0:n], reads=[xbt.b])
                    k.dma("pool", dr["gl"][half * 8:half * 8 + 8, :, t0:t0 + n].rearrange("c p t -> p c t"), glt.ap[:, :, 0:n], reads=[glt.b])
            k.barrier()
            gwt = k.sb("gwt", [128, 80], BF16, parts=80)
            k.dma("sp", gwt.ap, wbf[f"lgw{l}"].rearrange("p (b c) -> p b c", c=80), reads=[gb], writes=[gwt.b])
            et = k.sb("et", [32], parts=80)
            cdt = k.sb("cdt", [32], parts=80)
            hcdt = k.sb("hcdt", [32], parts=80)
            hbt = k.sb("hbt", [64], parts=80)
            k.act(et.ap, S80("lam", n_ * 32, 32), AF.Exp, scale=-1.0, R=[smt.b], W=[et.b])
            k.act(et.ap, et.ap, AF.Ln, bias=1.0, scale=1.0, R=[et.b], W=[et.b])
            k.ts("dve", cdt.ap, et.ap, -8.0, None, ALU.mult, R=[et.b], W=[cdt.b])
            k.ts("dve", hcdt.ap, et.ap, -4.0, None, ALU.mult, R=[et.b], W=[hcdt.b])
            k.ts("dve", hbt.ap, S80("gate_b", n_ * 64, 64), 0.5, None, ALU.mult, R=[smt.b], W=[hbt.b])
            xbg = k.sb("xbg", [2, TT], parts=80)
            xcr = Rot([k.sb(f"xc{i}", [2, 512], parts=80) for i in range(2)])
            xcbr = Rot([k.sb(f"xcb{i}", [2, 512], BF16, parts=80) for i in range(2)])
            trr = Rot([k.sb(f"tr{i}", [2, 512], parts=80) for i in range(2)])
            tir = Rot([k.sb(f"ti{i}", [2, 512], parts=80) for i in range(2)])
            atr = Rot([k.sb(f"at{i}", [2, 512], parts=80) for i in range(2)])
            a2r = Rot([k.sb(f"a2{i}", [2, 512], parts=80) for i in range(2)])
            btr = Rot([k.sb(f"bt{i}", [2, 512], parts=80) for i in range(2)])
            hbr = Rot([k.sb(f"hb{i}", [2, 512], parts=80) for i in range(2)])
            hfr = Rot([k.sb(f"hf{i}", [2, 512], parts=80) for i in range(2)])
            glr2 = Rot([k.sb(f"gl{i}", [2, 512], parts=80) for i in range(2)])
            ymr = Rot([k.sb(f"ym{i}", [2, 512], BF16, parts=80) for i in range(2)])
            ctr = Rot([k.sb(f"ct{i}", [512], parts=80) for i in range(2)])
            pp = Rot([ps[i] for i in range(8)])
            bl = blocks(512)
            for g in range(8):
                k.dma("sp", xbg.ap, dr["xb"][2 * g:2 * g + 2].rearrange("c p t -> p c t"), writes=[xbg.b])
                for d in range(2):
                    order = bl if d == 0 else [bl[0]] + bl[:0:-1]
                    prev = None
                    for (t0, n, kind) in order:
                        lo_reg, hi_reg = (0, CTX) if kind == 1 else (CTX, TT)
                        xc, xcb = xcr.next(), xcbr.next()
                        for jj in range(2):
                            c = 2 * g + jj
                            k.act(xc.ap[:, jj, 0:n], xbg.ap[:, jj, t0:t0 + n], AF.Identity, scale=S80("conv_w", (n_ * 4 + 1) * 16 + c),
                                  bias=S80("conv_b", n_ * 16 + c), R=[xbg.b, smt.b], W=[xc.b])
                            for j in (0, 2, 3):
                                o = j - 1
                                lo = max(t0, lo_reg - o)
                                hi = min(t0 + n, hi_reg - o)
                                ct = ctr.next()
                                k.ts("pool", ct.ap[:, 0:hi - lo], xbg.ap[:, jj, lo + o:hi + o], S80("conv_w", (n_ * 4 + j) * 16 + c), None,
                                     ALU.mult, R=[xbg.b, smt.b], W=[ct.b])
                                k.tt("pool", xc.ap[:, jj, lo - t0:hi - t0], xc.ap[:, jj, lo - t0:hi - t0], ct.ap[:, 0:hi - lo], ALU.add,
                                     R=[xc.b, ct.b], W=[xc.b])
                        k.copy("pool", xcb.ap[:, :, 0:n], xc.ap[:, :, 0:n], R=[xc.b], W=[xcb.b])
                        tr, ti, at, a2, bt, hcur = trr.next(), tir.next(), atr.next(), a2r.next(), btr.next(), hbr.next()
                        for jj in range(2):
                            c = 2 * g + jj
                            pr_, pi_ = pp.next(), pp.next()
                            for kk, pt in ((0, pr_), (1, pi_)):
                                for ii in range(2):
                                    blk = (((d * 2 + kk) * 8 + g) * 2 + ii) * 2 + jj
                                    k.mm(pt.ap[0:80, 0:n], gwt.ap[:, blk, :], xcb.ap[:, ii, 0:n], start=(ii == 0), stop=(ii == 1),
                                         R=[gwt.b, xcb.b], W=[pt.b])
                            k.act(tr.ap[:, jj, 0:n], pr_.ap[0:80, 0:n], AF.Tanh, scale=0.5, bias=hbt.ap[:, (d * 2 + 0) * 16 + c:(d * 2 + 0) * 16 + c + 1],
                                  R=[pr_.b, hbt.b], W=[tr.b])
                            k.act(ti.ap[:, jj, 0:n], pi_.ap[0:80, 0:n], AF.Tanh, scale=0.5, bias=hbt.ap[:, (d * 2 + 1) * 16 + c:(d * 2 + 1) * 16 + c + 1],
                                  R=[pi_.b, hbt.b], W=[ti.b])
                            k.act(at.ap[:, jj, 0:n], tr.ap[:, jj, 0:n], AF.Exp, scale=hcdt.ap[:, d * 16 + c:d * 16 + c + 1],
                                  bias=hcdt.ap[:, d * 16 + c:d * 16 + c + 1], R=[tr.b, hcdt.b], W=[at.b])
                            k.act(a2.ap[:, jj, 0:n], tr.ap[:, jj, 0:n], AF.Exp, scale=cdt.ap[:, d * 16 + c:d * 16 + c + 1],
                                  bias=cdt.ap[:, d * 16 + c:d * 16 + c + 1], R=[tr.b, cdt.b], W=[a2.b])
                        k.act(a2.ap[:, :, 0:n], a2.ap[:, :, 0:n], AF.Sqrt, scale=-1.0, bias=1.0, R=[a2.b], W=[a2.b])
                        k.stt("dve", ti.ap[:, :, 0:n], ti.ap[:, :, 0:n], 1.0, xc.ap[:, :, 0:n], ALU.add, ALU.mult, R=[ti.b, xc.b], W=[ti.b])
                        k.stt("dve", bt.ap[:, :, 0:n], ti.ap[:, :, 0:n], 0.5, a2.ap[:, :, 0:n], ALU.mult, ALU.mult, R=[ti.b, a2.b], W=[bt.b])
                        for jj in range(2):
                            if prev is None:
                                init = 0.0
                                rr = []
                            else:
                                ph, pn = prev
                                init = ph.ap[:, jj, pn - 1:pn] if d == 0 else ph.ap[:, jj, 0:1]
                                rr = [ph.b]
                            if d == 0:
                                k.scan(hcur.ap[:, jj, 0:n], at.ap[:, jj, 0:n], bt.ap[:, jj, 0:n], init, R=[at.b, bt.b] + rr, W=[hcur.b])
                            else:
                                k.scan(hcur.ap[:, jj, 0:n][:, ::-1], at.ap[:, jj, 0:n][:, ::-1], bt.ap[:, jj, 0:n][:, ::-1], init,
                                       R=[at.b, bt.b] + rr, W=[hcur.b])
                        prev = (hcur, n)
                        dsl = lambda nm: dr[nm][2 * g:2 * g + 2, :, t0:t0 + n].rearrange("c p t -> p c t")
                        if d == 0:
                            k.dma("pool", dsl("hf"), hcur.ap[:, :, 0:n], reads=[hcur.b])
                        else:
                            hf, gl, ymt = hfr.next(), glr2.next(), ymr.next()
                            k.dma("sp", hf.ap[:, :, 0:n], dsl("hf"), writes=[hf.b])
                            k.dma("sp", gl.ap[:, :, 0:n], dsl("gl"), writes=[gl.b])
                            k.tt("pool", hf.ap[:, :, 0:n], hf.ap[:, :, 0:n], hcur.ap[:, :, 0:n], ALU.add, R=[hf.b, hcur.b], W=[hf.b])
                            k.tt("pool", ymt.ap[:, :, 0:n], hf.ap[:, :, 0:n], gl.ap[:, :, 0:n], ALU.mult, R=[hf.b, gl.b], W=[ymt.b])
                            k.dma("pool", dsl("ym"), ymt.ap[:, :, 0:n], reads=[ymt.b])
                    k.barrier(soft=True)
            k.barrier()
            outproj_phase(l, dr["ym"], f"lwo{l}", 80, 16)
            return done()

        PH["lru"] = lru_phase

        def ret_phase(l):
            rstop = int(os.environ.get("RET_STOP", "99"))
            gb = GB("rwq")
            bl = blocks(512)

            def std_tiles():
                xt = Rot([k.sb(f"x{i}", [8, 512]) for i in range(2)])
                hr = Rot([k.sb(f"h{i}", [8, 512], BF16) for i in range(2)])
                sq = Rot([k.sb(f"sq{i}", [512]) for i in range(2)])
                tmp = Rot([k.sb(f"tmp{i}", [512]) for i in range(2)])
                rs = k.sb("rs", [512])
                return xt, hr, sq, tmp, rs

            def wload(name, ng, gw):
                wt = k.sb("w_" + name, [ng, 8, gw], BF16)
                k.dma("sp", wt.ap, wbf[name].rearrange("(g p) (kc c) -> p g kc c", p=128, kc=8), reads=[gb], writes=[wt.b])
                return wt

            def rope(src, dst, n, cos, sin, mr_d, mr_p):
                for hh in range(4):
                    x1, x2 = src.ap[:, 2 * hh, 0:n], src.ap[:, 2 * hh + 1, 0:n]
                    m1, m4 = mr_d.next(), mr_d.next()
                    m2, m3 = mr_p.next(), mr_p.next()
                    k.tt("dve", m1.ap[:, 0:n], x1, cos.ap[:, 0:n], ALU.mult, R=[src.b, cos.b], W=[m1.b])
                    k.tt("pool", m2.ap[:, 0:n], x2, sin.ap[:, 0:n], ALU.mult, R=[src.b, sin.b], W=[m2.b])
                    k.tt("dve", dst.ap[:, 2 * hh, 0:n], m1.ap[:, 0:n], m2.ap[:, 0:n], ALU.subtract, R=[m1.b, m2.b], W=[dst.b])
                    k.tt("pool", m3.ap[:, 0:n], x1, sin.ap[:, 0:n], ALU.mult, R=[src.b, sin.b], W=[m3.b])
                    k.tt("dve", m4.ap[:, 0:n], x2, cos.ap[:, 0:n], ALU.mult, R=[src.b, cos.b], W=[m4.b])
                    k.tt("pool", dst.ap[:, 2 * hh + 1, 0:n], m3.ap[:, 0:n], m4.ap[:, 0:n], ALU.add, R=[m3.b, m4.b], W=[dst.b])

            fm = lambda nm, t0, n: dr[nm][:, :, t0:t0 + n].rearrange("c p t -> p c t")
            tm = lambda nm, t0, n: dr[nm][t0:t0 + n, :].rearrange("(n p) d -> p n d", p=128)

            for which in ("q", "k"):
                if SIM_ONLY == "rk" and which == "q":
                    continue
                wt = wload("rw" + which, 8, 128)
                xt, hr, sq, tmp, rs = std_tiles()
                csr = Rot([k.sb(f"cos{i}", [512]) for i in range(2)])
                snr = Rot([k.sb(f"sin{i}", [512]) for i in range(2)])
                qf = k.sb("qf", [8, 512])
                mr_d = Rot([k.sb(f"md{i}", [512]) for i in range(4)])
                mr_p = Rot([k.sb(f"mp{i}", [512]) for i in range(4)])
                qbr = Rot([k.sb(f"qb{i}", [8, 512], BF16) for i in range(2)])
                if which == "q":
                    qtab = k.sb("qtab", [2, RET_H * 512])
                    k.dma("sp", qtab.ap, dr["rqtab"].rearrange("p (a b) -> p a b", a=2), writes=[qtab.b])
                    qdfr = Rot([k.sb(f"qdf{i}", [8, 512], BF16) for i in range(2)])
                    qdbr = Rot([k.sb(f"qdb{i}", [8, 512], BF16) for i in range(2)])
                else:
                    kdfr = Rot([k.sb(f"kdf{i}", [4, 1024], BF16) for i in range(2)])
                    kdbr = Rot([k.sb(f"kdb{i}", [4, 1024], BF16) for i in range(2)])
                pp = Rot([ps[i] for i in range(1, 8)])
                for (t0, n, kind) in bl:
                    x, h = xt.next(), hr.next()
                    load_norm(dr["xs"], t0, n, kind, l, 1, x, h, sq, rs, tmp, ps[0])
                    for oc in range(8):
                        p = pp.next()
                        for kc in range(8):
                            k.mm(p.ap[:, 0:n], wt.ap[:, oc, kc, :], h.ap[:, kc, 0:n], start=(kc == 0), stop=(kc == 7), R=[wt.b, h.b], W=[p.b])
                        k.act(qf.ap[:, oc, 0:n], p.ap[:, 0:n], AF.Identity, scale=(1.0 if which == "q" else 0.0625), R=[p.b], W=[qf.b])
                    qb = qbr.next()
                    if kind == 0:
                        cos, sin = csr.next(), snr.next()
                        k.dma("sp", cos.ap[:, 0:n], dr["r_cos"][:, t0 - CTX:t0 - CTX + n], writes=[cos.b])
                        k.dma("sp", sin.ap[:, 0:n], dr["r_sin"][:, t0 - CTX:t0 - CTX + n], writes=[sin.b])
                        rope(qf, qb, n, cos, sin, mr_d, mr_p)
                    else:
                        k.copy("dve", qb.ap[:, 0:4, 0:n], qf.ap[:, 0:4, 0:n], R=[qf.b], W=[qb.b])
                        k.copy("pool", qb.ap[:, 4:8, 0:n], qf.ap[:, 4:8, 0:n], R=[qf.b, qb.b], W=[qb.b])
                    k.dma("pool", fm("r" + which, t0, n), qb.ap[:, :, 0:n], reads=[qb.b])
                    if which == "q":
                        qdf, qdb = qdfr.next(), qdbr.next()
                        for hh in range(4):
                            for c in (2 * hh, 2 * hh + 1):
                                k.tt("dve", qdf.ap[:, c, 0:n], qb.ap[:, c, 0:n], qtab.ap[:, 0, hh * 512:hh * 512 + n], ALU.mult, R=[qb.b, qtab.b], W=[qdf.b])
                                k.tt("pool", qdb.ap[:, c, 0:n], qb.ap[:, c, 0:n], qtab.ap[:, 1, hh * 512:hh * 512 + n], ALU.mult, R=[qb.b, qtab.b], W=[qdb.b])
                        k.dma("pool", fm("rqdf", t0, n), qdf.ap[:, :, 0:n], reads=[qdf.b])
                        k.dma("pool", fm("rqdb", t0, n), qdb.ap[:, :, 0:n], reads=[qdb.b])
                    else:
                        kdf, kdb = kdfr.next(), kdbr.next()
                        rks = int(os.environ.get("RK_SKIP", "0"))
                        for ts_ in range(0 if rks == 1 else n // 128):
                            for cg in range(2):
                                p = pp.next()
                                pb = psbf[p]
                                for cc in range(4):
                                    k.tr(pb[:, cc * 128:(cc + 1) * 128], qb.ap[:, cg * 4 + cc, ts_ * 128:(ts_ + 1) * 128], ident_b.ap,
                                         R=[qb.b, ident_b.b], W=[p.b])
                                for h2 in range(2):
                                    hh = cg * 2 + h2
                                    k.act(kdf.ap[:, ts_, hh * 256:(hh + 1) * 256], pb[:, h2 * 256:(h2 + 1) * 256], AF.Identity,
                                          scale=S("r_kdf", hh, 1), R=[p.b, smt.b], W=[kdf.b])
                                for h2 in range(2):
                                    hh = cg * 2 + h2
                                    k.ts("dve", kdb.ap[:, ts_, hh * 256:(hh + 1) * 256], pb[:, h2 * 256:(h2 + 1) * 256], S("r_kdb", hh, 1), None,
                                         ALU.mult, R=[p.b, smt.b, kdf.b], W=[kdb.b])
                        if rks == 0:
                            k.dma("pool", tm("rkdf", t0, n), kdf.ap[:, 0:n // 128, :], reads=[kdf.b])
                            k.dma("pool", tm("rkdb", t0, n), kdb.ap[:, 0:n // 128, :], reads=[kdb.b])
                k.barrier()
                if rstop <= (0 if which == "q" else 1):
                    return True
            wv = wload("rwv", 4, 512)
            wg = wload("rwg", 16, 128)
            xt, hr, sq, tmp, rs = std_tiles()
            vt = k.sb("vt", [4, 2048], BF16)
            sgt = k.sb("sgt", [16, 512], BF16)
            pp = Rot([ps[i] for i in range(1, 8)])
            ev = 0
            for (t0, n, kind) in bl:
                x, h = xt.next(), hr.next()
                load_norm(dr["xs"], t0, n, kind, l, 1, x, h, sq, rs, tmp, ps[0])
                for ts_ in range(n // 128):
                    for vg in range(4):
                        p = pp.next()
                        for kc in range(8):
                            k.mm(p.ap, h.ap[:, kc, ts_ * 128:(ts_ + 1) * 128], wv.ap[:, vg, kc, :], start=(kc == 0), stop=(kc == 7),
                                 R=[h.b, wv.b], W=[p.b])
                        k.copy("act" if ev % 2 == 0 else "dve", vt.ap[:, ts_, vg * 512:(vg + 1) * 512], p.ap, R=[p.b], W=[vt.b])
                        ev += 1
                for oc in range(16):
                    p = pp.next()
                    for kc in range(8):
                        k.mm(p.ap[:, 0:n], wg.ap[:, oc, kc, :], h.ap[:, kc, 0:n], start=(kc == 0), stop=(kc == 7), R=[wg.b, h.b], W=[p.b])
                    k.act(sgt.ap[:, oc, 0:n], p.ap[:, 0:n], AF.Silu, R=[p.b], W=[sgt.b])
                k.dma("pool", tm("rv", t0, n), vt.ap[:, 0:n // 128, :], reads=[vt.b])
                k.dma("pool", fm("rsg", t0, n), sgt.ap[:, :, 0:n], reads=[sgt.b])
            k.barrier()
            if rstop <= 2:
                return True
            for sweep in range(2):
                St = k.sb("S", [2, 512])
                Sb = k.sb("Sb", [2, 512], BF16)
                qtr = Rot([k.sb(f"qt{i}", [2, 512], BF16) for i in range(2)])
                ktr = Rot([k.sb(f"kt{i}", [2, 512], BF16) for i in range(2)])
                qdr = Rot([k.sb(f"qd{i}", [2, 512], BF16) for i in range(2)])
                vtr = Rot([k.sb(f"vt{i}", [4, 512], BF16) for i in range(2)])
                kdr = Rot([k.sb(f"kd{i}", [4, 256], BF16) for i in range(2)])
                ptr_ = Rot([k.sb(f"pt{i}", [128], BF16) for i in range(2)])
                otr = Rot([k.sb(f"ot{i}", [4, 512]) for i in range(2)])
                o1r = Rot([k.sb(f"o1{i}", [4, 512]) for i in range(2)])
                sgr = Rot([k.sb(f"sg{i}", [4, 512], BF16) for i in range(2)])
                yrr = Rot([k.sb(f"yr{i}", [4, 512], BF16) for i in range(2)])
                sq = Rot([k.sb(f"sq{i}", [512]) for i in range(2)])
                rs = k.sb("rs", [512])
                o_ps = [ps[0], ps[1], ps[2], ps[3]]
                KV = [ps[5], ps[6]]
                sTr = Rot([ps[4], ps[7]])
                order = bl if sweep == 0 else [bl[0]] + bl[:0:-1]
                for hh in range(4):
                    k.memset("dve", St.ap, 0.0, W=[St.b])
                    k.memset("dve", Sb.ap, 0.0, W=[Sb.b])
                    hsl = lambda nm, t0, n: dr[nm][2 * hh:2 * hh + 2, :, t0:t0 + n].rearrange("c p t -> p c t")
                    osl = lambda nm, t0, n: dr[nm][4 * hh:4 * hh + 4, :, t0:t0 + n].rearrange("c p t -> p c t")
                    for (t0, n, kind) in order:
                        nch = n // 128
                        qd, vt_, kd = qdr.next(), vtr.next(), kdr.next()
                        k.dma("sp", qd.ap[:, :, 0:n], hsl("rqdf" if sweep == 0 else "rqdb", t0, n), writes=[qd.b])
                        k.dma("sp", vt_.ap[:, 0:nch, :], dr["rv"][t0:t0 + n, hh * 512:(hh + 1) * 512].rearrange("(n p) d -> p n d", p=128), writes=[vt_.b])
                        k.dma("sp", kd.ap[:, 0:nch, :], dr["rkdf" if sweep == 0 else "rkdb"][t0:t0 + n, hh * 256:(hh + 1) * 256].rearrange("(n p) d -> p n d", p=128),
                              writes=[kd.b])
                        if sweep == 0:
                            qt, kt = qtr.next(), ktr.next()
                            k.dma("sp", qt.ap[:, :, 0:n], hsl("rq", t0, n), writes=[qt.b])
                            k.dma("sp", kt.ap[:, :, 0:n], hsl("rk", t0, n), writes=[kt.b])
                        else:
                            o1, sg = o1r.next(), sgr.next()
                            k.dma("sp", o1.ap[:, :, 0:n], osl("ro", t0, n), writes=[o1.b])
                            k.dma("sp", sg.ap[:, :, 0:n], osl("rsg", t0, n), writes=[sg.b])
                        chs = range(nch) if sweep == 0 else range(nch - 1, -1, -1)
                        for ch in chs:
                            cs_ = slice(ch * 128, (ch + 1) * 128)
                            if sweep == 0:
                                sT = sTr.next()
                                for kc in range(2):
                                    k.mm(sT.ap[:, 0:128], kt.ap[:, kc, cs_], qt.ap[:, kc, cs_], start=(kc == 0), stop=(kc == 1), R=[kt.b, qt.b], W=[sT.b])
                                PT = ptr_.next()
                                k.tt("dve", PT.ap, sT.ap[:, 0:128], S("r_mask", hh * 128, 128), ALU.mult, R=[sT.b, smt.b], W=[PT.b])
                            for dvc in range(4):
                                if sweep == 0:
                                    k.mm(o_ps[dvc].ap[:, cs_], vt_.ap[:, ch, dvc * 128:(dvc + 1) * 128], PT.ap, start=True, stop=False,
                                         R=[vt_.b, PT.b], W=[o_ps[dvc].b])
                                for kc in range(2):
                                    k.mm(o_ps[dvc].ap[:, cs_], Sb.ap[:, kc, dvc * 128:(dvc + 1) * 128], qd.ap[:, kc, cs_],
                                         start=(sweep == 1 and kc == 0), stop=(kc == 1), R=[Sb.b, qd.b], W=[o_ps[dvc].b])
                            for kc in range(2):
                                k.mm(KV[kc].ap, kd.ap[:, ch, kc * 128:(kc + 1) * 128], vt_.ap[:, ch, :], R=[kd.b, vt_.b], W=[KV[kc].b])
                                k.stt("dve", St.ap[:, kc, :], St.ap[:, kc, :], S("r_cd", hh, 1), KV[kc].ap, ALU.mult, ALU.add,
                                      R=[St.b, KV[kc].b, smt.b], W=[St.b])
                                k.copy("act", Sb.ap[:, kc, :], St.ap[:, kc, :], R=[St.b], W=[Sb.b])
                        ot = otr.next()
                        if sweep == 0:
                            for dvc in range(4):
                                k.copy("act", ot.ap[:, dvc, 0:n], o_ps[dvc].ap[:, 0:n], R=[o_ps[dvc].b], W=[ot.b])
                            k.dma("pool", osl("ro", t0, n), ot.ap[:, :, 0:n], reads=[ot.b])
                        else:
                            yr = yrr.next()
                            for dvc in range(4):
                                k.tt("dve", ot.ap[:, dvc, 0:n], o1.ap[:, dvc, 0:n], o_ps[dvc].ap[:, 0:n], ALU.add, R=[o1.b, o_ps[dvc].b], W=[ot.b])
                                q_ = sq.next()
                                k.act(q_.ap[:, 0:n], ot.ap[:, dvc, 0:n], AF.Square, R=[ot.b], W=[q_.b])
                                k.mm(ps[4].ap[:, 0:n], ones_f.ap, q_.ap[:, 0:n], start=(dvc == 0), stop=(dvc == 3), R=[q_.b, ones_f.b], W=[ps[4].b])
                            k.rstd(rs.ap[:, 0:n], rs.b, ps[4].ap[:, 0:n], ps[4].b, 1.0 / 512)
                            for dvc in range(4):
                                k.tt("dve", ot.ap[:, dvc, 0:n], ot.ap[:, dvc, 0:n], rs.ap[:, 0:n], ALU.mult, R=[ot.b, rs.b], W=[ot.b])
                                k.tt("pool", yr.ap[:, dvc, 0:n], ot.ap[:, dvc, 0:n], sg.ap[:, dvc, 0:n], ALU.mult, R=[ot.b, sg.b], W=[yr.b])
                            k.dma("pool", osl("ryr", t0, n), yr.ap[:, :, 0:n], reads=[yr.b])
                k.barrier()
                if rstop <= 3 + sweep:
                    return True
            outproj_phase(l, dr["ryr"], "rwo", 128, 16)
            return done()

        PH["ret"] = ret_phase

        def dif_phase(l):
            dstop = int(os.environ.get("DIF_STOP", "99"))
            gb = GB("dwq")
            bl = blocks(512)
            fm = lambda nm, t0, n: dr[nm][:, :, t0:t0 + n].rearrange("c p t -> p c t")

            def wload(name, ng, gw):
                wt = k.sb("w_" + name, [ng, 8, gw], BF16)
                k.dma("sp", wt.ap, wbf[name].rearrange("(g p) (kc c) -> p g kc c", p=128, kc=8), reads=[gb], writes=[wt.b])
                return wt

            for which in ("q", "k"):
                w1 = wload("dw" + which, 8, 128)
                w2 = wload("dw" + which + "s", 8, 128)
                if which == "q":
                    wv = wload("dwv", 2, 512)
                    vt = k.sb("vt", [4, 1024], BF16)
                xt = Rot([k.sb(f"x{i}", [8, 512]) for i in range(2)])
                hr = Rot([k.sb(f"h{i}", [8, 512], BF16) for i in range(2)])
                sq = Rot([k.sb(f"sq{i}", [512]) for i in range(2)])
                tmp = Rot([k.sb(f"tmp{i}", [512]) for i in range(2)])
                rs = k.sb("rs", [512])
                csr = Rot([k.sb(f"cos{i}", [512]) for i in range(2)])
                snr = Rot([k.sb(f"sin{i}", [512]) for i in range(2)])
                t1r = Rot([k.sb(f"t1{i}", [512]) for i in range(3)])
                t2r = Rot([k.sb(f"t2{i}", [512]) for i in range(3)])
                qbr = Rot([k.sb(f"qb{i}", [8, 512], BF16) for i in range(2)])
                pp = Rot([ps[i] for i in range(1, 8)])
                ev = 0
                for (t0, n, kind) in bl:
                    x, h = xt.next(), hr.next()
                    load_norm(dr["xs"], t0, n, kind, l, 1, x, h, sq, rs, tmp, ps[0])
                    qb = qbr.next()
                    if kind == 0:
                        cos, sin = csr.next(), snr.next()
                        k.dma("sp", cos.ap[:, 0:n], dr["d_cos"][:, t0 - CTX:t0 - CTX + n], writes=[cos.b])
                        k.dma("sp", sin.ap[:, 0:n], dr["d_sin"][:, t0 - CTX:t0 - CTX + n], writes=[sin.b])
                    for oc in range(8):
                        p1 = pp.next()
                        for kc in range(8):
                            k.mm(p1.ap[:, 0:n], w1.ap[:, oc, kc, :], h.ap[:, kc, 0:n], start=(kc == 0), stop=(kc == 7), R=[w1.b, h.b], W=[p1.b])
                        if kind == 1:
                            k.copy("act", qb.ap[:, oc, 0:n], p1.ap[:, 0:n], R=[p1.b], W=[qb.b])
                            continue
                        p2 = pp.next()
                        for kc in range(8):
                            k.mm(p2.ap[:, 0:n], w2.ap[:, oc, kc, :], h.ap[:, kc, 0:n], start=(kc == 0), stop=(kc == 7), R=[w2.b, h.b], W=[p2.b])
                        t1, t2 = t1r.next(), t2r.next()
                        k.tt("dve", t1.ap[:, 0:n], p1.ap[:, 0:n], cos.ap[:, 0:n], ALU.mult, R=[p1.b, cos.b], W=[t1.b])
                        k.tt("dve", t2.ap[:, 0:n], p2.ap[:, 0:n], sin.ap[:, 0:n], ALU.mult, R=[p2.b, sin.b], W=[t2.b])
                        k.tt("pool", qb.ap[:, oc, 0:n], t1.ap[:, 0:n], t2.ap[:, 0:n], ALU.add, R=[t1.b, t2.b], W=[qb.b])
                    k.dma("pool", fm("d" + which, t0, n), qb.ap[:, :, 0:n], reads=[qb.b])
                    if which == "q":
                        for ts_ in range(n // 128):
                            for vg in range(2):
                                p = pp.next()
                                for kc in range(8):
                                    k.mm(p.ap, h.ap[:, kc, ts_ * 128:(ts_ + 1) * 128], wv.ap[:, vg, kc, :], start=(kc == 0), stop=(kc == 7),
                                         R=[h.b, wv.b], W=[p.b])
                                k.copy("act" if ev % 2 == 0 else "dve", vt.ap[:, ts_, vg * 512:(vg + 1) * 512], p.ap, R=[p.b], W=[vt.b])
                                ev += 1
                        k.dma("pool", dr["dv"][t0:t0 + n, :].rearrange("(n p) d -> p n d", p=128), vt.ap[:, 0:n // 128, :], reads=[vt.b])
                k.barrier()
            if dstop <= 0:
                return True
            NKT = TT // 128
            Khr = Rot([k.sb(f"Kh{i}", [TT], BF16) for i in range(2)])
            var = [k.sb(f"vaug{i}", [NKT, 129], BF16) for i in range(2)]
            for v_ in var:
                k.memset("dve", v_.ap, 1.0, W=[v_.b])
            qtr = Rot([k.sb(f"qt{i}", [512], BF16) for i in range(2)])
            p1r = Rot([k.sb(f"P1{i}", [512], BF16) for i in range(3)])
            p2r = Rot([k.sb(f"P2{i}", [512], BF16) for i in range(3)])
            rlr = Rot([k.sb(f"rl{i}", [8]) for i in range(4)])
            odr = Rot([k.sb(f"od{i}", [128]) for i in range(2)])
            jkr = Rot([k.sb(f"jk{i}", [128]) for i in range(2)])
            ytr = Rot([k.sb(f"yt{i}", [128], BF16) for i in range(2)])
            yfr = Rot([k.sb(f"yf{i}", [512], BF16) for i in range(2)])
            S1r = Rot([ps[0], ps[1]])
            S2r = Rot([ps[2], ps[3]])
            OB = [ps[4], ps[5], ps[6]]
            tp = ps[7]
            tpb = psbf[tp]
            for hh in range(8):
                Kh, vaug = Khr.next(), var[hh % 2]
                k.dma("sp", Kh.ap, dr["dk"][hh], writes=[Kh.b])
                k.dma("sp", vaug.ap[:, :, 0:128], dr["dv"][:, hh * 128:(hh + 1) * 128].rearrange("(n p) d -> p n d", p=128), writes=[vaug.b])
                for (t0, n, kind) in bl:
                    qt = qtr.next()
                    k.dma("sp", qt.ap[:, 0:n], dr["dq"][hh, :, t0:t0 + n], writes=[qt.b])
                    nkt = (CTX // 128) if kind == 1 else NKT
                    nqs = n // 128
                    for bank in OB:
                        k.mm(bank.ap, zeros_b.ap[:, 0:128], zeros_b.ap, start=True, stop=False, R=[zeros_b.b], W=[bank.b], skip_group_check=True)
                    for kt in range(nkt):
                        s1, s2 = S1r.next(), S2r.next()
                        k.mm(s1.ap[:, 0:n], Kh.ap[0:64, kt * 128:(kt + 1) * 128], qt.ap[0:64, 0:n], R=[Kh.b, qt.b], W=[s1.b])
                        k.mm(s2.ap[:, 0:n], Kh.ap[64:128, kt * 128:(kt + 1) * 128], qt.ap[64:128, 0:n], R=[Kh.b, qt.b], W=[s2.b])
                        P1, P2 = p1r.next(), p2r.next()
                        k.act(P1.ap[:, 0:n], s1.ap[:, 0:n], AF.Exp, scale=0.125, R=[s1.b], W=[P1.b])
                        k.act(P2.ap[:, 0:n], s2.ap[:, 0:n], AF.Exp, scale=0.125, R=[s2.b], W=[P2.b])
                        for qs in range(nqs):
                            for m, P in ((0, P1), (1, P2)):
                                ti = m * 4 + qs
                                bank, c0 = OB[ti // 3], (ti % 3) * 129
                                k.mm(bank.ap[:, c0:c0 + 129], P.ap[:, qs * 128:(qs + 1) * 128], vaug.ap[:, kt, :], start=False,
                                     stop=(kt == nkt - 1), R=[P.b, vaug.b], W=[bank.b], skip_group_check=True)
                    for qs in range(nqs):
                        b1, c1 = OB[qs // 3], (qs % 3) * 129
                        b2, c2 = OB[(4 + qs) // 3], ((4 + qs) % 3) * 129
                        rl, od, jk, yt = rlr.next(), odr.next(), jkr.next(), ytr.next()
                        k.op("dve", lambda e, rl=rl, b1=b1, c1=c1: e.reciprocal(out=rl.ap[:, 0:1], in_=b1.ap[:, c1 + 128:c1 + 129]), [b1.b], [rl.b])
                        k.op("dve", lambda e, rl=rl, b2=b2, c2=c2: e.reciprocal(out=rl.ap[:, 1:2], in_=b2.ap[:, c2 + 128:c2 + 129]), [b2.b, rl.b], [rl.b])
                        k.tt("dve", rl.ap[:, 2:3], rl.ap[:, 1:2], lamt.ap[:, 3:4], ALU.mult, R=[rl.b, lamt.b], W=[rl.b])
                        k.ts("dve", od.ap, b1.ap[:, c1:c1 + 128], rl.ap[:, 0:1], None, ALU.mult, R=[b1.b, rl.b], W=[od.b])
                        k.stt("dve", od.ap, b2.ap[:, c2:c2 + 128], rl.ap[:, 2:3], od.ap, ALU.mult, ALU.add, R=[b2.b, rl.b, od.b], W=[od.b])
                        k.act(jk.ap, od.ap, AF.Square, accum_out=rl.ap[:, 3:4], R=[od.b, rl.b], W=[jk.b, rl.b])
                        k.act(rl.ap[:, 4:5], rl.ap[:, 3:4], AF.Sqrt, bias=EPS, scale=1.0 / 128, R=[rl.b], W=[rl.b])
                        k.op("dve", lambda e, rl=rl: e.reciprocal(out=rl.ap[:, 5:6], in_=rl.ap[:, 4:5]), [rl.b], [rl.b])
                        k.stt("dve", yt.ap, od.ap, rl.ap[:, 5:6], gsub.ap, ALU.mult, ALU.mult, R=[od.b, rl.b, gsub.b], W=[yt.b])
                        k.tr(tpb[:, qs * 128:(qs + 1) * 128], yt.ap, ident_b.ap, R=[yt.b, ident_b.b], W=[tp.b])
                    yf = yfr.next()
                    k.copy("act", yf.ap[:, 0:n], tpb[:, 0:n], R=[tp.b], W=[yf.b])
                    k.dma("pool", dr["dya"][hh, :, t0:t0 + n], yf.ap[:, 0:n], reads=[yf.b])
            k.barrier()
            if dstop <= 1:
                return True
            outproj_phase(l, dr["dya"], "dwo", 128, 8)
            return done()

        PH["dif"] = dif_phase

        def final_phase():
            xt = Rot([k.sb(f"x{i}", [8, 1024]) for i in range(2)])
            sq = Rot([k.sb(f"sq{i}", [512]) for i in range(2)])
            rs = k.sb("rs", [1024])
            for i in range(SEQ // 1024):
                t0 = CTX + 1024 * i
                x = xt.next()
                k.dma("sp", x.ap, dr["xs"][:, t0:t0 + 1024].rearrange("(c p) t -> p c t", p=128), writes=[x.b])
                for sub in range(0, 1024, 512):
                    for c in range(8):
                        q = sq.next()
                        k.act(q.ap, x.ap[:, c, sub:sub + 512], AF.Square, R=[x.b], W=[q.b])
                        k.mm(ps[0].ap, ones_f.ap, q.ap, start=(c == 0), stop=(c == 7), R=[q.b, ones_f.b], W=[ps[0].b])
                    k.rstd(rs.ap[:, sub:sub + 512], rs.b, ps[0].ap, ps[0].b, 1.0 / D)
                    for c in range(8):
                        k.stt("dve", x.ap[:, c, sub:sub + 512], x.ap[:, c, sub:sub + 512], S("final_g", c, 1), rs.ap[:, sub:sub + 512],
                              ALU.mult, ALU.mult, R=[x.b, rs.b, smt.b], W=[x.b])
                k.dma("pool", dr["out"][:, 1024 * i:1024 * (i + 1)].rearrange("(c p) t -> p c t", p=128), x.ap, reads=[x.b])

        def program():
            if SIM_ONLY in ("rk",):
                ret_phase(1)
                return
            for l in range(DEPTH):
                ffn_phase(l, 0)
                if done():
                    return
                if l in MIXERS:
                    if MIXERS[l](l):
                        return
                ffn_phase(l, 1)
                if done():
                    return
            final_phase()

        MIXERS = {}
        if "lru" in PH:
            MIXERS[0] = PH["lru"]
            MIXERS[3] = PH["lru"]
        if "ret" in PH:
            MIXERS[1] = PH["ret"]
        if "dif" in PH:
            MIXERS[2] = PH["dif"]
        program()
        k.barrier(final=True)
        k.emit(esems, dsems)
    return nc


NONW = ("smalls", "mod_w", "r_cos", "r_sin", "d_cos", "d_sin", "rqtab")


def kernel(**inputs):
    inp = {k_: np.asarray(v) for k_, v in inputs.items()}
    sh, soff = prep_static(inp)
    wshapes = {k_: tuple(v.shape) for k_, v in sh.items()
               if k_ not in NONW}
    ns = sh["smalls"].shape[1]
    nc = build_program(soff, ns, wshapes)
    in_maps = []
    for b in range(NCORES):
        m = dict(sh)
        m["xin"] = np.ascontiguousarray(np.concatenate([inp["ctx"][b].T, inp["x"][b].T], axis=1), dtype=np.float32)
        m["cond"] = col128(inp["c"][b])
        in_maps.append(m)
    res = run_bass_kernel_spmd(nc, in_maps, core_ids=list(range(NCORES)))
    out = np.stack([np.ascontiguousarray(res.results[b]["out"].T) for b in range(NCORES)], axis=0)
    return out.astype(np.float32)
```

```python
import math
import os
import numpy as np
import concourse.bass as bass
import concourse.mybir as mybir
from concourse.bass_utils import run_bass_kernel_spmd

F32 = mybir.dt.float32
BF16 = mybir.dt.bfloat16
AF = mybir.ActivationFunctionType
ALU = mybir.AluOpType

D = 1024
SEQ = 8192
CTX = 256
TT = SEQ + CTX
DEPTH = 4
DFF = 2816
NFC = 22
DRNN = 1280
EPS = 1e-6
NCORES = 4
EPOCH = 60000
NENGSEM = 6
ARENA = 52000
DEBUG_LINES = bool(os.environ.get("KB_DUMP"))


class Buf:
    __slots__ = ("name", "w", "rd", "dsem")

    def __init__(self, name):
        self.name = name
        self.w = None
        self.rd = {}
        self.dsem = {}


class Ins:
    __slots__ = ("fn", "waits", "mark", "dma", "cnt", "real", "line")

    def __init__(self, fn):
        self.fn = fn
        self.line = 0
        self.waits = []
        self.mark = False
        self.dma = None
        self.cnt = 0
        self.real = fn is not None


class T:
    __slots__ = ("ap", "b")

    def __init__(self, ap, b):
        self.ap = ap
        self.b = b

    def __getitem__(self, idx):
        return self.ap[idx]


COMPUTE = ("pe", "act", "dve", "pool")
ENGS = ("pe", "act", "dve", "pool", "sp")


class KB:
    def __init__(self, nc):
        self.nc = nc
        self.streams = {e: [] for e in ENGS}
        self.seen = {e: {} for e in ENGS}
        self.last_real = {e: -1 for e in ENGS}
        self.ndsem = 100 - 4 * NENGSEM - 2
        self.dsem_vals = [0] * self.ndsem
        self.nsw = 30
        self.free_dsems = {"sw": list(range(self.nsw)), "hw": list(range(self.nsw, self.ndsem))}
        self.bg_sems = set()
        self.live_bufs = []
        self.arena_top = 0
        self.arena_base = 0
        self.persist = []

    def buf(self, name):
        b = Buf(name)
        self.live_bufs.append(b)
        return b

    def sb(self, name, shape, dtype=F32, parts=128):
        n = 1
        for s_ in shape:
            n *= s_
        words = n if dtype == F32 else (n + 1) // 2
        words += words & 1
        off = self.arena_top
        self.arena_top += words
        assert self.arena_top <= ARENA, (name, self.arena_top)
        if dtype == F32:
            ap = self.arena_f[0:parts, off:off + n]
        else:
            ap = self.arena_b[0:parts, 2 * off:2 * off + n]
        if len(shape) == 2:
            ap = ap.rearrange("p (a b) -> p a b", a=shape[0])
        elif len(shape) == 3:
            ap = ap.rearrange("p (a b c) -> p a b c", a=shape[0], b=shape[1])
        return T(ap, self.buf(name))

    def phase_reset(self):
        self.arena_top = self.arena_base

    def _wait_list(self, eng, reads, writes):
        waits = []
        raw = []
        war = []
        for b in reads:
            if b.w is not None:
                raw.append(b.w)
        for b in writes:
            if b.w is not None:
                raw.append(b.w)
            war.extend(b.rd.values())
        seen = self.seen[eng]
        for lst, is_war in ((raw, False), (war, True)):
            for t in lst:
                if t[0] == "e":
                    src, idx = t[1], t[2]
                    if src == eng and (eng == "pe" or is_war):
                        continue
                    if seen.get(src, -1) >= idx:
                        continue
                    seen[src] = idx
                    self.streams[src][idx].mark = True
                    waits.append(t)
                else:
                    key = ("d", t[1])
                    if seen.get(key, -1) >= t[2]:
                        continue
                    seen[key] = t[2]
                    waits.append(t)
        return waits

    def op(self, eng, fn, reads=(), writes=()):
        ins = Ins(fn)
        if DEBUG_LINES:
            import sys
            ins.line = sys._getframe(2).f_lineno
        ins.waits = self._wait_list(eng, reads, writes)
        idx = len(self.streams[eng])
        self.streams[eng].append(ins)
        self.last_real[eng] = idx
        tok = ("e", eng, idx)
        for b in reads:
            b.rd[eng] = tok
        for b in writes:
            b.w = tok
            b.rd = {}
        return ins

    def dma(self, q, out, in_, reads=(), writes=(), owner=None):
        ins = Ins(lambda e: e.dma_start(out=out, in_=in_))
        if DEBUG_LINES:
            import sys
            ins.line = -sys._getframe(1).f_lineno
        ins.waits = self._wait_list(q, reads, writes)
        if owner is None:
            owner = writes[0] if writes else reads[0]
        cls = "sw" if q == "pool" else "hw"
        if cls not in owner.dsem:
            owner.dsem[cls] = self.free_dsems[cls].pop()
        sem = owner.dsem[cls]
        self.dsem_vals[sem] += 16
        tok = ("d", sem, self.dsem_vals[sem])
        ins.dma = sem
        self.streams[q].append(ins)
        for b in reads:
            b.rd[("d", sem)] = tok
        for b in writes:
            b.w = tok
            b.rd = {}
        return tok

    def bg_dma(self, q, out, in_, gbuf):
        ins = Ins(lambda e: e.dma_start(out=out, in_=in_))
        if "sw" not in gbuf.dsem:
            gbuf.dsem["sw"] = self.free_dsems["sw"].pop()
            self.bg_sems.add(gbuf.dsem["sw"])
        sem = gbuf.dsem["sw"]
        self.dsem_vals[sem] += 16
        ins.dma = sem
        self.streams[q].append(ins)
        gbuf.w = ("d", sem, self.dsem_vals[sem])

    def barrier(self, final=False, soft=False):
        last = dict(self.last_real)
        for eng in ENGS:
            ins = Ins(None)
            seen = self.seen[eng]
            for src in COMPUTE:
                idx = last[src]
                if src == eng or idx < 0:
                    continue
                if seen.get(src, -1) >= idx:
                    continue
                seen[src] = idx
                self.streams[src][idx].mark = True
                ins.waits.append(("e", src, idx))
            for sem in range(self.ndsem):
                val = self.dsem_vals[sem]
                if val == 0 or (sem in self.bg_sems and not final):
                    continue
                key = ("d", sem)
                if seen.get(key, -1) >= val:
                    continue
                seen[key] = val
                ins.waits.append(("d", sem, val))
            self.streams[eng].append(ins)
        for b in self.live_bufs:
            for cls, sem in list(b.dsem.items()):
                if sem in self.bg_sems:
                    continue
                if sem not in self.free_dsems[cls]:
                    self.free_dsems[cls].append(sem)
                del b.dsem[cls]
        if not soft:
            self.live_bufs = list(self.persist)
            self.phase_reset()

    def emit(self, esems, dsems):
        nc = self.nc
        for eng in COMPUTE:
            c = 0
            for ins in self.streams[eng]:
                if ins.mark:
                    c += 1
                    ins.cnt = c
            assert c <= EPOCH * NENGSEM, (eng, c)
            if os.environ.get("KB_STATS"):
                print("KB", eng, "n_ins", len(self.streams[eng]), "marked", c, "waits", sum(len(i.waits) for i in self.streams[eng]))
        if os.environ.get("KB_STATS"):
            print("KB sp n_ins", len(self.streams["sp"]), "max dsem", max(self.dsem_vals), self.dsem_vals)
        streams = self.streams
        if os.environ.get("KB_DUMP"):
            with open(os.environ["KB_DUMP"], "w") as f:
                for eng in ENGS:
                    f.write("=== %s\n" % eng)
                    for i, ins in enumerate(streams[eng]):
                        ws = []
                        for t in ins.waits:
                            if t[0] == "e":
                                ws.append("%s#%d(c%d)" % (t[1], t[2], streams[t[1]][t[2]].cnt))
                            else:
                                ws.append("d%d>=%d" % (t[1], t[2]))
                        f.write("%d L%d %s%s waits=[%s]\n" % (i, ins.line, "M%d " % ins.cnt if ins.mark else "", "DMA%d " % ins.dma if ins.dma is not None else "", ",".join(ws)))

        def semval(t):
            if t[0] == "e":
                cnt = streams[t[1]][t[2]].cnt
                assert cnt > 0
                return esems[t[1]][(cnt - 1) // EPOCH], (cnt - 1) % EPOCH + 1
            return dsems[t[1]], t[2]

        def run(eng, e):
            for ins in streams[eng]:
                for t in ins.waits:
                    sm, v = semval(t)
                    e.wait_ge(sm, v)
                if ins.fn is None:
                    continue
                r = ins.fn(e)
                if ins.dma is not None:
                    r.then_inc(dsems[ins.dma], 16)
                elif ins.mark:
                    r.then_inc(esems[eng][(ins.cnt - 1) // EPOCH], 1)

        with nc.Block() as block:
            @block.tensor
            def _(e):
                run("pe", e)

            @block.scalar
            def _(e):
                run("act", e)

            @block.vector
            def _(e):
                run("dve", e)

            @block.gpsimd
            def _(e):
                run("pool", e)

            @block.sync
            def _(e):
                run("sp", e)

    def mm(self, out, lhsT, rhs, start=True, stop=True, R=(), W=(), **kw):
        return self.op("pe", lambda e: e.matmul(out, lhsT, rhs, start=start, stop=stop, **kw), R, W)

    def tr(self, out, in_, ident, R=(), W=()):
        return self.op("pe", lambda e: e.transpose(out, in_, ident), R, W)

    def act(self, out, in_, func, R=(), W=(), bias=None, scale=None, accum_out=None):
        kw = {}
        if accum_out is not None:
            kw["accum_out"] = accum_out
        if bias is not None:
            kw["bias"] = bias
        if scale is not None:
            kw["scale"] = scale
        return self.op("act", lambda e: e.activation(out=out, in_=in_, func=func, **kw), R, W)

    def tt(self, eng, out, in0, in1, op, R=(), W=()):
        return self.op(eng, lambda e: e.tensor_tensor(out=out, in0=in0, in1=in1, op=op), R, W)

    def ts(self, eng, out, in0, s1, s2, op0, op1=None, R=(), W=()):
        if op1 is None:
            return self.op(eng, lambda e: e.tensor_scalar(out=out, in0=in0, scalar1=s1, scalar2=None, op0=op0), R, W)
        return self.op(eng, lambda e: e.tensor_scalar(out=out, in0=in0, scalar1=s1, scalar2=s2, op0=op0, op1=op1), R, W)

    def stt(self, eng, out, in0, scalar, in1, op0, op1, R=(), W=()):
        return self.op(eng, lambda e: e.scalar_tensor_tensor(out=out, in0=in0, scalar=scalar, in1=in1, op0=op0, op1=op1), R, W)

    def copy(self, eng, out, in_, R=(), W=()):
        if eng == "act":
            return self.op("act", lambda e: e.copy(out=out, in_=in_), R, W)
        return self.op(eng, lambda e: e.tensor_copy(out=out, in_=in_), R, W)

    def rstd(self, out, outb, in_, inb, scale):
        self.act(out, in_, AF.Sqrt, bias=EPS, scale=scale, R=[inb], W=[outb])
        self.op("dve", lambda e: e.reciprocal(out=out, in_=out), [outb], [outb])

    def scan(self, out, a, b, init, R=(), W=()):
        return self.op("dve", lambda e: e.tensor_tensor_scan(out=out, data0=a, data1=b, initial=init,
                                                             op0=ALU.mult, op1=ALU.add), R, W)

    def memset(self, eng, ap, val, W=()):
        return self.op(eng, lambda e: e.memset(ap, val), (), W)


class Rot:
    def __init__(self, tiles):
        self.t = tiles
        self.i = 0

    def next(self):
        r = self.t[self.i % len(self.t)]
        self.i += 1
        return r


def pack_w(W, kp, gw):
    K, M = W.shape
    nk, ng = K // kp, M // gw
    a = W.reshape(nk, kp, ng, gw).transpose(2, 1, 0, 3)
    return np.ascontiguousarray(a).reshape(ng, kp, nk * gw)


def col128(v):
    v = np.asarray(v, np.float32)
    lead = v.shape[:-1]
    n = v.shape[-1] // 128
    a = v.reshape(*lead, n, 128)
    a = np.moveaxis(a, -1, 0)
    return np.ascontiguousarray(a).reshape(128, -1)


def col80(v):
    v = np.asarray(v, np.float32)
    lead = v.shape[:-1]
    n = v.shape[-1] // 80
    a = v.reshape(*lead, n, 80)
    a = np.moveaxis(a, -1, 0)
    out = np.zeros((128, a.reshape(80, -1).shape[1]), np.float32)
    out[:80] = a.reshape(80, -1)
    return out


class Smalls:
    def __init__(self):
        self.cols = []
        self.off = {}
        self.n = 0

    def add(self, name, arr):
        arr = np.asarray(arr, np.float32)
        assert arr.shape[0] == 128
        self.off[name] = (self.n, arr.shape[1])
        self.cols.append(arr)
        self.n += arr.shape[1]

    def build(self):
        return np.ascontiguousarray(np.concatenate(self.cols, axis=1))


RET_H = 4
RET_DK = 256
RET_DV = 512


def ret_consts():
    lg = np.log1p(-np.exp2(-5.0 - np.arange(RET_H, dtype=np.float32))).astype(np.float32)
    idx = np.arange(128, dtype=np.float32)
    dist = np.abs(idx[None, :] - idx[:, None])
    maskT = np.exp(lg[:, None, None] * dist[None]).astype(np.float32)
    qdf = np.exp(lg[:, None] * (idx[None] + 1.0)).astype(np.float32)
    qdb = np.exp(lg[:, None] * (128.0 - idx[None])).astype(np.float32)
    kdf = np.exp(lg[:, None] * (127.0 - idx[None])).astype(np.float32)
    kdb = np.exp(lg[:, None] * idx[None]).astype(np.float32)
    cd = np.exp(lg * 128.0).astype(np.float32)
    return maskT, qdf, qdb, kdf, kdb, cd


def prep_static(inp):
    sm = Smalls()
    sm.add("norm_g", col128(inp["norm_g"]))
    sm.add("final_g", col128(inp["final_g"]))
    sm.add("mod_b", col128(inp["mod_b"]))
    sm.add("c_ctx", col128(inp["c_ctx"]))
    sm.add("conv_w", col80(inp["lru_conv_w"]))
    sm.add("conv_b", col80(inp["lru_conv_b"]))
    sm.add("gate_b", col80(inp["lru_gate_b"].reshape(2, 2, 2, DRNN)))
    sm.add("lam", col80(inp["lru_lam"]))
    maskT, qdf, qdb, kdf, kdb, cd = ret_consts()
    sm.add("r_mask", np.ascontiguousarray(maskT.transpose(1, 0, 2)).reshape(128, RET_H * 128))
    sm.add("r_kdf", np.ascontiguousarray(kdf.T))
    sm.add("r_kdb", np.ascontiguousarray(kdb.T))
    rqtab = np.concatenate([np.tile(qdf, (1, 4)).reshape(1, RET_H * 512), np.tile(qdb, (1, 4)).reshape(1, RET_H * 512)], axis=1)
    sm.add("r_cd", np.broadcast_to(cd.reshape(1, RET_H), (128, RET_H)))
    sm.add("dif_lam", np.pad(np.ascontiguousarray(inp["dif_lam"][0].T), ((0, 64), (0, 0))))
    sm.add("dif_subln", np.broadcast_to(inp["dif_subln"][0].reshape(1, 128), (128, 128)))
    sm.add("ident", np.eye(128, dtype=np.float32))
    sh = {"smalls": sm.build()}
    sh["rqtab"] = np.ascontiguousarray(np.broadcast_to(rqtab, (128, 2 * RET_H * 512)), dtype=np.float32)
    W = {}
    for l in range(DEPTH):
        for s in range(2):
            wi = inp["ffn_w_in"][l, s]
            a = wi.reshape(8, 128, 2, NFC, 128).transpose(3, 1, 0, 2, 4)
            W[f"win{l}{s}"] = np.ascontiguousarray(a).reshape(NFC * 128, 2048)
            wo = inp["ffn_w_out"][l, s]
            b = wo.reshape(NFC, 128, 8, 128).transpose(2, 1, 0, 3)
            W[f"wout{l}{s}"] = np.ascontiguousarray(b).reshape(8 * 128, NFC * 128)
    for n, l in enumerate((0, 3)):
        w_in = inp["lru_w_in"][n]
        W[f"lwx{l}"] = pack_w(w_in[:, :DRNN], 128, 80).reshape(16 * 128, 640)
        W[f"lwg{l}"] = pack_w(w_in[:, DRNN:], 128, 80).reshape(16 * 128, 640)
        gw = inp["lru_gate_w"][n]
        a = gw.reshape(2, 2, 8, 2, 80, 2, 80).transpose(4, 0, 1, 2, 3, 5, 6)
        W[f"lgw{l}"] = np.ascontiguousarray(a).reshape(80, 128 * 80)
        W[f"lwo{l}"] = pack_w(inp["lru_w_out"][n], 80, 128).reshape(8 * 80, 16 * 128)
    rw = inp["ret_w_in"][0]
    W["rwq"] = pack_w(rw[:, 0:1024], 128, 128).reshape(8 * 128, 1024)
    W["rwk"] = pack_w(rw[:, 1024:2048], 128, 128).reshape(8 * 128, 1024)
    W["rwv"] = pack_w(rw[:, 2048:4096], 128, 512).reshape(4 * 128, 4096)
    W["rwg"] = pack_w(rw[:, 4096:6144], 128, 128).reshape(16 * 128, 1024)
    W["rwo"] = pack_w(inp["ret_w_out"][0], 128, 128).reshape(8 * 128, 16 * 128)
    dw = inp["dif_w_in"][0]
    perm = np.arange(1024).reshape(16, 2, 32)[:, ::-1, :].reshape(-1)
    W["dwq"] = pack_w(dw[:, 0:1024], 128, 128).reshape(8 * 128, 1024)
    W["dwqs"] = pack_w(dw[:, 0:1024][:, perm], 128, 128).reshape(8 * 128, 1024)
    W["dwk"] = pack_w(dw[:, 1024:2048], 128, 128).reshape(8 * 128, 1024)
    W["dwks"] = pack_w(dw[:, 1024:2048][:, perm], 128, 128).reshape(8 * 128, 1024)
    W["dwv"] = pack_w(dw[:, 2048:3072], 128, 512).reshape(2 * 128, 4096)
    W["dwo"] = pack_w(inp["dif_w_out"][0], 128, 128).reshape(8 * 128, 1024)
    for k_, v_ in W.items():
        sh[k_] = np.ascontiguousarray(v_, dtype=np.float32)
    sh["mod_w"] = np.ascontiguousarray(inp["mod_w"], dtype=np.float32).reshape(DEPTH * D, 9 * D)
    theta = (np.float32(10000.0) ** (-np.linspace(0.0, 1.0, 128, dtype=np.float32))).astype(np.float32)
    ang = np.arange(SEQ, dtype=np.float32)[None, :] * theta[:, None]
    sh["r_cos"] = np.cos(ang).astype(np.float32)
    sh["r_sin"] = np.sin(ang).astype(np.float32)
    nfreq = 16
    inv = (np.float32(10000.0) ** (-np.arange(nfreq, dtype=np.float32) / nfreq)).astype(np.float32)
    rows = SEQ // 64
    row = np.broadcast_to(np.arange(rows, dtype=np.float32)[:, None], (rows, 64)).reshape(-1)
    col = np.broadcast_to(np.arange(64, dtype=np.float32)[None, :], (rows, 64)).reshape(-1)
    angd = np.concatenate([row[:, None] * inv, col[:, None] * inv], axis=-1).astype(np.float32)
    cosd, sind = np.cos(angd).astype(np.float32), np.sin(angd).astype(np.float32)
    ii = (np.arange(128) % 64) % 32
    sgn = np.where((np.arange(128) % 64) < 32, -1.0, 1.0).astype(np.float32)
    sh["d_cos"] = np.ascontiguousarray(cosd[:, ii].T)
    sh["d_sin"] = np.ascontiguousarray((sind[:, ii] * sgn[None, :]).T)
    return sh, sm.off


from contextlib import ExitStack
import os

GELU_C0 = math.sqrt(2.0 / math.pi)
GELU_C1 = GELU_C0 * 0.044715
LAM_INIT2 = 0.8 - 0.6 * math.exp(-0.3 * 2)

SCRATCH = {
    "xs": ([D, TT], F32),
    "xb": ([16, 80, TT], F32), "gl": ([16, 80, TT], F32), "hf": ([16, 80, TT], F32),
    "ym": ([16, 80, TT], BF16),
    "rq": ([8, 128, TT], BF16), "rk": ([8, 128, TT], BF16),
    "rqdf": ([8, 128, TT], BF16), "rqdb": ([8, 128, TT], BF16),
    "rkdf": ([TT, 1024], BF16), "rkdb": ([TT, 1024], BF16),
    "rv": ([TT, 2048], BF16), "rsg": ([16, 128, TT], BF16),
    "ro": ([16, 128, TT], F32), "ryr": ([16, 128, TT], BF16),
    "dq": ([8, 128, TT], BF16), "dk": ([8, 128, TT], BF16), "dv": ([TT, 1024], BF16),
    "dya": ([8, 128, TT], BF16),
}


def wgroup(name):
    if name.startswith("win") or name.startswith("wout"):
        return (int(name[-2]), "f" + name[-1])
    if name.startswith("l"):
        return (int(name[-1]), "m")
    if name.startswith("r"):
        return (1, "m")
    return (2, "m")


def build_program(soff, ns, wshapes, stop_after=None, debug_out=()):
    nc = bass.Bass("TRN2", target_bir_lowering=False)
    k = KB(nc)
    dr = {}

    def din(name, shape, dt=F32):
        dr[name] = nc.dram_tensor(name, list(shape), dt, kind="ExternalInput").ap()

    din("xin", [D, TT])
    din("cond", [128, 8])
    din("smalls", [128, ns])
    din("mod_w", [DEPTH * D, 9 * D])
    for nm in ("r_cos", "r_sin", "d_cos", "d_sin"):
        din(nm, [128, SEQ])
    din("rqtab", [128, 2 * RET_H * 512])
    wbf = {}
    for nm, shp in wshapes.items():
        din(nm, shp)
        wbf[nm] = nc.dram_tensor("b_" + nm, list(shp), BF16, kind="Internal").ap()
    for nm, (shp, dt) in SCRATCH.items():
        kind = "ExternalOutput" if nm in debug_out else "Internal"
        dr[nm] = nc.dram_tensor(nm, list(shp), dt, kind=kind).ap()
    dr["out"] = nc.dram_tensor("out", [D, SEQ], F32, kind="ExternalOutput").ap()

    es = ExitStack()
    with es:
        arena = es.enter_context(nc.sbuf_tensor("arena", [128, ARENA], F32))
        k.arena_f = arena
        k.arena_b = arena.bitcast(BF16)
        psb = [es.enter_context(nc.psum_tensor(f"ps{i}", [128, 512], F32)) for i in range(8)]
        esems = {e: [es.enter_context(nc.semaphore(f"s_{e}{i}")) for i in range(NENGSEM)] for e in COMPUTE}
        dsems = [es.enter_context(nc.semaphore(f"d{i}")) for i in range(k.ndsem)]
        ps = [T(psb[i][:, :], Buf(f"ps{i}")) for i in range(8)]
        psbf = {ps[i]: psb[i].bitcast(BF16)[:, :] for i in range(8)}

        smt = k.sb("smalls", [ns])
        modt = k.sb("modt", [DEPTH * 2 * 72])
        tabA = k.sb("tabA", [DEPTH * 2 * 3 * 8])
        tabG = k.sb("tabG", [DEPTH * 2 * 3 * 8])
        ones_f = k.sb("ones_f", [128])
        ident_b = k.sb("ident_b", [128], BF16)
        zeros_b = k.sb("zeros_b", [512], BF16)
        lamt = k.sb("lamt", [8])
        gsub = k.sb("gsub", [128])
        k.persist = [smt.b, modt.b, tabA.b, tabG.b, ones_f.b, ident_b.b, zeros_b.b, lamt.b, gsub.b] + [p.b for p in ps]
        k.arena_base = k.arena_top

        def S(name, lo=0, n=None):
            off, w = soff[name]
            if n is None:
                n = w - lo
            return smt.ap[:, off + lo:off + lo + n]

        def S80(name, lo=0, n=1):
            off, w = soff[name]
            return smt.ap[0:80, off + lo:off + lo + n]

        def A_(l, kind, s, dc):
            c = ((l * 2 + kind) * 3 + s) * 8 + dc
            return tabA.ap[:, c:c + 1]

        def G_(l, kind, s, dc):
            c = ((l * 2 + kind) * 3 + s) * 8 + dc
            return tabG.ap[:, c:c + 1]

        def B_(l, kind, s, dc):
            c = (l * 2 + kind) * 72 + 3 * s * 8 + dc
            return modt.ap[:, c:c + 1]

        gbufs = {}
        order = sorted(wshapes.keys(), key=lambda n_: (wgroup(n_)[0], {"f0": 0, "m": 1, "f1": 2}[wgroup(n_)[1]]))
        for nm in order:
            if os.environ.get("SIM_ONLY", "") == "rk" and nm != "rwk":
                continue
            g = wgroup(nm)
            if g not in gbufs:
                gbufs[g] = Buf("wg%s%s" % g)
            R_, C_ = wshapes[nm]
            rp = max(1, (1 << 21) // C_)
            for r0 in range(0, R_, rp):
                r1 = min(R_, r0 + rp)
                k.bg_dma("pool", wbf[nm][r0:r1, :], dr[nm][r0:r1, :], gbufs[g])

        def GB(nm):
            return gbufs[wgroup(nm)]

        k.dma("sp", smt.ap, dr["smalls"], writes=[smt.b])
        k.memset("dve", ones_f.ap, 1.0, W=[ones_f.b])
        k.memset("dve", zeros_b.ap, 0.0, W=[zeros_b.b])
        k.copy("dve", ident_b.ap, S("ident"), R=[smt.b], W=[ident_b.b])
        cond = k.sb("cond", [16])
        sc = k.sb("sc", [16])
        k.dma("sp", cond.ap[:, 0:8], dr["cond"], writes=[cond.b])
        k.copy("dve", cond.ap[:, 8:16], S("c_ctx"), R=[smt.b, cond.b], W=[cond.b])
        k.act(sc.ap, cond.ap, AF.Silu, R=[cond.b], W=[sc.b])
        mwr = Rot([k.sb(f"mw{i}", [8, 1152]) for i in range(2)])
        for l in range(DEPTH):
            mps = ps[l % 2]
            for cg in range(8):
                wt = mwr.next()
                k.dma("sp", wt.ap, dr["mod_w"][l * D:(l + 1) * D, cg * 1152:(cg + 1) * 1152].rearrange("(kc p) n -> p kc n", p=128), writes=[wt.b])
                for cc in range(9):
                    j = cg * 9 + cc
                    for kc in range(8):
                        k.mm(mps.ap[:, 2 * j:2 * j + 2], wt.ap[:, kc, cc * 128:(cc + 1) * 128], sc.ap[:, kc:16:8],
                             start=(kc == 0), stop=(kc == 7), R=[wt.b, sc.b], W=[mps.b])
            for kind in range(2):
                base = (l * 2 + kind) * 72
                k.tt("dve", modt.ap[:, base:base + 72], mps.ap[:, kind:144:2], S("mod_b", l * 72, 72), ALU.add,
                     R=[mps.b, smt.b], W=[modt.b])
                for s in range(3):
                    col = ((l * 2 + kind) * 3 + s) * 8
                    k.stt("dve", tabA.ap[:, col:col + 8], modt.ap[:, base + (3 * s + 1) * 8:base + (3 * s + 2) * 8], 1.0,
                          S("norm_g", (l * 3 + s) * 8, 8), ALU.add, ALU.mult, R=[modt.b, smt.b], W=[tabA.b])
                    k.ts("dve", tabG.ap[:, col:col + 8], modt.ap[:, base + (3 * s + 2) * 8:base + (3 * s + 3) * 8],
                         (1.0 if s == 1 else 0.5), None, ALU.mult, R=[modt.b], W=[tabG.b])
        pr = k.sb("lamprod", [2])
        k.tt("dve", pr.ap[0:64, 0:1], S("dif_lam")[0:64, 0:1], S("dif_lam")[0:64, 1:2], ALU.mult, R=[smt.b], W=[pr.b])
        k.tt("dve", pr.ap[0:64, 1:2], S("dif_lam")[0:64, 2:3], S("dif_lam")[0:64, 3:4], ALU.mult, R=[smt.b, pr.b], W=[pr.b])
        k.mm(ps[2].ap[:, 0:2], ones_f.ap[0:64, :], pr.ap[0:64, 0:2], R=[pr.b, ones_f.b], W=[ps[2].b])
        k.act(lamt.ap[:, 0:2], ps[2].ap[:, 0:2], AF.Exp, R=[ps[2].b], W=[lamt.b])
        k.tt("dve", lamt.ap[:, 2:3], lamt.ap[:, 0:1], lamt.ap[:, 1:2], ALU.subtract, R=[lamt.b], W=[lamt.b])
        k.ts("dve", lamt.ap[:, 3:4], lamt.ap[:, 2:3], -1.0, -LAM_INIT2, ALU.mult, ALU.add, R=[lamt.b], W=[lamt.b])
        k.ts("dve", gsub.ap, S("dif_subln"), 1.0 - LAM_INIT2, None, ALU.mult, R=[smt.b], W=[gsub.b])
        k.barrier()
        nph = [0]

        def done():
            k.barrier()
            nph[0] += 1
            return stop_after is not None and nph[0] >= stop_after

        def load_norm(src, t0, n, kind, l, sl, x, h, sq, rs, tmp, pss, nrm=True):
            k.dma("sp", x.ap[:, :, 0:n], src[:, t0:t0 + n].rearrange("(c p) t -> p c t", p=128), writes=[x.b])
            if not nrm:
                return
            for sub in range(0, n, 512):
                w = min(512, n - sub)
                for c in range(8):
                    q = sq.next()
                    k.act(q.ap[:, 0:w], x.ap[:, c, sub:sub + w], AF.Square, R=[x.b], W=[q.b])
                    k.mm(pss.ap[:, 0:w], ones_f.ap, q.ap[:, 0:w], start=(c == 0), stop=(c == 7), R=[q.b, ones_f.b], W=[pss.b])
                k.rstd(rs.ap[:, sub:sub + w], rs.b, pss.ap[:, 0:w], pss.b, 1.0 / D)
                for c in range(8):
                    t = tmp.next()
                    k.stt("dve", t.ap[:, 0:w], x.ap[:, c, sub:sub + w], A_(l, kind, sl, c), rs.ap[:, sub:sub + w],
                          ALU.mult, ALU.mult, R=[x.b, rs.b, tabA.b], W=[t.b])
                    k.act(h.ap[:, c, sub:sub + w], t.ap[:, 0:w], AF.Identity, bias=B_(l, kind, sl, c), scale=1.0,
                          R=[t.b, modt.b], W=[h.b])

        NBLK = int(os.environ.get("NBLK", "0"))
        SIM_ONLY = os.environ.get("SIM_ONLY", "")

        def blocks(bs):
            r = [(0, CTX, 1)] + [(CTX + bs * i, bs, 0) for i in range(SEQ // bs)]
            return r[:NBLK] if NBLK else r

        def xsrc(l, first):
            return dr["xin"] if (l == 0 and first) else dr["xs"]

        def ffn_phase(l, s):
            sl = 0 if s == 0 else 2
            src = xsrc(l, s == 0)
            gb = GB(f"win{l}{s}")
            xt = Rot([k.sb(f"x{i}", [8, 1024]) for i in range(2)])
            hr = Rot([k.sb(f"h{i}", [8, 1024], BF16) for i in range(2)])
            u = [k.sb(f"u{f}", [1024], BF16) for f in range(NFC)]
            wi = Rot([k.sb(f"wi{i}", [8, 256], BF16) for i in range(3)])
            wo = Rot([k.sb(f"wo{i}", [NFC, 128], BF16) for i in range(2)])
            sq = Rot([k.sb(f"sq{i}", [512]) for i in range(2)])
            tmp = Rot([k.sb(f"tmp{i}", [512]) for i in range(2)])
            sa = Rot([k.sb(f"sa{i}", [512]) for i in range(2)])
            rs = k.sb("rs", [1024])
            pss = ps[0]
            pa = Rot([ps[1], ps[2]])
            pg = Rot([ps[3], ps[4]])
            py = Rot([ps[5], ps[6]])
            bl = blocks(1024)
            xs_ = [None] * len(bl)
            hs_ = [None] * len(bl)
            xs_[0], hs_[0] = xt.next(), hr.next()
            load_norm(src, bl[0][0], bl[0][1], bl[0][2], l, sl, xs_[0], hs_[0], sq, rs, tmp, pss)
            for bi, (t0, n, kind) in enumerate(bl):
                x, h = xs_[bi], hs_[bi]
                for f in range(NFC):
                    w_ = wi.next()
                    k.dma("sp", w_.ap, wbf[f"win{l}{s}"][f * 128:(f + 1) * 128, :].rearrange("p (kc c) -> p kc c", kc=8),
                          reads=[gb], writes=[w_.b])
                    for sub in range(0, n, 512):
                        wd = min(512, n - sub)
                        a, g = pa.next(), pg.next()
                        for kc in range(8):
                            k.mm(a.ap[:, 0:wd], w_.ap[:, kc, 0:128], h.ap[:, kc, sub:sub + wd], start=(kc == 0), stop=(kc == 7),
                                 R=[w_.b, h.b], W=[a.b])
                        for kc in range(8):
                            k.mm(g.ap[:, 0:wd], w_.ap[:, kc, 128:256], h.ap[:, kc, sub:sub + wd], start=(kc == 0), stop=(kc == 7),
                                 R=[w_.b, h.b], W=[g.b])
                        s_ = sa.next()
                        k.act(s_.ap[:, 0:wd], a.ap[:, 0:wd], AF.Silu, R=[a.b], W=[s_.b])
                        k.tt("dve", u[f].ap[:, sub:sub + wd], s_.ap[:, 0:wd], g.ap[:, 0:wd], ALU.mult, R=[s_.b, g.b], W=[u[f].b])
                if bi + 1 < len(bl):
                    xs_[bi + 1], hs_[bi + 1] = xt.next(), hr.next()
                    nb = bl[bi + 1]
                    load_norm(src, nb[0], nb[1], nb[2], l, sl, xs_[bi + 1], hs_[bi + 1], sq, rs, tmp, pss)
                for dc in range(8):
                    w_ = wo.next()
                    k.dma("sp", w_.ap, wbf[f"wout{l}{s}"][dc * 128:(dc + 1) * 128, :].rearrange("p (fc c) -> p fc c", fc=NFC),
                          reads=[gb], writes=[w_.b])
                    for sub in range(0, n, 512):
                        wd = min(512, n - sub)
                        y = py.next()
                        for f in range(NFC):
                            k.mm(y.ap[:, 0:wd], w_.ap[:, f, :], u[f].ap[:, sub:sub + wd], start=(f == 0), stop=(f == NFC - 1),
                                 R=[w_.b, u[f].b], W=[y.b])
                        k.stt("dve", x.ap[:, dc, sub:sub + wd], y.ap[:, 0:wd], G_(l, kind, sl, dc), x.ap[:, dc, sub:sub + wd],
                              ALU.mult, ALU.add, R=[y.b, x.b, tabG.b], W=[x.b])
                k.dma("pool", dr["xs"][:, t0:t0 + n].rearrange("(c p) t -> p c t", p=128), x.ap[:, :, 0:n], reads=[x.b])

        def outproj_phase(l, ysrc, wname, kp, nkc):
            gb = GB(wname)
            wt = k.sb("wo", [8, nkc, 128], BF16, parts=kp)
            k.dma("sp", wt.ap, wbf[wname].rearrange("(dc p) (c m) -> p dc c m", p=kp, c=nkc), reads=[gb], writes=[wt.b])
            xt = Rot([k.sb(f"x{i}", [8, 512]) for i in range(2)])
            yt = Rot([k.sb(f"y{i}", [nkc, 512], BF16, parts=kp) for i in range(2)])
            py = Rot([ps[0], ps[1], ps[2]])
            for (t0, n, kind) in blocks(512):
                x, y = xt.next(), yt.next()
                k.dma("sp", x.ap[:, :, 0:n], dr["xs"][:, t0:t0 + n].rearrange("(c p) t -> p c t", p=128), writes=[x.b])
                k.dma("sp", y.ap[:, :, 0:n], ysrc[:, :, t0:t0 + n].rearrange("c p t -> p c t"), writes=[y.b])
                for dc in range(8):
                    p_ = py.next()
                    for c in range(nkc):
                        k.mm(p_.ap[:, 0:n], wt.ap[:, dc, c, :], y.ap[:, c, 0:n], start=(c == 0), stop=(c == nkc - 1),
                             R=[wt.b, y.b], W=[p_.b])
                    k.stt("dve", x.ap[:, dc, 0:n], p_.ap[:, 0:n], G_(l, kind, 1, dc), x.ap[:, dc, 0:n], ALU.mult, ALU.add,
                          R=[p_.b, x.b, tabG.b], W=[x.b])
                k.dma("pool", dr["xs"][:, t0:t0 + n].rearrange("(c p) t -> p c t", p=128), x.ap[:, :, 0:n], reads=[x.b])

        PH = {}

        def lru_phase(l):
            n_ = 0 if l == 0 else 1
            gb = GB(f"lwx{l}")
            wx = k.sb("wx", [16, 8, 80], BF16)
            wg = k.sb("wg", [16, 8, 80], BF16)
            k.dma("sp", wx.ap, wbf[f"lwx{l}"].rearrange("(g p) (kc c) -> p g kc c", p=128, kc=8), reads=[gb], writes=[wx.b])
            k.dma("sp", wg.ap, wbf[f"lwg{l}"].rearrange("(g p) (kc c) -> p g kc c", p=128, kc=8), reads=[gb], writes=[wg.b])
            xt = Rot([k.sb(f"x{i}", [8, 512]) for i in range(2)])
            hr = Rot([k.sb(f"h{i}", [8, 512], BF16) for i in range(2)])
            sq = Rot([k.sb(f"sq{i}", [512]) for i in range(2)])
            tmp = Rot([k.sb(f"tmp{i}", [512]) for i in range(2)])
            rs = k.sb("rs", [512])
            xbr = Rot([k.sb(f"xbt{i}", [8, 512], parts=80) for i in range(2)])
            glr = Rot([k.sb(f"glt{i}", [8, 512], parts=80) for i in range(2)])
            pp = Rot([ps[1], ps[2], ps[3], ps[4]])
            for (t0, n, kind) in blocks(512):
                x, h = xt.next(), hr.next()
                load_norm(dr["xs"], t0, n, kind, l, 1, x, h, sq, rs, tmp, ps[0])
                for half in range(2):
                    xbt, glt = xbr.next(), glr.next()
                    for c8 in range(8):
                        c = half * 8 + c8
                        p1 = pp.next()
                        for kc in range(8):
                            k.mm(p1.ap[0:80, 0:n], wx.ap[:, c, kc, :], h.ap[:, kc, 0:n], start=(kc == 0), stop=(kc == 7),
                                 R=[wx.b, h.b], W=[p1.b])
                        k.copy("dve", xbt.ap[:, c8, 0:n], p1.ap[0:80, 0:n], R=[p1.b], W=[xbt.b])
                        p2 = pp.next()
                        for kc in range(8):
                            k.mm(p2.ap[0:80, 0:n], wg.ap[:, c, kc, :], h.ap[:, kc, 0:n], start=(kc == 0), stop=(kc == 7),
                                 R=[wg.b, h.b], W=[p2.b])
                        k.act(glt.ap[:, c8, 0:n], p2.ap[0:80, 0:n], AF.Gelu_apprx_tanh, R=[p2.b], W=[glt.b])
                    k.dma("pool", dr["xb"][half * 8:half * 8 + 8, :, t0:t0 + n].rearrange("c p t -> p c t"), xbt.ap[:, :, 0:n], reads=[xbt.b])
                    k.dma("pool", dr["gl"][half * 8:half * 8 + 8, :, t0:t0 + n].rearrange("c p t -> p c t"), glt.ap[:, :, 0:n], reads=[glt.b])
            k.barrier()
            gwt = k.sb("gwt", [128, 80], BF16, parts=80)
            k.dma("sp", gwt.ap, wbf[f"lgw{l}"].rearrange("p (b c) -> p b c", c=80), reads=[gb], writes=[gwt.b])
            et = k.sb("et", [32], parts=80)
            cdt = k.sb("cdt", [32], parts=80)
            hcdt = k.sb("hcdt", [32], parts=80)
            hbt = k.sb("hbt", [64], parts=80)
            k.act(et.ap, S80("lam", n_ * 32, 32), AF.Exp, scale=-1.0, R=[smt.b], W=[et.b])
            k.act(et.ap, et.ap, AF.Ln, bias=1.0, scale=1.0, R=[et.b], W=[et.b])
            k.ts("dve", cdt.ap, et.ap, -8.0, None, ALU.mult, R=[et.b], W=[cdt.b])
            k.ts("dve", hcdt.ap, et.ap, -4.0, None, ALU.mult, R=[et.b], W=[hcdt.b])
            k.ts("dve", hbt.ap, S80("gate_b", n_ * 64, 64), 0.5, None, ALU.mult, R=[smt.b], W=[hbt.b])
            xbg = k.sb("xbg", [2, TT], parts=80)
            xcr = Rot([k.sb(f"xc{i}", [2, 512], parts=80) for i in range(2)])
            xcbr = Rot([k.sb(f"xcb{i}", [2, 512], BF16, parts=80) for i in range(2)])
            trr = Rot([k.sb(f"tr{i}", [2, 512], parts=80) for i in range(2)])
            tir = Rot([k.sb(f"ti{i}", [2, 512], parts=80) for i in range(2)])
            atr = Rot([k.sb(f"at{i}", [2, 512], parts=80) for i in range(2)])
            a2r = Rot([k.sb(f"a2{i}", [2, 512], parts=80) for i in range(2)])
            btr = Rot([k.sb(f"bt{i}", [2, 512], parts=80) for i in range(2)])
            hbr = Rot([k.sb(f"hb{i}", [2, 512], parts=80) for i in range(2)])
            hfr = Rot([k.sb(f"hf{i}", [2, 512], parts=80) for i in range(2)])
            glr2 = Rot([k.sb(f"gl{i}", [2, 512], parts=80) for i in range(2)])
            ymr = Rot([k.sb(f"ym{i}", [2, 512], BF16, parts=80) for i in range(2)])
            ctr = Rot([k.sb(f"ct{i}", [512], parts=80) for i in range(2)])
            pp = Rot([ps[i] for i in range(8)])
            bl = blocks(512)
            for g in range(8):
                k.dma("sp", xbg.ap, dr["xb"][2 * g:2 * g + 2].rearrange("c p t -> p c t"), writes=[xbg.b])
                for d in range(2):
                    order = bl if d == 0 else [bl[0]] + bl[:0:-1]
                    prev = None
                    for (t0, n, kind) in order:
                        lo_reg, hi_reg = (0, CTX) if kind == 1 else (CTX, TT)
                        xc, xcb = xcr.next(), xcbr.next()
                        for jj in range(2):
                            c = 2 * g + jj
                            k.act(xc.ap[:, jj, 0:n], xbg.ap[:, jj, t0:t0 + n], AF.Identity, scale=S80("conv_w", (n_ * 4 + 1) * 16 + c),
                                  bias=S80("conv_b", n_ * 16 + c), R=[xbg.b, smt.b], W=[xc.b])
                            for j in (0, 2, 3):
                                o = j - 1
                                lo = max(t0, lo_reg - o)
                                hi = min(t0 + n, hi_reg - o)
                                k.stt("dve", xc.ap[:, jj, lo - t0:hi - t0], xbg.ap[:, jj, lo + o:hi + o], S80("conv_w", (n_ * 4 + j) * 16 + c),
                                      xc.ap[:, jj, lo - t0:hi - t0], ALU.mult, ALU.add, R=[xbg.b, xc.b, smt.b], W=[xc.b])
                        k.copy("act", xcb.ap[:, :, 0:n], xc.ap[:, :, 0:n], R=[xc.b], W=[xcb.b])
                        tr, ti, at, a2, bt, hcur = trr.next(), tir.next(), atr.next(), a2r.next(), btr.next(), hbr.next()
                        for jj in range(2):
                            c = 2 * g + jj
                            pr_, pi_ = pp.next(), pp.next()
                            for kk, pt in ((0, pr_), (1, pi_)):
                                for ii in range(2):
                                    blk = (((d * 2 + kk) * 8 + g) * 2 + ii) * 2 + jj
                                    k.mm(pt.ap[0:80, 0:n], gwt.ap[:, blk, :], xcb.ap[:, ii, 0:n], start=(ii == 0), stop=(ii == 1),
                                         R=[gwt.b, xcb.b], W=[pt.b])
                            k.act(tr.ap[:, jj, 0:n], pr_.ap[0:80, 0:n], AF.Tanh, scale=0.5, bias=hbt.ap[:, (d * 2 + 0) * 16 + c:(d * 2 + 0) * 16 + c + 1],
                                  R=[pr_.b, hbt.b], W=[tr.b])
                            k.act(ti.ap[:, jj, 0:n], pi_.ap[0:80, 0:n], AF.Tanh, scale=0.5, bias=hbt.ap[:, (d * 2 + 1) * 16 + c:(d * 2 + 1) * 16 + c + 1],
                                  R=[pi_.b, hbt.b], W=[ti.b])
                            k.act(at.ap[:, jj, 0:n], tr.ap[:, jj, 0:n], AF.Exp, scale=hcdt.ap[:, d * 16 + c:d * 16 + c + 1],
                                  bias=hcdt.ap[:, d * 16 + c:d * 16 + c + 1], R=[tr.b, hcdt.b], W=[at.b])
                            k.act(a2.ap[:, jj, 0:n], tr.ap[:, jj, 0:n], AF.Exp, scale=cdt.ap[:, d * 16 + c:d * 16 + c + 1],
                                  bias=cdt.ap[:, d * 16 + c:d * 16 + c + 1], R=[tr.b, cdt.b], W=[a2.b])
                        k.act(a2.ap[:, :, 0:n], a2.ap[:, :, 0:n], AF.Sqrt, scale=-1.0, bias=1.0, R=[a2.b], W=[a2.b])
                        k.stt("dve", ti.ap[:, :, 0:n], ti.ap[:, :, 0:n], 1.0, xc.ap[:, :, 0:n], ALU.add, ALU.mult, R=[ti.b, xc.b], W=[ti.b])
                        k.stt("dve", bt.ap[:, :, 0:n], ti.ap[:, :, 0:n], 0.5, a2.ap[:, :, 0:n], ALU.mult, ALU.mult, R=[ti.b, a2.b], W=[bt.b])
                        for jj in range(2):
                            if prev is None:
                                init = 0.0
                                rr = []
                            else:
                                ph, pn = prev
                                init = ph.ap[:, jj, pn - 1:pn] if d == 0 else ph.ap[:, jj, 0:1]
                                rr = [ph.b]
                            if d == 0:
                                k.scan(hcur.ap[:, jj, 0:n], at.ap[:, jj, 0:n], bt.ap[:, jj, 0:n], init, R=[at.b, bt.b] + rr, W=[hcur.b])
                            else:
                                k.scan(hcur.ap[:, jj, 0:n][:, ::-1], at.ap[:, jj, 0:n][:, ::-1], bt.ap[:, jj, 0:n][:, ::-1], init,
                                       R=[at.b, bt.b] + rr, W=[hcur.b])
                        prev = (hcur, n)
                        dsl = lambda nm: dr[nm][2 * g:2 * g + 2, :, t0:t0 + n].rearrange("c p t -> p c t")
                        if d == 0:
                            k.dma("pool", dsl("hf"), hcur.ap[:, :, 0:n], reads=[hcur.b])
                        else:
                            hf, gl, ymt = hfr.next(), glr2.next(), ymr.next()
                            k.dma("sp", hf.ap[:, :, 0:n], dsl("hf"), writes=[hf.b])
                            k.dma("sp", gl.ap[:, :, 0:n], dsl("gl"), writes=[gl.b])
                            k.tt("dve", hf.ap[:, :, 0:n], hf.ap[:, :, 0:n], hcur.ap[:, :, 0:n], ALU.add, R=[hf.b, hcur.b], W=[hf.b])
                            k.tt("dve", ymt.ap[:, :, 0:n], hf.ap[:, :, 0:n], gl.ap[:, :, 0:n], ALU.mult, R=[hf.b, gl.b], W=[ymt.b])
                            k.dma("pool", dsl("ym"), ymt.ap[:, :, 0:n], reads=[ymt.b])
                    k.barrier(soft=True)
            k.barrier()
            outproj_phase(l, dr["ym"], f"lwo{l}", 80, 16)
            return done()

        PH["lru"] = lru_phase

        def ret_phase(l):
            rstop = int(os.environ.get("RET_STOP", "99"))
            gb = GB("rwq")
            bl = blocks(512)

            def std_tiles():
                xt = Rot([k.sb(f"x{i}", [8, 512]) for i in range(2)])
                hr = Rot([k.sb(f"h{i}", [8, 512], BF16) for i in range(2)])
                sq = Rot([k.sb(f"sq{i}", [512]) for i in range(2)])
                tmp = Rot([k.sb(f"tmp{i}", [512]) for i in range(2)])
                rs = k.sb("rs", [512])
                return xt, hr, sq, tmp, rs

            def wload(name, ng, gw):
                wt = k.sb("w_" + name, [ng, 8, gw], BF16)
                k.dma("sp", wt.ap, wbf[name].rearrange("(g p) (kc c) -> p g kc c", p=128, kc=8), reads=[gb], writes=[wt.b])
                return wt

            def rope(src, dst, n, cos, sin, mr_d, mr_p):
                for hh in range(4):
                    x1, x2 = src.ap[:, 2 * hh, 0:n], src.ap[:, 2 * hh + 1, 0:n]
                    m1, m4 = mr_d.next(), mr_d.next()
                    m2, m3 = mr_p.next(), mr_p.next()
                    k.tt("dve", m1.ap[:, 0:n], x1, cos.ap[:, 0:n], ALU.mult, R=[src.b, cos.b], W=[m1.b])
                    k.tt("dve", m2.ap[:, 0:n], x2, sin.ap[:, 0:n], ALU.mult, R=[src.b, sin.b], W=[m2.b])
                    k.tt("dve", dst.ap[:, 2 * hh, 0:n], m1.ap[:, 0:n], m2.ap[:, 0:n], ALU.subtract, R=[m1.b, m2.b], W=[dst.b])
                    k.tt("dve", m3.ap[:, 0:n], x1, sin.ap[:, 0:n], ALU.mult, R=[src.b, sin.b], W=[m3.b])
                    k.tt("dve", m4.ap[:, 0:n], x2, cos.ap[:, 0:n], ALU.mult, R=[src.b, cos.b], W=[m4.b])
                    k.tt("dve", dst.ap[:, 2 * hh + 1, 0:n], m3.ap[:, 0:n], m4.ap[:, 0:n], ALU.add, R=[m3.b, m4.b], W=[dst.b])

            fm = lambda nm, t0, n: dr[nm][:, :, t0:t0 + n].rearrange("c p t -> p c t")
            tm = lambda nm, t0, n: dr[nm][t0:t0 + n, :].rearrange("(n p) d -> p n d", p=128)

            for which in ("q", "k"):
                if SIM_ONLY == "rk" and which == "q":
                    continue
                wt = wload("rw" + which, 8, 128)
                xt, hr, sq, tmp, rs = std_tiles()
                csr = Rot([k.sb(f"cos{i}", [512]) for i in range(2)])
                snr = Rot([k.sb(f"sin{i}", [512]) for i in range(2)])
                qf = k.sb("qf", [8, 512])
                mr_d = Rot([k.sb(f"md{i}", [512]) for i in range(4)])
                mr_p = Rot([k.sb(f"mp{i}", [512]) for i in range(4)])
                qbr = Rot([k.sb(f"qb{i}", [8, 512], BF16) for i in range(2)])
                if which == "q":
                    qtab = k.sb("qtab", [2, RET_H * 512])
                    k.dma("sp", qtab.ap, dr["rqtab"].rearrange("p (a b) -> p a b", a=2), writes=[qtab.b])
                    qdfr = Rot([k.sb(f"qdf{i}", [8, 512], BF16) for i in range(2)])
                    qdbr = Rot([k.sb(f"qdb{i}", [8, 512], BF16) for i in range(2)])
                else:
                    kdfr = Rot([k.sb(f"kdf{i}", [4, 1024], BF16) for i in range(2)])
                    kdbr = Rot([k.sb(f"kdb{i}", [4, 1024], BF16) for i in range(2)])
                pp = Rot([ps[i] for i in range(1, 8)])
                for (t0, n, kind) in bl:
                    x, h = xt.next(), hr.next()
                    load_norm(dr["xs"], t0, n, kind, l, 1, x, h, sq, rs, tmp, ps[0])
                    for oc in range(8):
                        p = pp.next()
                        for kc in range(8):
                            k.mm(p.ap[:, 0:n], wt.ap[:, oc, kc, :], h.ap[:, kc, 0:n], start=(kc == 0), stop=(kc == 7), R=[wt.b, h.b], W=[p.b])
                        k.act(qf.ap[:, oc, 0:n], p.ap[:, 0:n], AF.Identity, scale=(1.0 if which == "q" else 0.0625), R=[p.b], W=[qf.b])
                    qb = qbr.next()
                    if kind == 0:
                        cos, sin = csr.next(), snr.next()
                        k.dma("sp", cos.ap[:, 0:n], dr["r_cos"][:, t0 - CTX:t0 - CTX + n], writes=[cos.b])
                        k.dma("sp", sin.ap[:, 0:n], dr["r_sin"][:, t0 - CTX:t0 - CTX + n], writes=[sin.b])
                        rope(qf, qb, n, cos, sin, mr_d, mr_p)
                    else:
                        k.copy("dve", qb.ap[:, 0:4, 0:n], qf.ap[:, 0:4, 0:n], R=[qf.b], W=[qb.b])
                        k.copy("act", qb.ap[:, 4:8, 0:n], qf.ap[:, 4:8, 0:n], R=[qf.b, qb.b], W=[qb.b])
                    k.dma("pool", fm("r" + which, t0, n), qb.ap[:, :, 0:n], reads=[qb.b])
                    if which == "q":
                        qdf, qdb = qdfr.next(), qdbr.next()
                        for hh in range(4):
                            for c in (2 * hh, 2 * hh + 1):
                                k.tt("dve", qdf.ap[:, c, 0:n], qb.ap[:, c, 0:n], qtab.ap[:, 0, hh * 512:hh * 512 + n], ALU.mult, R=[qb.b, qtab.b], W=[qdf.b])
                                k.tt("dve", qdb.ap[:, c, 0:n], qb.ap[:, c, 0:n], qtab.ap[:, 1, hh * 512:hh * 512 + n], ALU.mult, R=[qb.b, qtab.b], W=[qdb.b])
                        k.dma("pool", fm("rqdf", t0, n), qdf.ap[:, :, 0:n], reads=[qdf.b])
                        k.dma("pool", fm("rqdb", t0, n), qdb.ap[:, :, 0:n], reads=[qdb.b])
                    else:
                        kdf, kdb = kdfr.next(), kdbr.next()
                        rks = int(os.environ.get("RK_SKIP", "0"))
                        for ts_ in range(0 if rks == 1 else n // 128):
                            for cg in range(2):
                                p = pp.next()
                                pb = psbf[p]
                                for cc in range(4):
                                    k.tr(pb[:, cc * 128:(cc + 1) * 128], qb.ap[:, cg * 4 + cc, ts_ * 128:(ts_ + 1) * 128], ident_b.ap,
                                         R=[qb.b, ident_b.b], W=[p.b])
                                for h2 in range(2):
                                    hh = cg * 2 + h2
                                    k.act(kdf.ap[:, ts_, hh * 256:(hh + 1) * 256], pb[:, h2 * 256:(h2 + 1) * 256], AF.Identity,
                                          scale=S("r_kdf", hh, 1), R=[p.b, smt.b], W=[kdf.b])
                                for h2 in range(2):
                                    hh = cg * 2 + h2
                                    k.ts("dve", kdb.ap[:, ts_, hh * 256:(hh + 1) * 256], pb[:, h2 * 256:(h2 + 1) * 256], S("r_kdb", hh, 1), None,
                                         ALU.mult, R=[p.b, smt.b, kdf.b], W=[kdb.b])
                        if rks == 0:
                            k.dma("pool", tm("rkdf", t0, n), kdf.ap[:, 0:n // 128, :], reads=[kdf.b])
                            k.dma("pool", tm("rkdb", t0, n), kdb.ap[:, 0:n // 128, :], reads=[kdb.b])
                k.barrier()
                if rstop <= (0 if which == "q" else 1):
                    return True
            wv = wload("rwv", 4, 512)
            wg = wload("rwg", 16, 128)
            xt, hr, sq, tmp, rs = std_tiles()
            vt = k.sb("vt", [4, 2048], BF16)
            sgt = k.sb("sgt", [16, 512], BF16)
            pp = Rot([ps[i] for i in range(1, 8)])
            ev = 0
            for (t0, n, kind) in bl:
                x, h = xt.next(), hr.next()
                load_norm(dr["xs"], t0, n, kind, l, 1, x, h, sq, rs, tmp, ps[0])
                for ts_ in range(n // 128):
                    for vg in range(4):
                        p = pp.next()
                        for kc in range(8):
                            k.mm(p.ap, h.ap[:, kc, ts_ * 128:(ts_ + 1) * 128], wv.ap[:, vg, kc, :], start=(kc == 0), stop=(kc == 7),
                                 R=[h.b, wv.b], W=[p.b])
                        k.copy("act" if ev % 2 == 0 else "dve", vt.ap[:, ts_, vg * 512:(vg + 1) * 512], p.ap, R=[p.b], W=[vt.b])
                        ev += 1
                for oc in range(16):
                    p = pp.next()
                    for kc in range(8):
                        k.mm(p.ap[:, 0:n], wg.ap[:, oc, kc, :], h.ap[:, kc, 0:n], start=(kc == 0), stop=(kc == 7), R=[wg.b, h.b], W=[p.b])
                    k.act(sgt.ap[:, oc, 0:n], p.ap[:, 0:n], AF.Silu, R=[p.b], W=[sgt.b])
                k.dma("pool", tm("rv", t0, n), vt.ap[:, 0:n // 128, :], reads=[vt.b])
                k.dma("pool", fm("rsg", t0, n), sgt.ap[:, :, 0:n], reads=[sgt.b])
            k.barrier()
            if rstop <= 2:
                return True
            for sweep in range(2):
                St = k.sb("S", [2, 512])
                Sb = k.sb("Sb", [2, 512], BF16)
                qtr = Rot([k.sb(f"qt{i}", [2, 512], BF16) for i in range(2)])
                ktr = Rot([k.sb(f"kt{i}", [2, 512], BF16) for i in range(2)])
                qdr = Rot([k.sb(f"qd{i}", [2, 512], BF16) for i in range(2)])
                vtr = Rot([k.sb(f"vt{i}", [4, 512], BF16) for i in range(2)])
                kdr = Rot([k.sb(f"kd{i}", [4, 256], BF16) for i in range(2)])
                ptr_ = Rot([k.sb(f"pt{i}", [128], BF16) for i in range(2)])
                otr = Rot([k.sb(f"ot{i}", [4, 512]) for i in range(2)])
                o1r = Rot([k.sb(f"o1{i}", [4, 512]) for i in range(2)])
                sgr = Rot([k.sb(f"sg{i}", [4, 512], BF16) for i in range(2)])
                yrr = Rot([k.sb(f"yr{i}", [4, 512], BF16) for i in range(2)])
                sq = Rot([k.sb(f"sq{i}", [512]) for i in range(2)])
                rs = k.sb("rs", [512])
                o_ps = [ps[0], ps[1], ps[2], ps[3]]
                KV = [ps[5], ps[6]]
                sTr = Rot([ps[4], ps[7]])
                order = bl if sweep == 0 else [bl[0]] + bl[:0:-1]
                for hh in range(4):
                    k.memset("dve", St.ap, 0.0, W=[St.b])
                    k.memset("dve", Sb.ap, 0.0, W=[Sb.b])
                    hsl = lambda nm, t0, n: dr[nm][2 * hh:2 * hh + 2, :, t0:t0 + n].rearrange("c p t -> p c t")
                    osl = lambda nm, t0, n: dr[nm][4 * hh:4 * hh + 4, :, t0:t0 + n].rearrange("c p t -> p c t")
                    for (t0, n, kind) in order:
                        nch = n // 128
                        qd, vt_, kd = qdr.next(), vtr.next(), kdr.next()
                        k.dma("sp", qd.ap[:, :, 0:n], hsl("rqdf" if sweep == 0 else "rqdb", t0, n), writes=[qd.b])
                        k.dma("sp", vt_.ap[:, 0:nch, :], dr["rv"][t0:t0 + n, hh * 512:(hh + 1) * 512].rearrange("(n p) d -> p n d", p=128), writes=[vt_.b])
                        k.dma("sp", kd.ap[:, 0:nch, :], dr["rkdf" if sweep == 0 else "rkdb"][t0:t0 + n, hh * 256:(hh + 1) * 256].rearrange("(n p) d -> p n d", p=128),
                              writes=[kd.b])
                        if sweep == 0:
                            qt, kt = qtr.next(), ktr.next()
                            k.dma("sp", qt.ap[:, :, 0:n], hsl("rq", t0, n), writes=[qt.b])
                            k.dma("sp", kt.ap[:, :, 0:n], hsl("rk", t0, n), writes=[kt.b])
                        else:
                            o1, sg = o1r.next(), sgr.next()
                            k.dma("sp", o1.ap[:, :, 0:n], osl("ro", t0, n), writes=[o1.b])
                            k.dma("sp", sg.ap[:, :, 0:n], osl("rsg", t0, n), writes=[sg.b])
                        chs = range(nch) if sweep == 0 else range(nch - 1, -1, -1)
                        for ch in chs:
                            cs_ = slice(ch * 128, (ch + 1) * 128)
                            if sweep == 0:
                                sT = sTr.next()
                                for kc in range(2):
                                    k.mm(sT.ap[:, 0:128], kt.ap[:, kc, cs_], qt.ap[:, kc, cs_], start=(kc == 0), stop=(kc == 1), R=[kt.b, qt.b], W=[sT.b])
                                PT = ptr_.next()
                                k.tt("dve", PT.ap, sT.ap[:, 0:128], S("r_mask", hh * 128, 128), ALU.mult, R=[sT.b, smt.b], W=[PT.b])
                            for dvc in range(4):
                                if sweep == 0:
                                    k.mm(o_ps[dvc].ap[:, cs_], vt_.ap[:, ch, dvc * 128:(dvc + 1) * 128], PT.ap, start=True, stop=False,
                                         R=[vt_.b, PT.b], W=[o_ps[dvc].b])
                                for kc in range(2):
                                    k.mm(o_ps[dvc].ap[:, cs_], Sb.ap[:, kc, dvc * 128:(dvc + 1) * 128], qd.ap[:, kc, cs_],
                                         start=(sweep == 1 and kc == 0), stop=(kc == 1), R=[Sb.b, qd.b], W=[o_ps[dvc].b])
                            for kc in range(2):
                                k.mm(KV[kc].ap, kd.ap[:, ch, kc * 128:(kc + 1) * 128], vt_.ap[:, ch, :], R=[kd.b, vt_.b], W=[KV[kc].b])
                                k.stt("dve", St.ap[:, kc, :], St.ap[:, kc, :], S("r_cd", hh, 1), KV[kc].ap, ALU.mult, ALU.add,
                                      R=[St.b, KV[kc].b, smt.b], W=[St.b])
                                k.copy("act", Sb.ap[:, kc, :], St.ap[:, kc, :], R=[St.b], W=[Sb.b])
                        ot = otr.next()
                        if sweep == 0:
                            for dvc in range(4):
                                k.copy("act", ot.ap[:, dvc, 0:n], o_ps[dvc].ap[:, 0:n], R=[o_ps[dvc].b], W=[ot.b])
                            k.dma("pool", osl("ro", t0, n), ot.ap[:, :, 0:n], reads=[ot.b])
                        else:
                            yr = yrr.next()
                            for dvc in range(4):
                                k.tt("dve", ot.ap[:, dvc, 0:n], o1.ap[:, dvc, 0:n], o_ps[dvc].ap[:, 0:n], ALU.add, R=[o1.b, o_ps[dvc].b], W=[ot.b])
                                q_ = sq.next()
                                k.act(q_.ap[:, 0:n], ot.ap[:, dvc, 0:n], AF.Square, R=[ot.b], W=[q_.b])
                                k.mm(ps[4].ap[:, 0:n], ones_f.ap, q_.ap[:, 0:n], start=(dvc == 0), stop=(dvc == 3), R=[q_.b, ones_f.b], W=[ps[4].b])
                            k.rstd(rs.ap[:, 0:n], rs.b, ps[4].ap[:, 0:n], ps[4].b, 1.0 / 512)
                            for dvc in range(4):
                                k.tt("dve", ot.ap[:, dvc, 0:n], ot.ap[:, dvc, 0:n], rs.ap[:, 0:n], ALU.mult, R=[ot.b, rs.b], W=[ot.b])
                                k.tt("dve", yr.ap[:, dvc, 0:n], ot.ap[:, dvc, 0:n], sg.ap[:, dvc, 0:n], ALU.mult, R=[ot.b, sg.b], W=[yr.b])
                            k.dma("pool", osl("ryr", t0, n), yr.ap[:, :, 0:n], reads=[yr.b])
                k.barrier()
                if rstop <= 3 + sweep:
                    return True
            outproj_phase(l, dr["ryr"], "rwo", 128, 16)
            return done()

        PH["ret"] = ret_phase

        def dif_phase(l):
            dstop = int(os.environ.get("DIF_STOP", "99"))
            gb = GB("dwq")
            bl = blocks(512)
            fm = lambda nm, t0, n: dr[nm][:, :, t0:t0 + n].rearrange("c p t -> p c t")

            def wload(name, ng, gw):
                wt = k.sb("w_" + name, [ng, 8, gw], BF16)
                k.dma("sp", wt.ap, wbf[name].rearrange("(g p) (kc c) -> p g kc c", p=128, kc=8), reads=[gb], writes=[wt.b])
                return wt

            for which in ("q", "k"):
                w1 = wload("dw" + which, 8, 128)
                w2 = wload("dw" + which + "s", 8, 128)
                if which == "q":
                    wv = wload("dwv", 2, 512)
                    vt = k.sb("vt", [4, 1024], BF16)
                xt = Rot([k.sb(f"x{i}", [8, 512]) for i in range(2)])
                hr = Rot([k.sb(f"h{i}", [8, 512], BF16) for i in range(2)])
                sq = Rot([k.sb(f"sq{i}", [512]) for i in range(2)])
                tmp = Rot([k.sb(f"tmp{i}", [512]) for i in range(2)])
                rs = k.sb("rs", [512])
                csr = Rot([k.sb(f"cos{i}", [512]) for i in range(2)])
                snr = Rot([k.sb(f"sin{i}", [512]) for i in range(2)])
                t1r = Rot([k.sb(f"t1{i}", [512]) for i in range(3)])
                t2r = Rot([k.sb(f"t2{i}", [512]) for i in range(3)])
                qbr = Rot([k.sb(f"qb{i}", [8, 512], BF16) for i in range(2)])
                pp = Rot([ps[i] for i in range(1, 8)])
                ev = 0
                for (t0, n, kind) in bl:
                    x, h = xt.next(), hr.next()
                    load_norm(dr["xs"], t0, n, kind, l, 1, x, h, sq, rs, tmp, ps[0])
                    qb = qbr.next()
                    if kind == 0:
                        cos, sin = csr.next(), snr.next()
                        k.dma("sp", cos.ap[:, 0:n], dr["d_cos"][:, t0 - CTX:t0 - CTX + n], writes=[cos.b])
                        k.dma("sp", sin.ap[:, 0:n], dr["d_sin"][:, t0 - CTX:t0 - CTX + n], writes=[sin.b])
                    for oc in range(8):
                        p1 = pp.next()
                        for kc in range(8):
                            k.mm(p1.ap[:, 0:n], w1.ap[:, oc, kc, :], h.ap[:, kc, 0:n], start=(kc == 0), stop=(kc == 7), R=[w1.b, h.b], W=[p1.b])
                        if kind == 1:
                            k.copy("act", qb.ap[:, oc, 0:n], p1.ap[:, 0:n], R=[p1.b], W=[qb.b])
                            continue
                        p2 = pp.next()
                        for kc in range(8):
                            k.mm(p2.ap[:, 0:n], w2.ap[:, oc, kc, :], h.ap[:, kc, 0:n], start=(kc == 0), stop=(kc == 7), R=[w2.b, h.b], W=[p2.b])
                        t1, t2 = t1r.next(), t2r.next()
                        k.tt("dve", t1.ap[:, 0:n], p1.ap[:, 0:n], cos.ap[:, 0:n], ALU.mult, R=[p1.b, cos.b], W=[t1.b])
                        k.tt("dve", t2.ap[:, 0:n], p2.ap[:, 0:n], sin.ap[:, 0:n], ALU.mult, R=[p2.b, sin.b], W=[t2.b])
                        k.tt("dve", qb.ap[:, oc, 0:n], t1.ap[:, 0:n], t2.ap[:, 0:n], ALU.add, R=[t1.b, t2.b], W=[qb.b])
                    k.dma("pool", fm("d" + which, t0, n), qb.ap[:, :, 0:n], reads=[qb.b])
                    if which == "q":
                        for ts_ in range(n // 128):
                            for vg in range(2):
                                p = pp.next()
                                for kc in range(8):
                                    k.mm(p.ap, h.ap[:, kc, ts_ * 128:(ts_ + 1) * 128], wv.ap[:, vg, kc, :], start=(kc == 0), stop=(kc == 7),
                                         R=[h.b, wv.b], W=[p.b])
                                k.copy("act" if ev % 2 == 0 else "dve", vt.ap[:, ts_, vg * 512:(vg + 1) * 512], p.ap, R=[p.b], W=[vt.b])
                                ev += 1
                        k.dma("pool", dr["dv"][t0:t0 + n, :].rearrange("(n p) d -> p n d", p=128), vt.ap[:, 0:n // 128, :], reads=[vt.b])
                k.barrier()
            if dstop <= 0:
                return True
            NKT = TT // 128
            Khr = Rot([k.sb(f"Kh{i}", [TT], BF16) for i in range(2)])
            var = [k.sb(f"vaug{i}", [NKT, 129], BF16) for i in range(2)]
            for v_ in var:
                k.memset("dve", v_.ap, 1.0, W=[v_.b])
            qtr = Rot([k.sb(f"qt{i}", [512], BF16) for i in range(2)])
            p1r = Rot([k.sb(f"P1{i}", [512], BF16) for i in range(3)])
            p2r = Rot([k.sb(f"P2{i}", [512], BF16) for i in range(3)])
            rlr = Rot([k.sb(f"rl{i}", [8]) for i in range(4)])
            odr = Rot([k.sb(f"od{i}", [128]) for i in range(2)])
            jkr = Rot([k.sb(f"jk{i}", [128]) for i in range(2)])
            ytr = Rot([k.sb(f"yt{i}", [128], BF16) for i in range(2)])
            yfr = Rot([k.sb(f"yf{i}", [512], BF16) for i in range(2)])
            S1r = Rot([ps[0], ps[1]])
            S2r = Rot([ps[2], ps[3]])
            OB = [ps[4], ps[5], ps[6]]
            tp = ps[7]
            tpb = psbf[tp]
            for hh in range(8):
                Kh, vaug = Khr.next(), var[hh % 2]
                k.dma("sp", Kh.ap, dr["dk"][hh], writes=[Kh.b])
                k.dma("sp", vaug.ap[:, :, 0:128], dr["dv"][:, hh * 128:(hh + 1) * 128].rearrange("(n p) d -> p n d", p=128), writes=[vaug.b])
                for (t0, n, kind) in bl:
                    qt = qtr.next()
                    k.dma("sp", qt.ap[:, 0:n], dr["dq"][hh, :, t0:t0 + n], writes=[qt.b])
                    nkt = (CTX // 128) if kind == 1 else NKT
                    nqs = n // 128
                    for bank in OB:
                        k.mm(bank.ap, zeros_b.ap[:, 0:128], zeros_b.ap, start=True, stop=False, R=[zeros_b.b], W=[bank.b], skip_group_check=True)
                    def scores(kt_):
                        a_, b_ = S1r.next(), S2r.next()
                        k.mm(a_.ap[:, 0:n], Kh.ap[0:64, kt_ * 128:(kt_ + 1) * 128], qt.ap[0:64, 0:n], R=[Kh.b, qt.b], W=[a_.b])
                        k.mm(b_.ap[:, 0:n], Kh.ap[64:128, kt_ * 128:(kt_ + 1) * 128], qt.ap[64:128, 0:n], R=[Kh.b, qt.b], W=[b_.b])
                        return a_, b_

                    nxt = scores(0)
                    for kt in range(nkt):
                        s1, s2 = nxt
                        if kt + 1 < nkt:
                            nxt = scores(kt + 1)
                        P1, P2 = p1r.next(), p2r.next()
                        k.act(P1.ap[:, 0:n], s1.ap[:, 0:n], AF.Exp, scale=0.125, R=[s1.b], W=[P1.b])
                        k.act(P2.ap[:, 0:n], s2.ap[:, 0:n], AF.Exp, scale=0.125, R=[s2.b], W=[P2.b])
                        for qs in range(nqs):
                            for m, P in ((0, P1), (1, P2)):
                                ti = m * 4 + qs
                                bank, c0 = OB[ti // 3], (ti % 3) * 129
                                k.mm(bank.ap[:, c0:c0 + 129], P.ap[:, qs * 128:(qs + 1) * 128], vaug.ap[:, kt, :], start=False,
                                     stop=(kt == nkt - 1), R=[P.b, vaug.b], W=[bank.b], skip_group_check=True)
                    for qs in range(nqs):
                        b1, c1 = OB[qs // 3], (qs % 3) * 129
                        b2, c2 = OB[(4 + qs) // 3], ((4 + qs) % 3) * 129
                        rl, od, jk, yt = rlr.next(), odr.next(), jkr.next(), ytr.next()
                        k.op("dve", lambda e, rl=rl, b1=b1, c1=c1: e.reciprocal(out=rl.ap[:, 0:1], in_=b1.ap[:, c1 + 128:c1 + 129]), [b1.b], [rl.b])
                        k.op("dve", lambda e, rl=rl, b2=b2, c2=c2: e.reciprocal(out=rl.ap[:, 1:2], in_=b2.ap[:, c2 + 128:c2 + 129]), [b2.b, rl.b], [rl.b])
                        k.tt("dve", rl.ap[:, 2:3], rl.ap[:, 1:2], lamt.ap[:, 3:4], ALU.mult, R=[rl.b, lamt.b], W=[rl.b])
                        k.ts("dve", od.ap, b1.ap[:, c1:c1 + 128], rl.ap[:, 0:1], None, ALU.mult, R=[b1.b, rl.b], W=[od.b])
                        k.stt("dve", od.ap, b2.ap[:, c2:c2 + 128], rl.ap[:, 2:3], od.ap, ALU.mult, ALU.add, R=[b2.b, rl.b, od.b], W=[od.b])
                        k.act(jk.ap, od.ap, AF.Square, accum_out=rl.ap[:, 3:4], R=[od.b, rl.b], W=[jk.b, rl.b])
                        k.act(rl.ap[:, 4:5], rl.ap[:, 3:4], AF.Sqrt, bias=EPS, scale=1.0 / 128, R=[rl.b], W=[rl.b])
                        k.op("dve", lambda e, rl=rl: e.reciprocal(out=rl.ap[:, 5:6], in_=rl.ap[:, 4:5]), [rl.b], [rl.b])
                        k.stt("dve", yt.ap, od.ap, rl.ap[:, 5:6], gsub.ap, ALU.mult, ALU.mult, R=[od.b, rl.b, gsub.b], W=[yt.b])
                        k.tr(tpb[:, qs * 128:(qs + 1) * 128], yt.ap, ident_b.ap, R=[yt.b, ident_b.b], W=[tp.b])
                    yf = yfr.next()
                    k.copy("act", yf.ap[:, 0:n], tpb[:, 0:n], R=[tp.b], W=[yf.b])
                    k.dma("pool", dr["dya"][hh, :, t0:t0 + n], yf.ap[:, 0:n], reads=[yf.b])
            k.barrier()
            if dstop <= 1:
                return True
            outproj_phase(l, dr["dya"], "dwo", 128, 8)
            return done()

        PH["dif"] = dif_phase

        def final_phase():
            xt = Rot([k.sb(f"x{i}", [8, 1024]) for i in range(2)])
            sq = Rot([k.sb(f"sq{i}", [512]) for i in range(2)])
            rs = k.sb("rs", [1024])
            for i in range(SEQ // 1024):
                t0 = CTX + 1024 * i
                x = xt.next()
                k.dma("sp", x.ap, dr["xs"][:, t0:t0 + 1024].rearrange("(c p) t -> p c t", p=128), writes=[x.b])
                for sub in range(0, 1024, 512):
                    for c in range(8):
                        q = sq.next()
                        k.act(q.ap, x.ap[:, c, sub:sub + 512], AF.Square, R=[x.b], W=[q.b])
                        k.mm(ps[0].ap, ones_f.ap, q.ap, start=(c == 0), stop=(c == 7), R=[q.b, ones_f.b], W=[ps[0].b])
                    k.rstd(rs.ap[:, sub:sub + 512], rs.b, ps[0].ap, ps[0].b, 1.0 / D)
                    for c in range(8):
                        k.stt("dve", x.ap[:, c, sub:sub + 512], x.ap[:, c, sub:sub + 512], S("final_g", c, 1), rs.ap[:, sub:sub + 512],
                              ALU.mult, ALU.mult, R=[x.b, rs.b, smt.b], W=[x.b])
                k.dma("pool", dr["out"][:, 1024 * i:1024 * (i + 1)].rearrange("(c p) t -> p c t", p=128), x.ap, reads=[x.b])

        def program():
            if SIM_ONLY in ("rk",):
                ret_phase(1)
                return
            for l in range(DEPTH):
                ffn_phase(l, 0)
                if done():
                    return
                if l in MIXERS:
                    if MIXERS[l](l):
                        return
                ffn_phase(l, 1)
                if done():
                    return
            final_phase()

        MIXERS = {}
        if "lru" in PH:
            MIXERS[0] = PH["lru"]
            MIXERS[3] = PH["lru"]
        if "ret" in PH:
            MIXERS[1] = PH["ret"]
        if "dif" in PH:
            MIXERS[2] = PH["dif"]
        program()
        k.barrier(final=True)
        k.emit(esems, dsems)
    return nc


NONW = ("smalls", "mod_w", "r_cos", "r_sin", "d_cos", "d_sin", "rqtab")


def kernel(**inputs):
    inp = {k_: np.asarray(v) for k_, v in inputs.items()}
    sh, soff = prep_static(inp)
    wshapes = {k_: tuple(v.shape) for k_, v in sh.items()
               if k_ not in NONW}
    ns = sh["smalls"].shape[1]
    nc = build_program(soff, ns, wshapes)
    in_maps = []
    for b in range(NCORES):
        m = dict(sh)
        m["xin"] = np.ascontiguousarray(np.concatenate([inp["ctx"][b].T, inp["x"][b].T], axis=1), dtype=np.float32)
        m["cond"] = col128(inp["c"][b])
        in_maps.append(m)
    res = run_bass_kernel_spmd(nc, in_maps, core_ids=list(range(NCORES)))
    out = np.stack([np.ascontiguousarray(res.results[b]["out"].T) for b in range(NCORES)], axis=0)
    return out.astype(np.float32)
```

```python
import math
import os
import numpy as np
import concourse.bass as bass
import concourse.mybir as mybir
from concourse.bass_utils import run_bass_kernel_spmd

F32 = mybir.dt.float32
BF16 = mybir.dt.bfloat16
AF = mybir.ActivationFunctionType
ALU = mybir.AluOpType

D = 1024
SEQ = 8192
CTX = 256
TT = SEQ + CTX
DEPTH = 4
DFF = 2816
NFC = 22
DRNN = 1280
EPS = 1e-6
NCORES = 4
EPOCH = 60000
NENGSEM = 6
ARENA = 52000
DEBUG_LINES = bool(os.environ.get("KB_DUMP"))


class Buf:
    __slots__ = ("name", "w", "rd", "dsem")

    def __init__(self, name):
        self.name = name
        self.w = None
        self.rd = {}
        self.dsem = {}


class Ins:
    __slots__ = ("fn", "waits", "mark", "dma", "cnt", "real", "line")

    def __init__(self, fn):
        self.fn = fn
        self.line = 0
        self.waits = []
        self.mark = False
        self.dma = None
        self.cnt = 0
        self.real = fn is not None


class T:
    __slots__ = ("ap", "b")

    def __init__(self, ap, b):
        self.ap = ap
        self.b = b

    def __getitem__(self, idx):
        return self.ap[idx]


COMPUTE = ("pe", "act", "dve", "pool")
ENGS = ("pe", "act", "dve", "pool", "sp")


class KB:
    def __init__(self, nc):
        self.nc = nc
        self.streams = {e: [] for e in ENGS}
        self.seen = {e: {} for e in ENGS}
        self.last_real = {e: -1 for e in ENGS}
        self.ndsem = 100 - 4 * NENGSEM - 2
        self.dsem_vals = [0] * self.ndsem
        self.nsw = 30
        self.free_dsems = {"sw": list(range(self.nsw)), "hw": list(range(self.nsw, self.ndsem))}
        self.bg_sems = set()
        self.live_bufs = []
        self.arena_top = 0
        self.arena_base = 0
        self.persist = []

    def buf(self, name):
        b = Buf(name)
        self.live_bufs.append(b)
        return b

    def sb(self, name, shape, dtype=F32, parts=128):
        n = 1
        for s_ in shape:
            n *= s_
        words = n if dtype == F32 else (n + 1) // 2
        words += words & 1
        off = self.arena_top
        self.arena_top += words
        assert self.arena_top <= ARENA, (name, self.arena_top)
        if dtype == F32:
            ap = self.arena_f[0:parts, off:off + n]
        else:
            ap = self.arena_b[0:parts, 2 * off:2 * off + n]
        if len(shape) == 2:
            ap = ap.rearrange("p (a b) -> p a b", a=shape[0])
        elif len(shape) == 3:
            ap = ap.rearrange("p (a b c) -> p a b c", a=shape[0], b=shape[1])
        return T(ap, self.buf(name))

    def phase_reset(self):
        self.arena_top = self.arena_base

    def _wait_list(self, eng, reads, writes):
        waits = []
        raw = []
        war = []
        for b in reads:
            if b.w is not None:
                raw.append(b.w)
        rset = set(id(b) for b in reads)
        for b in writes:
            if b.w is not None:
                (raw if id(b) in rset else war).append(b.w)
            war.extend(b.rd.values())
        seen = self.seen[eng]
        for lst, is_war in ((raw, False), (war, True)):
            for t in lst:
                if t[0] == "e":
                    src, idx = t[1], t[2]
                    if src == eng and (eng == "pe" or is_war):
                        continue
                    if seen.get(src, -1) >= idx:
                        continue
                    seen[src] = idx
                    self.streams[src][idx].mark = True
                    waits.append(t)
                else:
                    key = ("d", t[1])
                    if seen.get(key, -1) >= t[2]:
                        continue
                    seen[key] = t[2]
                    waits.append(t)
        return waits

    def op(self, eng, fn, reads=(), writes=()):
        ins = Ins(fn)
        if DEBUG_LINES:
            import sys
            ins.line = sys._getframe(2).f_lineno
        ins.waits = self._wait_list(eng, reads, writes)
        idx = len(self.streams[eng])
        self.streams[eng].append(ins)
        self.last_real[eng] = idx
        tok = ("e", eng, idx)
        for b in reads:
            b.rd[eng] = tok
        for b in writes:
            b.w = tok
            b.rd = {}
        return ins

    def dma(self, q, out, in_, reads=(), writes=(), owner=None):
        ins = Ins(lambda e: e.dma_start(out=out, in_=in_))
        if DEBUG_LINES:
            import sys
            ins.line = -sys._getframe(1).f_lineno
        ins.waits = self._wait_list(q, reads, writes)
        if owner is None:
            owner = writes[0] if writes else reads[0]
        cls = "sw" if q == "pool" else "hw"
        if cls not in owner.dsem:
            owner.dsem[cls] = self.free_dsems[cls].pop()
        sem = owner.dsem[cls]
        self.dsem_vals[sem] += 16
        tok = ("d", sem, self.dsem_vals[sem])
        ins.dma = sem
        self.streams[q].append(ins)
        for b in reads:
            b.rd[("d", sem)] = tok
        for b in writes:
            b.w = tok
            b.rd = {}
        return tok

    def bg_dma(self, q, out, in_, gbuf):
        ins = Ins(lambda e: e.dma_start(out=out, in_=in_))
        if "sw" not in gbuf.dsem:
            gbuf.dsem["sw"] = self.free_dsems["sw"].pop()
            self.bg_sems.add(gbuf.dsem["sw"])
        sem = gbuf.dsem["sw"]
        self.dsem_vals[sem] += 16
        ins.dma = sem
        self.streams[q].append(ins)
        gbuf.w = ("d", sem, self.dsem_vals[sem])

    def barrier(self, final=False, soft=False):
        last = dict(self.last_real)
        for eng in ENGS:
            ins = Ins(None)
            seen = self.seen[eng]
            for src in COMPUTE:
                idx = last[src]
                if src == eng or idx < 0:
                    continue
                if seen.get(src, -1) >= idx:
                    continue
                seen[src] = idx
                self.streams[src][idx].mark = True
                ins.waits.append(("e", src, idx))
            for sem in range(self.ndsem):
                val = self.dsem_vals[sem]
                if val == 0 or (sem in self.bg_sems and not final):
                    continue
                key = ("d", sem)
                if seen.get(key, -1) >= val:
                    continue
                seen[key] = val
                ins.waits.append(("d", sem, val))
            self.streams[eng].append(ins)
        for b in self.live_bufs:
            for cls, sem in list(b.dsem.items()):
                if sem in self.bg_sems:
                    continue
                if sem not in self.free_dsems[cls]:
                    self.free_dsems[cls].append(sem)
                del b.dsem[cls]
        if not soft:
            self.live_bufs = list(self.persist)
            self.phase_reset()

    def emit(self, esems, dsems):
        nc = self.nc
        for eng in COMPUTE:
            c = 0
            for ins in self.streams[eng]:
                if ins.mark:
                    c += 1
                    ins.cnt = c
            assert c <= EPOCH * NENGSEM, (eng, c)
            if os.environ.get("KB_STATS"):
                print("KB", eng, "n_ins", len(self.streams[eng]), "marked", c, "waits", sum(len(i.waits) for i in self.streams[eng]))
        if os.environ.get("KB_STATS"):
            print("KB sp n_ins", len(self.streams["sp"]), "max dsem", max(self.dsem_vals), self.dsem_vals)
        streams = self.streams
        if os.environ.get("KB_DUMP"):
            with open(os.environ["KB_DUMP"], "w") as f:
                for eng in ENGS:
                    f.write("=== %s\n" % eng)
                    for i, ins in enumerate(streams[eng]):
                        ws = []
                        for t in ins.waits:
                            if t[0] == "e":
                                ws.append("%s#%d(c%d)" % (t[1], t[2], streams[t[1]][t[2]].cnt))
                            else:
                                ws.append("d%d>=%d" % (t[1], t[2]))
                        f.write("%d L%d %s%s waits=[%s]\n" % (i, ins.line, "M%d " % ins.cnt if ins.mark else "", "DMA%d " % ins.dma if ins.dma is not None else "", ",".join(ws)))

        def semval(t):
            if t[0] == "e":
                cnt = streams[t[1]][t[2]].cnt
                assert cnt > 0
                return esems[t[1]][(cnt - 1) // EPOCH], (cnt - 1) % EPOCH + 1
            return dsems[t[1]], t[2]

        def run(eng, e):
            for ins in streams[eng]:
                for t in ins.waits:
                    sm, v = semval(t)
                    e.wait_ge(sm, v)
                if ins.fn is None:
                    continue
                r = ins.fn(e)
                if ins.dma is not None:
                    r.then_inc(dsems[ins.dma], 16)
                elif ins.mark:
                    r.then_inc(esems[eng][(ins.cnt - 1) // EPOCH], 1)

        with nc.Block() as block:
            @block.tensor
            def _(e):
                run("pe", e)

            @block.scalar
            def _(e):
                run("act", e)

            @block.vector
            def _(e):
                run("dve", e)

            @block.gpsimd
            def _(e):
                run("pool", e)

            @block.sync
            def _(e):
                run("sp", e)

    def mm(self, out, lhsT, rhs, start=True, stop=True, R=(), W=(), **kw):
        return self.op("pe", lambda e: e.matmul(out, lhsT, rhs, start=start, stop=stop, **kw), R, W)

    def tr(self, out, in_, ident, R=(), W=()):
        return self.op("pe", lambda e: e.transpose(out, in_, ident), R, W)

    def act(self, out, in_, func, R=(), W=(), bias=None, scale=None, accum_out=None):
        kw = {}
        if accum_out is not None:
            kw["accum_out"] = accum_out
        if bias is not None:
            kw["bias"] = bias
        if scale is not None:
            kw["scale"] = scale
        return self.op("act", lambda e: e.activation(out=out, in_=in_, func=func, **kw), R, W)

    def tt(self, eng, out, in0, in1, op, R=(), W=()):
        return self.op(eng, lambda e: e.tensor_tensor(out=out, in0=in0, in1=in1, op=op), R, W)

    def ts(self, eng, out, in0, s1, s2, op0, op1=None, R=(), W=()):
        if op1 is None:
            return self.op(eng, lambda e: e.tensor_scalar(out=out, in0=in0, scalar1=s1, scalar2=None, op0=op0), R, W)
        return self.op(eng, lambda e: e.tensor_scalar(out=out, in0=in0, scalar1=s1, scalar2=s2, op0=op0, op1=op1), R, W)

    def stt(self, eng, out, in0, scalar, in1, op0, op1, R=(), W=()):
        return self.op(eng, lambda e: e.scalar_tensor_tensor(out=out, in0=in0, scalar=scalar, in1=in1, op0=op0, op1=op1), R, W)

    def copy(self, eng, out, in_, R=(), W=()):
        if eng == "act":
            return self.op("act", lambda e: e.copy(out=out, in_=in_), R, W)
        return self.op(eng, lambda e: e.tensor_copy(out=out, in_=in_), R, W)

    def rstd(self, out, outb, in_, inb, scale):
        self.act(out, in_, AF.Sqrt, bias=EPS, scale=scale, R=[inb], W=[outb])
        self.op("dve", lambda e: e.reciprocal(out=out, in_=out), [outb], [outb])

    def scan(self, out, a, b, init, R=(), W=()):
        return self.op("dve", lambda e: e.tensor_tensor_scan(out=out, data0=a, data1=b, initial=init,
                                                             op0=ALU.mult, op1=ALU.add), R, W)

    def memset(self, eng, ap, val, W=()):
        return self.op(eng, lambda e: e.memset(ap, val), (), W)


class Rot:
    def __init__(self, tiles):
        self.t = tiles
        self.i = 0

    def next(self):
        r = self.t[self.i % len(self.t)]
        self.i += 1
        return r


def pack_w(W, kp, gw):
    K, M = W.shape
    nk, ng = K // kp, M // gw
    a = W.reshape(nk, kp, ng, gw).transpose(2, 1, 0, 3)
    return np.ascontiguousarray(a).reshape(ng, kp, nk * gw)


def col128(v):
    v = np.asarray(v, np.float32)
    lead = v.shape[:-1]
    n = v.shape[-1] // 128
    a = v.reshape(*lead, n, 128)
    a = np.moveaxis(a, -1, 0)
    return np.ascontiguousarray(a).reshape(128, -1)


def col80(v):
    v = np.asarray(v, np.float32)
    lead = v.shape[:-1]
    n = v.shape[-1] // 80
    a = v.reshape(*lead, n, 80)
    a = np.moveaxis(a, -1, 0)
    out = np.zeros((128, a.reshape(80, -1).shape[1]), np.float32)
    out[:80] = a.reshape(80, -1)
    return out


class Smalls:
    def __init__(self):
        self.cols = []
        self.off = {}
        self.n = 0

    def add(self, name, arr):
        arr = np.asarray(arr, np.float32)
        assert arr.shape[0] == 128
        self.off[name] = (self.n, arr.shape[1])
        self.cols.append(arr)
        self.n += arr.shape[1]

    def build(self):
        return np.ascontiguousarray(np.concatenate(self.cols, axis=1))


RET_H = 4
RET_DK = 256
RET_DV = 512


def ret_consts():
    lg = np.log1p(-np.exp2(-5.0 - np.arange(RET_H, dtype=np.float32))).astype(np.float32)
    idx = np.arange(128, dtype=np.float32)
    dist = np.abs(idx[None, :] - idx[:, None])
    maskT = np.exp(lg[:, None, None] * dist[None]).astype(np.float32)
    qdf = np.exp(lg[:, None] * (idx[None] + 1.0)).astype(np.float32)
    qdb = np.exp(lg[:, None] * (128.0 - idx[None])).astype(np.float32)
    kdf = np.exp(lg[:, None] * (127.0 - idx[None])).astype(np.float32)
    kdb = np.exp(lg[:, None] * idx[None]).astype(np.float32)
    cd = np.exp(lg * 128.0).astype(np.float32)
    return maskT, qdf, qdb, kdf, kdb, cd


def prep_static(inp):
    sm = Smalls()
    sm.add("norm_g", col128(inp["norm_g"]))
    sm.add("final_g", col128(inp["final_g"]))
    sm.add("mod_b", col128(inp["mod_b"]))
    sm.add("c_ctx", col128(inp["c_ctx"]))
    sm.add("conv_w", col80(inp["lru_conv_w"]))
    sm.add("conv_b", col80(inp["lru_conv_b"]))
    sm.add("gate_b", col80(inp["lru_gate_b"].reshape(2, 2, 2, DRNN)))
    sm.add("lam", col80(inp["lru_lam"]))
    maskT, qdf, qdb, kdf, kdb, cd = ret_consts()
    sm.add("r_mask", np.ascontiguousarray(maskT.transpose(1, 0, 2)).reshape(128, RET_H * 128))
    sm.add("r_kdf", np.ascontiguousarray(kdf.T))
    sm.add("r_kdb", np.ascontiguousarray(kdb.T))
    rqtab = np.concatenate([np.tile(qdf, (1, 4)).reshape(1, RET_H * 512), np.tile(qdb, (1, 4)).reshape(1, RET_H * 512)], axis=1)
    sm.add("r_cd", np.broadcast_to(cd.reshape(1, RET_H), (128, RET_H)))
    sm.add("dif_lam", np.pad(np.ascontiguousarray(inp["dif_lam"][0].T), ((0, 64), (0, 0))))
    sm.add("dif_subln", np.broadcast_to(inp["dif_subln"][0].reshape(1, 128), (128, 128)))
    sm.add("ident", np.eye(128, dtype=np.float32))
    sh = {"smalls": sm.build()}
    sh["rqtab"] = np.ascontiguousarray(np.broadcast_to(rqtab, (128, 2 * RET_H * 512)), dtype=np.float32)
    W = {}
    for l in range(DEPTH):
        for s in range(2):
            wi = inp["ffn_w_in"][l, s]
            a = wi.reshape(8, 128, 2, NFC, 128).transpose(3, 1, 0, 2, 4)
            W[f"win{l}{s}"] = np.ascontiguousarray(a).reshape(NFC * 128, 2048)
            wo = inp["ffn_w_out"][l, s]
            b = wo.reshape(NFC, 128, 8, 128).transpose(2, 1, 0, 3)
            W[f"wout{l}{s}"] = np.ascontiguousarray(b).reshape(8 * 128, NFC * 128)
    for n, l in enumerate((0, 3)):
        w_in = inp["lru_w_in"][n]
        W[f"lwx{l}"] = pack_w(w_in[:, :DRNN], 128, 80).reshape(16 * 128, 640)
        W[f"lwg{l}"] = pack_w(w_in[:, DRNN:], 128, 80).reshape(16 * 128, 640)
        gw = inp["lru_gate_w"][n]
        a = gw.reshape(2, 2, 8, 2, 80, 2, 80).transpose(4, 0, 1, 2, 3, 5, 6)
        W[f"lgw{l}"] = np.ascontiguousarray(a).reshape(80, 128 * 80)
        W[f"lwo{l}"] = pack_w(inp["lru_w_out"][n], 80, 128).reshape(8 * 80, 16 * 128)
    rw = inp["ret_w_in"][0]
    W["rwq"] = pack_w(rw[:, 0:1024], 128, 128).reshape(8 * 128, 1024)
    W["rwk"] = pack_w(rw[:, 1024:2048], 128, 128).reshape(8 * 128, 1024)
    W["rwv"] = pack_w(rw[:, 2048:4096], 128, 512).reshape(4 * 128, 4096)
    W["rwg"] = pack_w(rw[:, 4096:6144], 128, 128).reshape(16 * 128, 1024)
    W["rwo"] = pack_w(inp["ret_w_out"][0], 128, 128).reshape(8 * 128, 16 * 128)
    dw = inp["dif_w_in"][0]
    perm = np.arange(1024).reshape(16, 2, 32)[:, ::-1, :].reshape(-1)
    W["dwq"] = pack_w(dw[:, 0:1024], 128, 128).reshape(8 * 128, 1024)
    W["dwqs"] = pack_w(dw[:, 0:1024][:, perm], 128, 128).reshape(8 * 128, 1024)
    W["dwk"] = pack_w(dw[:, 1024:2048], 128, 128).reshape(8 * 128, 1024)
    W["dwks"] = pack_w(dw[:, 1024:2048][:, perm], 128, 128).reshape(8 * 128, 1024)
    W["dwv"] = pack_w(dw[:, 2048:3072], 128, 512).reshape(2 * 128, 4096)
    W["dwo"] = pack_w(inp["dif_w_out"][0], 128, 128).reshape(8 * 128, 1024)
    for k_, v_ in W.items():
        sh[k_] = np.ascontiguousarray(v_, dtype=np.float32)
    sh["mod_w"] = np.ascontiguousarray(inp["mod_w"], dtype=np.float32).reshape(DEPTH * D, 9 * D)
    theta = (np.float32(10000.0) ** (-np.linspace(0.0, 1.0, 128, dtype=np.float32))).astype(np.float32)
    ang = np.arange(SEQ, dtype=np.float32)[None, :] * theta[:, None]
    sh["r_cos"] = np.cos(ang).astype(np.float32)
    sh["r_sin"] = np.sin(ang).astype(np.float32)
    nfreq = 16
    inv = (np.float32(10000.0) ** (-np.arange(nfreq, dtype=np.float32) / nfreq)).astype(np.float32)
    rows = SEQ // 64
    row = np.broadcast_to(np.arange(rows, dtype=np.float32)[:, None], (rows, 64)).reshape(-1)
    col = np.broadcast_to(np.arange(64, dtype=np.float32)[None, :], (rows, 64)).reshape(-1)
    angd = np.concatenate([row[:, None] * inv, col[:, None] * inv], axis=-1).astype(np.float32)
    cosd, sind = np.cos(angd).astype(np.float32), np.sin(angd).astype(np.float32)
    ii = (np.arange(128) % 64) % 32
    sgn = np.where((np.arange(128) % 64) < 32, -1.0, 1.0).astype(np.float32)
    sh["d_cos"] = np.ascontiguousarray(cosd[:, ii].T)
    sh["d_sin"] = np.ascontiguousarray((sind[:, ii] * sgn[None, :]).T)
    return sh, sm.off


from contextlib import ExitStack
import os

GELU_C0 = math.sqrt(2.0 / math.pi)
GELU_C1 = GELU_C0 * 0.044715
LAM_INIT2 = 0.8 - 0.6 * math.exp(-0.3 * 2)

SCRATCH = {
    "xs": ([D, TT], F32),
    "xb": ([16, 80, TT], F32), "gl": ([16, 80, TT], F32), "hf": ([16, 80, TT], F32),
    "ym": ([16, 80, TT], BF16),
    "rq": ([8, 128, TT], BF16), "rk": ([8, 128, TT], BF16),
    "rqdf": ([8, 128, TT], BF16), "rqdb": ([8, 128, TT], BF16),
    "rkdf": ([TT, 1024], BF16), "rkdb": ([TT, 1024], BF16),
    "rv": ([TT, 2048], BF16), "rsg": ([16, 128, TT], BF16),
    "ro": ([16, 128, TT], F32), "ryr": ([16, 128, TT], BF16),
    "dq": ([8, 128, TT], BF16), "dk": ([8, 128, TT], BF16), "dv": ([TT, 1024], BF16),
    "dya": ([8, 128, TT], BF16),
}


def wgroup(name):
    if name.startswith("win") or name.startswith("wout"):
        return (int(name[-2]), "f" + name[-1])
    if name.startswith("l"):
        return (int(name[-1]), "m")
    if name.startswith("r"):
        return (1, "m")
    return (2, "m")


def build_program(soff, ns, wshapes, stop_after=None, debug_out=()):
    nc = bass.Bass("TRN2", target_bir_lowering=False)
    k = KB(nc)
    dr = {}

    def din(name, shape, dt=F32):
        dr[name] = nc.dram_tensor(name, list(shape), dt, kind="ExternalInput").ap()

    din("xin", [D, TT])
    din("cond", [128, 8])
    din("smalls", [128, ns])
    din("mod_w", [DEPTH * D, 9 * D])
    for nm in ("r_cos", "r_sin", "d_cos", "d_sin"):
        din(nm, [128, SEQ])
    din("rqtab", [128, 2 * RET_H * 512])
    wbf = {}
    for nm, shp in wshapes.items():
        din(nm, shp)
        wbf[nm] = nc.dram_tensor("b_" + nm, list(shp), BF16, kind="Internal").ap()
    for nm, (shp, dt) in SCRATCH.items():
        kind = "ExternalOutput" if nm in debug_out else "Internal"
        dr[nm] = nc.dram_tensor(nm, list(shp), dt, kind=kind).ap()
    dr["out"] = nc.dram_tensor("out", [D, SEQ], F32, kind="ExternalOutput").ap()

    es = ExitStack()
    with es:
        arena = es.enter_context(nc.sbuf_tensor("arena", [128, ARENA], F32))
        k.arena_f = arena
        k.arena_b = arena.bitcast(BF16)
        psb = [es.enter_context(nc.psum_tensor(f"ps{i}", [128, 512], F32)) for i in range(8)]
        esems = {e: [es.enter_context(nc.semaphore(f"s_{e}{i}")) for i in range(NENGSEM)] for e in COMPUTE}
        dsems = [es.enter_context(nc.semaphore(f"d{i}")) for i in range(k.ndsem)]
        ps = [T(psb[i][:, :], Buf(f"ps{i}")) for i in range(8)]
        psbf = {ps[i]: psb[i].bitcast(BF16)[:, :] for i in range(8)}

        smt = k.sb("smalls", [ns])
        modt = k.sb("modt", [DEPTH * 2 * 72])
        tabA = k.sb("tabA", [DEPTH * 2 * 3 * 8])
        tabG = k.sb("tabG", [DEPTH * 2 * 3 * 8])
        ones_f = k.sb("ones_f", [128])
        ident_b = k.sb("ident_b", [128], BF16)
        zeros_b = k.sb("zeros_b", [512], BF16)
        lamt = k.sb("lamt", [8])
        gsub = k.sb("gsub", [128])
        k.persist = [smt.b, modt.b, tabA.b, tabG.b, ones_f.b, ident_b.b, zeros_b.b, lamt.b, gsub.b] + [p.b for p in ps]
        k.arena_base = k.arena_top

        def S(name, lo=0, n=None):
            off, w = soff[name]
            if n is None:
                n = w - lo
            return smt.ap[:, off + lo:off + lo + n]

        def S80(name, lo=0, n=1):
            off, w = soff[name]
            return smt.ap[0:80, off + lo:off + lo + n]

        def A_(l, kind, s, dc):
            c = ((l * 2 + kind) * 3 + s) * 8 + dc
            return tabA.ap[:, c:c + 1]

        def G_(l, kind, s, dc):
            c = ((l * 2 + kind) * 3 + s) * 8 + dc
            return tabG.ap[:, c:c + 1]

        def B_(l, kind, s, dc):
            c = (l * 2 + kind) * 72 + 3 * s * 8 + dc
            return modt.ap[:, c:c + 1]

        gbufs = {}
        order = sorted(wshapes.keys(), key=lambda n_: (wgroup(n_)[0], {"f0": 0, "m": 1, "f1": 2}[wgroup(n_)[1]]))
        for nm in order:
            if os.environ.get("SIM_ONLY", "") == "rk" and nm != "rwk":
                continue
            g = wgroup(nm)
            if g not in gbufs:
                gbufs[g] = Buf("wg%s%s" % g)
            R_, C_ = wshapes[nm]
            rp = max(1, (1 << 21) // C_)
            for r0 in range(0, R_, rp):
                r1 = min(R_, r0 + rp)
                k.bg_dma("pool", wbf[nm][r0:r1, :], dr[nm][r0:r1, :], gbufs[g])

        def GB(nm):
            return gbufs[wgroup(nm)]

        k.dma("sp", smt.ap, dr["smalls"], writes=[smt.b])
        k.memset("dve", ones_f.ap, 1.0, W=[ones_f.b])
        k.memset("dve", zeros_b.ap, 0.0, W=[zeros_b.b])
        k.copy("dve", ident_b.ap, S("ident"), R=[smt.b], W=[ident_b.b])
        cond = k.sb("cond", [16])
        sc = k.sb("sc", [16])
        k.dma("sp", cond.ap[:, 0:8], dr["cond"], writes=[cond.b])
        k.copy("dve", cond.ap[:, 8:16], S("c_ctx"), R=[smt.b, cond.b], W=[cond.b])
        k.act(sc.ap, cond.ap, AF.Silu, R=[cond.b], W=[sc.b])
        mwr = Rot([k.sb(f"mw{i}", [8, 1152]) for i in range(2)])
        for l in range(DEPTH):
            mps = ps[l % 2]
            for cg in range(8):
                wt = mwr.next()
                k.dma("sp", wt.ap, dr["mod_w"][l * D:(l + 1) * D, cg * 1152:(cg + 1) * 1152].rearrange("(kc p) n -> p kc n", p=128), writes=[wt.b])
                for cc in range(9):
                    j = cg * 9 + cc
                    for kc in range(8):
                        k.mm(mps.ap[:, 2 * j:2 * j + 2], wt.ap[:, kc, cc * 128:(cc + 1) * 128], sc.ap[:, kc:16:8],
                             start=(kc == 0), stop=(kc == 7), R=[wt.b, sc.b], W=[mps.b])
            for kind in range(2):
                base = (l * 2 + kind) * 72
                k.tt("dve", modt.ap[:, base:base + 72], mps.ap[:, kind:144:2], S("mod_b", l * 72, 72), ALU.add,
                     R=[mps.b, smt.b], W=[modt.b])
                for s in range(3):
                    col = ((l * 2 + kind) * 3 + s) * 8
                    k.stt("dve", tabA.ap[:, col:col + 8], modt.ap[:, base + (3 * s + 1) * 8:base + (3 * s + 2) * 8], 1.0,
                          S("norm_g", (l * 3 + s) * 8, 8), ALU.add, ALU.mult, R=[modt.b, smt.b], W=[tabA.b])
                    k.ts("dve", tabG.ap[:, col:col + 8], modt.ap[:, base + (3 * s + 2) * 8:base + (3 * s + 3) * 8],
                         (1.0 if s == 1 else 0.5), None, ALU.mult, R=[modt.b], W=[tabG.b])
        pr = k.sb("lamprod", [2])
        k.tt("dve", pr.ap[0:64, 0:1], S("dif_lam")[0:64, 0:1], S("dif_lam")[0:64, 1:2], ALU.mult, R=[smt.b], W=[pr.b])
        k.tt("dve", pr.ap[0:64, 1:2], S("dif_lam")[0:64, 2:3], S("dif_lam")[0:64, 3:4], ALU.mult, R=[smt.b, pr.b], W=[pr.b])
        k.mm(ps[2].ap[:, 0:2], ones_f.ap[0:64, :], pr.ap[0:64, 0:2], R=[pr.b, ones_f.b], W=[ps[2].b])
        k.act(lamt.ap[:, 0:2], ps[2].ap[:, 0:2], AF.Exp, R=[ps[2].b], W=[lamt.b])
        k.tt("dve", lamt.ap[:, 2:3], lamt.ap[:, 0:1], lamt.ap[:, 1:2], ALU.subtract, R=[lamt.b], W=[lamt.b])
        k.ts("dve", lamt.ap[:, 3:4], lamt.ap[:, 2:3], -1.0, -LAM_INIT2, ALU.mult, ALU.add, R=[lamt.b], W=[lamt.b])
        k.ts("dve", gsub.ap, S("dif_subln"), 1.0 - LAM_INIT2, None, ALU.mult, R=[smt.b], W=[gsub.b])
        k.barrier()
        nph = [0]

        def done():
            k.barrier()
            nph[0] += 1
            return stop_after is not None and nph[0] >= stop_after

        def load_norm(src, t0, n, kind, l, sl, x, h, sq, rs, tmp, pss, nrm=True):
            k.dma("sp", x.ap[:, :, 0:n], src[:, t0:t0 + n].rearrange("(c p) t -> p c t", p=128), writes=[x.b])
            if not nrm:
                return
            for sub in range(0, n, 512):
                w = min(512, n - sub)
                for c in range(8):
                    q = sq.next()
                    k.act(q.ap[:, 0:w], x.ap[:, c, sub:sub + w], AF.Square, R=[x.b], W=[q.b])
                    k.mm(pss.ap[:, 0:w], ones_f.ap, q.ap[:, 0:w], start=(c == 0), stop=(c == 7), R=[q.b, ones_f.b], W=[pss.b])
                k.rstd(rs.ap[:, sub:sub + w], rs.b, pss.ap[:, 0:w], pss.b, 1.0 / D)
                for c in range(8):
                    t = tmp.next()
                    k.stt("dve", t.ap[:, 0:w], x.ap[:, c, sub:sub + w], A_(l, kind, sl, c), rs.ap[:, sub:sub + w],
                          ALU.mult, ALU.mult, R=[x.b, rs.b, tabA.b], W=[t.b])
                    k.act(h.ap[:, c, sub:sub + w], t.ap[:, 0:w], AF.Identity, bias=B_(l, kind, sl, c), scale=1.0,
                          R=[t.b, modt.b], W=[h.b])

        NBLK = int(os.environ.get("NBLK", "0"))
        SIM_ONLY = os.environ.get("SIM_ONLY", "")

        def blocks(bs):
            r = [(0, CTX, 1)] + [(CTX + bs * i, bs, 0) for i in range(SEQ // bs)]
            return r[:NBLK] if NBLK else r

        def xsrc(l, first):
            return dr["xin"] if (l == 0 and first) else dr["xs"]

        def ffn_phase(l, s):
            sl = 0 if s == 0 else 2
            src = xsrc(l, s == 0)
            gb = GB(f"win{l}{s}")
            xt = Rot([k.sb(f"x{i}", [8, 1024]) for i in range(2)])
            hr = Rot([k.sb(f"h{i}", [8, 1024], BF16) for i in range(2)])
            u = [k.sb(f"u{f}", [1024], BF16) for f in range(NFC)]
            wi = Rot([k.sb(f"wi{i}", [8, 256], BF16) for i in range(3)])
            wo = Rot([k.sb(f"wo{i}", [NFC, 128], BF16) for i in range(2)])
            sq = Rot([k.sb(f"sq{i}", [512]) for i in range(2)])
            tmp = Rot([k.sb(f"tmp{i}", [512]) for i in range(2)])
            sa = Rot([k.sb(f"sa{i}", [512]) for i in range(2)])
            rs = k.sb("rs", [1024])
            pss = ps[0]
            pa = Rot([ps[1], ps[2]])
            pg = Rot([ps[3], ps[4]])
            py = Rot([ps[5], ps[6]])
            bl = blocks(1024)
            xs_ = [None] * len(bl)
            hs_ = [None] * len(bl)
            xs_[0], hs_[0] = xt.next(), hr.next()
            load_norm(src, bl[0][0], bl[0][1], bl[0][2], l, sl, xs_[0], hs_[0], sq, rs, tmp, pss)
            for bi, (t0, n, kind) in enumerate(bl):
                x, h = xs_[bi], hs_[bi]
                for f in range(NFC):
                    w_ = wi.next()
                    k.dma("sp", w_.ap, wbf[f"win{l}{s}"][f * 128:(f + 1) * 128, :].rearrange("p (kc c) -> p kc c", kc=8),
                          reads=[gb], writes=[w_.b])
                    for sub in range(0, n, 512):
                        wd = min(512, n - sub)
                        a, g = pa.next(), pg.next()
                        for kc in range(8):
                            k.mm(a.ap[:, 0:wd], w_.ap[:, kc, 0:128], h.ap[:, kc, sub:sub + wd], start=(kc == 0), stop=(kc == 7),
                                 R=[w_.b, h.b], W=[a.b])
                        for kc in range(8):
                            k.mm(g.ap[:, 0:wd], w_.ap[:, kc, 128:256], h.ap[:, kc, sub:sub + wd], start=(kc == 0), stop=(kc == 7),
                                 R=[w_.b, h.b], W=[g.b])
                        s_ = sa.next()
                        k.act(s_.ap[:, 0:wd], a.ap[:, 0:wd], AF.Silu, R=[a.b], W=[s_.b])
                        k.tt("dve", u[f].ap[:, sub:sub + wd], s_.ap[:, 0:wd], g.ap[:, 0:wd], ALU.mult, R=[s_.b, g.b], W=[u[f].b])
                if bi + 1 < len(bl):
                    xs_[bi + 1], hs_[bi + 1] = xt.next(), hr.next()
                    nb = bl[bi + 1]
                    load_norm(src, nb[0], nb[1], nb[2], l, sl, xs_[bi + 1], hs_[bi + 1], sq, rs, tmp, pss)
                for dc in range(8):
                    w_ = wo.next()
                    k.dma("sp", w_.ap, wbf[f"wout{l}{s}"][dc * 128:(dc + 1) * 128, :].rearrange("p (fc c) -> p fc c", fc=NFC),
                          reads=[gb], writes=[w_.b])
                    for sub in range(0, n, 512):
                        wd = min(512, n - sub)
                        y = py.next()
                        for f in range(NFC):
                            k.mm(y.ap[:, 0:wd], w_.ap[:, f, :], u[f].ap[:, sub:sub + wd], start=(f == 0), stop=(f == NFC - 1),
                                 R=[w_.b, u[f].b], W=[y.b])
                        k.stt("dve", x.ap[:, dc, sub:sub + wd], y.ap[:, 0:wd], G_(l, kind, sl, dc), x.ap[:, dc, sub:sub + wd],
                              ALU.mult, ALU.add, R=[y.b, x.b, tabG.b], W=[x.b])
                k.dma("pool", dr["xs"][:, t0:t0 + n].rearrange("(c p) t -> p c t", p=128), x.ap[:, :, 0:n], reads=[x.b])

        def outproj_phase(l, ysrc, wname, kp, nkc):
            gb = GB(wname)
            wt = k.sb("wo", [8, nkc, 128], BF16, parts=kp)
            k.dma("sp", wt.ap, wbf[wname].rearrange("(dc p) (c m) -> p dc c m", p=kp, c=nkc), reads=[gb], writes=[wt.b])
            xt = Rot([k.sb(f"x{i}", [8, 512]) for i in range(2)])
            yt = Rot([k.sb(f"y{i}", [nkc, 512], BF16, parts=kp) for i in range(2)])
            py = Rot([ps[0], ps[1], ps[2]])
            for (t0, n, kind) in blocks(512):
                x, y = xt.next(), yt.next()
                k.dma("sp", x.ap[:, :, 0:n], dr["xs"][:, t0:t0 + n].rearrange("(c p) t -> p c t", p=128), writes=[x.b])
                k.dma("sp", y.ap[:, :, 0:n], ysrc[:, :, t0:t0 + n].rearrange("c p t -> p c t"), writes=[y.b])
                for dc in range(8):
                    p_ = py.next()
                    for c in range(nkc):
                        k.mm(p_.ap[:, 0:n], wt.ap[:, dc, c, :], y.ap[:, c, 0:n], start=(c == 0), stop=(c == nkc - 1),
                             R=[wt.b, y.b], W=[p_.b])
                    k.stt("dve", x.ap[:, dc, 0:n], p_.ap[:, 0:n], G_(l, kind, 1, dc), x.ap[:, dc, 0:n], ALU.mult, ALU.add,
                          R=[p_.b, x.b, tabG.b], W=[x.b])
                k.dma("pool", dr["xs"][:, t0:t0 + n].rearrange("(c p) t -> p c t", p=128), x.ap[:, :, 0:n], reads=[x.b])

        PH = {}

        def lru_phase(l):
            n_ = 0 if l == 0 else 1
            gb = GB(f"lwx{l}")
            wx = k.sb("wx", [16, 8, 80], BF16)
            wg = k.sb("wg", [16, 8, 80], BF16)
            k.dma("sp", wx.ap, wbf[f"lwx{l}"].rearrange("(g p) (kc c) -> p g kc c", p=128, kc=8), reads=[gb], writes=[wx.b])
            k.dma("sp", wg.ap, wbf[f"lwg{l}"].rearrange("(g p) (kc c) -> p g kc c", p=128, kc=8), reads=[gb], writes=[wg.b])
            xt = Rot([k.sb(f"x{i}", [8, 512]) for i in range(2)])
            hr = Rot([k.sb(f"h{i}", [8, 512], BF16) for i in range(2)])
            sq = Rot([k.sb(f"sq{i}", [512]) for i in range(2)])
            tmp = Rot([k.sb(f"tmp{i}", [512]) for i in range(2)])
            rs = k.sb("rs", [512])
            xbr = Rot([k.sb(f"xbt{i}", [8, 512], parts=80) for i in range(2)])
            glr = Rot([k.sb(f"glt{i}", [8, 512], parts=80) for i in range(2)])
            pp = Rot([ps[1], ps[2], ps[3], ps[4]])
            for (t0, n, kind) in blocks(512):
                x, h = xt.next(), hr.next()
                load_norm(dr["xs"], t0, n, kind, l, 1, x, h, sq, rs, tmp, ps[0])
                for half in range(2):
                    xbt, glt = xbr.next(), glr.next()
                    for c8 in range(8):
                        c = half * 8 + c8
                        p1 = pp.next()
                        for kc in range(8):
                            k.mm(p1.ap[0:80, 0:n], wx.ap[:, c, kc, :], h.ap[:, kc, 0:n], start=(kc == 0), stop=(kc == 7),
                                 R=[wx.b, h.b], W=[p1.b])
                        k.copy("dve", xbt.ap[:, c8, 0:n], p1.ap[0:80, 0:n], R=[p1.b], W=[xbt.b])
                        p2 = pp.next()
                        for kc in range(8):
                            k.mm(p2.ap[0:80, 0:n], wg.ap[:, c, kc, :], h.ap[:, kc, 0:n], start=(kc == 0), stop=(kc == 7),
                                 R=[wg.b, h.b], W=[p2.b])
                        k.act(glt.ap[:, c8, 0:n], p2.ap[0:80, 0:n], AF.Gelu_apprx_tanh, R=[p2.b], W=[glt.b])
                    k.dma("pool", dr["xb"][half * 8:half * 8 + 8, :, t0:t0 + n].rearrange("c p t -> p c t"), xbt.ap[:, :, 0:n], reads=[xbt.b])
                    k.dma("pool", dr["gl"][half * 8:half * 8 + 8, :, t0:t0 + n].rearrange("c p t -> p c t"), glt.ap[:, :, 0:n], reads=[glt.b])
            k.barrier()
            gwt = k.sb("gwt", [128, 80], BF16, parts=80)
            k.dma("sp", gwt.ap, wbf[f"lgw{l}"].rearrange("p (b c) -> p b c", c=80), reads=[gb], writes=[gwt.b])
            et = k.sb("et", [32], parts=80)
            cdt = k.sb("cdt", [32], parts=80)
            hcdt = k.sb("hcdt", [32], parts=80)
            hbt = k.sb("hbt", [64], parts=80)
            k.act(et.ap, S80("lam", n_ * 32, 32), AF.Exp, scale=-1.0, R=[smt.b], W=[et.b])
            k.act(et.ap, et.ap, AF.Ln, bias=1.0, scale=1.0, R=[et.b], W=[et.b])
            k.ts("dve", cdt.ap, et.ap, -8.0, None, ALU.mult, R=[et.b], W=[cdt.b])
            k.ts("dve", hcdt.ap, et.ap, -4.0, None, ALU.mult, R=[et.b], W=[hcdt.b])
            k.ts("dve", hbt.ap, S80("gate_b", n_ * 64, 64), 0.5, None, ALU.mult, R=[smt.b], W=[hbt.b])
            xbg = k.sb("xbg", [2, TT], parts=80)
            xcr = Rot([k.sb(f"xc{i}", [2, 512], parts=80) for i in range(2)])
            xcbr = Rot([k.sb(f"xcb{i}", [2, 512], BF16, parts=80) for i in range(2)])
            trr = Rot([k.sb(f"tr{i}", [2, 512], parts=80) for i in range(2)])
            tir = Rot([k.sb(f"ti{i}", [2, 512], parts=80) for i in range(2)])
            atr = Rot([k.sb(f"at{i}", [2, 512], parts=80) for i in range(2)])
            a2r = Rot([k.sb(f"a2{i}", [2, 512], parts=80) for i in range(2)])
            btr = Rot([k.sb(f"bt{i}", [2, 512], parts=80) for i in range(2)])
            hbr = Rot([k.sb(f"hb{i}", [2, 512], parts=80) for i in range(2)])
            hfr = Rot([k.sb(f"hf{i}", [2, 512], parts=80) for i in range(2)])
            glr2 = Rot([k.sb(f"gl{i}", [2, 512], parts=80) for i in range(2)])
            ymr = Rot([k.sb(f"ym{i}", [2, 512], BF16, parts=80) for i in range(2)])
            ctr = Rot([k.sb(f"ct{i}", [512], parts=80) for i in range(2)])
            pp = Rot([ps[i] for i in range(8)])
            bl = blocks(512)
            for g in range(8):
                k.dma("sp", xbg.ap, dr["xb"][2 * g:2 * g + 2].rearrange("c p t -> p c t"), writes=[xbg.b])
                for d in range(2):
                    order = bl if d == 0 else [bl[0]] + bl[:0:-1]
                    def stage1(t0, n, kind):
                        lo_reg, hi_reg = (0, CTX) if kind == 1 else (CTX, TT)
                        xc, xcb = xcr.next(), xcbr.next()
                        for jj in range(2):
                            c = 2 * g + jj
                            k.act(xc.ap[:, jj, 0:n], xbg.ap[:, jj, t0:t0 + n], AF.Identity, scale=S80("conv_w", (n_ * 4 + 1) * 16 + c),
                                  bias=S80("conv_b", n_ * 16 + c), R=[xbg.b, smt.b], W=[xc.b])
                            for j in (0, 2, 3):
                                o = j - 1
                                lo = max(t0, lo_reg - o)
                                hi = min(t0 + n, hi_reg - o)
                                k.stt("dve", xc.ap[:, jj, lo - t0:hi - t0], xbg.ap[:, jj, lo + o:hi + o], S80("conv_w", (n_ * 4 + j) * 16 + c),
                                      xc.ap[:, jj, lo - t0:hi - t0], ALU.mult, ALU.add, R=[xbg.b, xc.b, smt.b], W=[xc.b])
                        k.copy("act", xcb.ap[:, :, 0:n], xc.ap[:, :, 0:n], R=[xc.b], W=[xcb.b])
                        tr, ti, at, a2, bt, hcur = trr.next(), tir.next(), atr.next(), a2r.next(), btr.next(), hbr.next()
                        for jj in range(2):
                            c = 2 * g + jj
                            pr_, pi_ = pp.next(), pp.next()
                            for kk, pt in ((0, pr_), (1, pi_)):
                                for ii in range(2):
                                    blk = (((d * 2 + kk) * 8 + g) * 2 + ii) * 2 + jj
                                    k.mm(pt.ap[0:80, 0:n], gwt.ap[:, blk, :], xcb.ap[:, ii, 0:n], start=(ii == 0), stop=(ii == 1),
                                         R=[gwt.b, xcb.b], W=[pt.b])
                            k.act(tr.ap[:, jj, 0:n], pr_.ap[0:80, 0:n], AF.Tanh, scale=0.5, bias=hbt.ap[:, (d * 2 + 0) * 16 + c:(d * 2 + 0) * 16 + c + 1],
                                  R=[pr_.b, hbt.b], W=[tr.b])
                            k.act(ti.ap[:, jj, 0:n], pi_.ap[0:80, 0:n], AF.Tanh, scale=0.5, bias=hbt.ap[:, (d * 2 + 1) * 16 + c:(d * 2 + 1) * 16 + c + 1],
                                  R=[pi_.b, hbt.b], W=[ti.b])
                            k.act(at.ap[:, jj, 0:n], tr.ap[:, jj, 0:n], AF.Exp, scale=hcdt.ap[:, d * 16 + c:d * 16 + c + 1],
                                  bias=hcdt.ap[:, d * 16 + c:d * 16 + c + 1], R=[tr.b, hcdt.b], W=[at.b])
                            k.act(a2.ap[:, jj, 0:n], tr.ap[:, jj, 0:n], AF.Exp, scale=cdt.ap[:, d * 16 + c:d * 16 + c + 1],
                                  bias=cdt.ap[:, d * 16 + c:d * 16 + c + 1], R=[tr.b, cdt.b], W=[a2.b])
                        k.act(a2.ap[:, :, 0:n], a2.ap[:, :, 0:n], AF.Sqrt, scale=-1.0, bias=1.0, R=[a2.b], W=[a2.b])
                        return (t0, n, kind, xc, tr, ti, at, a2, bt, hcur)

                    def stage2(ctx_, prev):
                        t0, n, kind, xc, tr, ti, at, a2, bt, hcur = ctx_
                        k.stt("dve", ti.ap[:, :, 0:n], ti.ap[:, :, 0:n], 1.0, xc.ap[:, :, 0:n], ALU.add, ALU.mult, R=[ti.b, xc.b], W=[ti.b])
                        k.stt("dve", bt.ap[:, :, 0:n], ti.ap[:, :, 0:n], 0.5, a2.ap[:, :, 0:n], ALU.mult, ALU.mult, R=[ti.b, a2.b], W=[bt.b])
                        for jj in range(2):
                            if prev is None:
                                init = 0.0
                                rr = []
                            else:
                                ph, pn = prev
                                init = ph.ap[:, jj, pn - 1:pn] if d == 0 else ph.ap[:, jj, 0:1]
                                rr = [ph.b]
                            if d == 0:
                                k.scan(hcur.ap[:, jj, 0:n], at.ap[:, jj, 0:n], bt.ap[:, jj, 0:n], init, R=[at.b, bt.b] + rr, W=[hcur.b])
                            else:
                                k.scan(hcur.ap[:, jj, 0:n][:, ::-1], at.ap[:, jj, 0:n][:, ::-1], bt.ap[:, jj, 0:n][:, ::-1], init,
                                       R=[at.b, bt.b] + rr, W=[hcur.b])
                        dsl = lambda nm: dr[nm][2 * g:2 * g + 2, :, t0:t0 + n].rearrange("c p t -> p c t")
                        if d == 0:
                            k.dma("pool", dsl("hf"), hcur.ap[:, :, 0:n], reads=[hcur.b])
                        else:
                            hf, gl, ymt = hfr.next(), glr2.next(), ymr.next()
                            k.dma("sp", hf.ap[:, :, 0:n], dsl("hf"), writes=[hf.b])
                            k.dma("sp", gl.ap[:, :, 0:n], dsl("gl"), writes=[gl.b])
                            k.tt("dve", hf.ap[:, :, 0:n], hf.ap[:, :, 0:n], hcur.ap[:, :, 0:n], ALU.add, R=[hf.b, hcur.b], W=[hf.b])
                            k.tt("dve", ymt.ap[:, :, 0:n], hf.ap[:, :, 0:n], gl.ap[:, :, 0:n], ALU.mult, R=[hf.b, gl.b], W=[ymt.b])
                            k.dma("pool", dsl("ym"), ymt.ap[:, :, 0:n], reads=[ymt.b])
                        return (hcur, n)

                    prev = None
                    cur = stage1(*order[0])
                    for bi_ in range(len(order)):
                        nxt_ = stage1(*order[bi_ + 1]) if bi_ + 1 < len(order) else None
                        prev = stage2(cur, prev)
                        cur = nxt_
                    k.barrier(soft=True)
            k.barrier()
            outproj_phase(l, dr["ym"], f"lwo{l}", 80, 16)
            return done()

        PH["lru"] = lru_phase

        def ret_phase(l):
            rstop = int(os.environ.get("RET_STOP", "99"))
            gb = GB("rwq")
            bl = blocks(512)

            def std_tiles():
                xt = Rot([k.sb(f"x{i}", [8, 512]) for i in range(2)])
                hr = Rot([k.sb(f"h{i}", [8, 512], BF16) for i in range(2)])
                sq = Rot([k.sb(f"sq{i}", [512]) for i in range(2)])
                tmp = Rot([k.sb(f"tmp{i}", [512]) for i in range(2)])
                rs = k.sb("rs", [512])
                return xt, hr, sq, tmp, rs

            def wload(name, ng, gw):
                wt = k.sb("w_" + name, [ng, 8, gw], BF16)
                k.dma("sp", wt.ap, wbf[name].rearrange("(g p) (kc c) -> p g kc c", p=128, kc=8), reads=[gb], writes=[wt.b])
                return wt

            def rope(src, dst, n, cos, sin, mr_d, mr_p):
                for hh in range(4):
                    x1, x2 = src.ap[:, 2 * hh, 0:n], src.ap[:, 2 * hh + 1, 0:n]
                    m1, m4 = mr_d.next(), mr_d.next()
                    m2, m3 = mr_p.next(), mr_p.next()
                    k.tt("dve", m1.ap[:, 0:n], x1, cos.ap[:, 0:n], ALU.mult, R=[src.b, cos.b], W=[m1.b])
                    k.tt("dve", m2.ap[:, 0:n], x2, sin.ap[:, 0:n], ALU.mult, R=[src.b, sin.b], W=[m2.b])
                    k.tt("dve", dst.ap[:, 2 * hh, 0:n], m1.ap[:, 0:n], m2.ap[:, 0:n], ALU.subtract, R=[m1.b, m2.b], W=[dst.b])
                    k.tt("dve", m3.ap[:, 0:n], x1, sin.ap[:, 0:n], ALU.mult, R=[src.b, sin.b], W=[m3.b])
                    k.tt("dve", m4.ap[:, 0:n], x2, cos.ap[:, 0:n], ALU.mult, R=[src.b, cos.b], W=[m4.b])
                    k.tt("dve", dst.ap[:, 2 * hh + 1, 0:n], m3.ap[:, 0:n], m4.ap[:, 0:n], ALU.add, R=[m3.b, m4.b], W=[dst.b])

            fm = lambda nm, t0, n: dr[nm][:, :, t0:t0 + n].rearrange("c p t -> p c t")
            tm = lambda nm, t0, n: dr[nm][t0:t0 + n, :].rearrange("(n p) d -> p n d", p=128)

            for which in ("q", "k"):
                if SIM_ONLY == "rk" and which == "q":
                    continue
                wt = wload("rw" + which, 8, 128)
                xt, hr, sq, tmp, rs = std_tiles()
                csr = Rot([k.sb(f"cos{i}", [512]) for i in range(2)])
                snr = Rot([k.sb(f"sin{i}", [512]) for i in range(2)])
                qf = k.sb("qf", [8, 512])
                mr_d = Rot([k.sb(f"md{i}", [512]) for i in range(4)])
                mr_p = Rot([k.sb(f"mp{i}", [512]) for i in range(4)])
                qbr = Rot([k.sb(f"qb{i}", [8, 512], BF16) for i in range(2)])
                if which == "q":
                    qtab = k.sb("qtab", [2, RET_H * 512])
                    k.dma("sp", qtab.ap, dr["rqtab"].rearrange("p (a b) -> p a b", a=2), writes=[qtab.b])
                    qdfr = Rot([k.sb(f"qdf{i}", [8, 512], BF16) for i in range(2)])
                    qdbr = Rot([k.sb(f"qdb{i}", [8, 512], BF16) for i in range(2)])
                else:
                    kdfr = Rot([k.sb(f"kdf{i}", [4, 1024], BF16) for i in range(2)])
                    kdbr = Rot([k.sb(f"kdb{i}", [4, 1024], BF16) for i in range(2)])
                pp = Rot([ps[i] for i in range(1, 8)])
                for (t0, n, kind) in bl:
                    x, h = xt.next(), hr.next()
                    load_norm(dr["xs"], t0, n, kind, l, 1, x, h, sq, rs, tmp, ps[0])
                    for oc in range(8):
                        p = pp.next()
                        for kc in range(8):
                            k.mm(p.ap[:, 0:n], wt.ap[:, oc, kc, :], h.ap[:, kc, 0:n], start=(kc == 0), stop=(kc == 7), R=[wt.b, h.b], W=[p.b])
                        k.act(qf.ap[:, oc, 0:n], p.ap[:, 0:n], AF.Identity, scale=(1.0 if which == "q" else 0.0625), R=[p.b], W=[qf.b])
                    qb = qbr.next()
                    if kind == 0:
                        cos, sin = csr.next(), snr.next()
                        k.dma("sp", cos.ap[:, 0:n], dr["r_cos"][:, t0 - CTX:t0 - CTX + n], writes=[cos.b])
                        k.dma("sp", sin.ap[:, 0:n], dr["r_sin"][:, t0 - CTX:t0 - CTX + n], writes=[sin.b])
                        rope(qf, qb, n, cos, sin, mr_d, mr_p)
                    else:
                        k.copy("dve", qb.ap[:, 0:4, 0:n], qf.ap[:, 0:4, 0:n], R=[qf.b], W=[qb.b])
                        k.copy("act", qb.ap[:, 4:8, 0:n], qf.ap[:, 4:8, 0:n], R=[qf.b, qb.b], W=[qb.b])
                    k.dma("pool", fm("r" + which, t0, n), qb.ap[:, :, 0:n], reads=[qb.b])
                    if which == "q":
                        qdf, qdb = qdfr.next(), qdbr.next()
                        for hh in range(4):
                            for c in (2 * hh, 2 * hh + 1):
                                k.tt("dve", qdf.ap[:, c, 0:n], qb.ap[:, c, 0:n], qtab.ap[:, 0, hh * 512:hh * 512 + n], ALU.mult, R=[qb.b, qtab.b], W=[qdf.b])
                                k.tt("dve", qdb.ap[:, c, 0:n], qb.ap[:, c, 0:n], qtab.ap[:, 1, hh * 512:hh * 512 + n], ALU.mult, R=[qb.b, qtab.b], W=[qdb.b])
                        k.dma("pool", fm("rqdf", t0, n), qdf.ap[:, :, 0:n], reads=[qdf.b])
                        k.dma("pool", fm("rqdb", t0, n), qdb.ap[:, :, 0:n], reads=[qdb.b])
                    else:
                        kdf, kdb = kdfr.next(), kdbr.next()
                        rks = int(os.environ.get("RK_SKIP", "0"))
                        for ts_ in range(0 if rks == 1 else n // 128):
                            for cg in range(2):
                                p = pp.next()
                                pb = psbf[p]
                                for cc in range(4):
                                    k.tr(pb[:, cc * 128:(cc + 1) * 128], qb.ap[:, cg * 4 + cc, ts_ * 128:(ts_ + 1) * 128], ident_b.ap,
                                         R=[qb.b, ident_b.b], W=[p.b])
                                for h2 in range(2):
                                    hh = cg * 2 + h2
                                    k.act(kdf.ap[:, ts_, hh * 256:(hh + 1) * 256], pb[:, h2 * 256:(h2 + 1) * 256], AF.Identity,
                                          scale=S("r_kdf", hh, 1), R=[p.b, smt.b], W=[kdf.b])
                                for h2 in range(2):
                                    hh = cg * 2 + h2
                                    k.ts("dve", kdb.ap[:, ts_, hh * 256:(hh + 1) * 256], pb[:, h2 * 256:(h2 + 1) * 256], S("r_kdb", hh, 1), None,
                                         ALU.mult, R=[p.b, smt.b, kdf.b], W=[kdb.b])
                        if rks == 0:
                            k.dma("pool", tm("rkdf", t0, n), kdf.ap[:, 0:n // 128, :], reads=[kdf.b])
                            k.dma("pool", tm("rkdb", t0, n), kdb.ap[:, 0:n // 128, :], reads=[kdb.b])
                k.barrier()
                if rstop <= (0 if which == "q" else 1):
                    return True
            wv = wload("rwv", 4, 512)
            wg = wload("rwg", 16, 128)
            xt, hr, sq, tmp, rs = std_tiles()
            vt = k.sb("vt", [4, 2048], BF16)
            sgt = k.sb("sgt", [16, 512], BF16)
            pp = Rot([ps[i] for i in range(1, 8)])
            ev = 0
            for (t0, n, kind) in bl:
                x, h = xt.next(), hr.next()
                load_norm(dr["xs"], t0, n, kind, l, 1, x, h, sq, rs, tmp, ps[0])
                for ts_ in range(n // 128):
                    for vg in range(4):
                        p = pp.next()
                        for kc in range(8):
                            k.mm(p.ap, h.ap[:, kc, ts_ * 128:(ts_ + 1) * 128], wv.ap[:, vg, kc, :], start=(kc == 0), stop=(kc == 7),
                                 R=[h.b, wv.b], W=[p.b])
                        k.copy("act" if ev % 2 == 0 else "dve", vt.ap[:, ts_, vg * 512:(vg + 1) * 512], p.ap, R=[p.b], W=[vt.b])
                        ev += 1
                for oc in range(16):
                    p = pp.next()
                    for kc in range(8):
                        k.mm(p.ap[:, 0:n], wg.ap[:, oc, kc, :], h.ap[:, kc, 0:n], start=(kc == 0), stop=(kc == 7), R=[wg.b, h.b], W=[p.b])
                    k.act(sgt.ap[:, oc, 0:n], p.ap[:, 0:n], AF.Silu, R=[p.b], W=[sgt.b])
                k.dma("pool", tm("rv", t0, n), vt.ap[:, 0:n // 128, :], reads=[vt.b])
                k.dma("pool", fm("rsg", t0, n), sgt.ap[:, :, 0:n], reads=[sgt.b])
            k.barrier()
            if rstop <= 2:
                return True
            for sweep in range(2):
                St = k.sb("S", [2, 512])
                Sb = k.sb("Sb", [2, 512], BF16)
                qtr = Rot([k.sb(f"qt{i}", [2, 512], BF16) for i in range(2)])
                ktr = Rot([k.sb(f"kt{i}", [2, 512], BF16) for i in range(2)])
                qdr = Rot([k.sb(f"qd{i}", [2, 512], BF16) for i in range(2)])
                vtr = Rot([k.sb(f"vt{i}", [4, 512], BF16) for i in range(2)])
                kdr = Rot([k.sb(f"kd{i}", [4, 256], BF16) for i in range(2)])
                ptr_ = Rot([k.sb(f"pt{i}", [128], BF16) for i in range(2)])
                otr = Rot([k.sb(f"ot{i}", [4, 512]) for i in range(2)])
                o1r = Rot([k.sb(f"o1{i}", [4, 512]) for i in range(2)])
                sgr = Rot([k.sb(f"sg{i}", [4, 512], BF16) for i in range(2)])
                yrr = Rot([k.sb(f"yr{i}", [4, 512], BF16) for i in range(2)])
                sq = Rot([k.sb(f"sq{i}", [512]) for i in range(2)])
                rs = k.sb("rs", [512])
                o_ps = [ps[0], ps[1], ps[2], ps[3]]
                KV = [ps[5], ps[6]]
                sTr = Rot([ps[4], ps[7]])
                order = bl if sweep == 0 else [bl[0]] + bl[:0:-1]
                for hh in range(4):
                    k.memset("dve", St.ap, 0.0, W=[St.b])
                    k.memset("dve", Sb.ap, 0.0, W=[Sb.b])
                    hsl = lambda nm, t0, n: dr[nm][2 * hh:2 * hh + 2, :, t0:t0 + n].rearrange("c p t -> p c t")
                    osl = lambda nm, t0, n: dr[nm][4 * hh:4 * hh + 4, :, t0:t0 + n].rearrange("c p t -> p c t")
                    for (t0, n, kind) in order:
                        nch = n // 128
                        qd, vt_, kd = qdr.next(), vtr.next(), kdr.next()
                        k.dma("sp", qd.ap[:, :, 0:n], hsl("rqdf" if sweep == 0 else "rqdb", t0, n), writes=[qd.b])
                        k.dma("sp", vt_.ap[:, 0:nch, :], dr["rv"][t0:t0 + n, hh * 512:(hh + 1) * 512].rearrange("(n p) d -> p n d", p=128), writes=[vt_.b])
                        k.dma("sp", kd.ap[:, 0:nch, :], dr["rkdf" if sweep == 0 else "rkdb"][t0:t0 + n, hh * 256:(hh + 1) * 256].rearrange("(n p) d -> p n d", p=128),
                              writes=[kd.b])
                        if sweep == 0:
                            qt, kt = qtr.next(), ktr.next()
                            k.dma("sp", qt.ap[:, :, 0:n], hsl("rq", t0, n), writes=[qt.b])
                            k.dma("sp", kt.ap[:, :, 0:n], hsl("rk", t0, n), writes=[kt.b])
                        else:
                            o1, sg = o1r.next(), sgr.next()
                            k.dma("sp", o1.ap[:, :, 0:n], osl("ro", t0, n), writes=[o1.b])
                            k.dma("sp", sg.ap[:, :, 0:n], osl("rsg", t0, n), writes=[sg.b])
                        chs = range(nch) if sweep == 0 else range(nch - 1, -1, -1)
                        for ch in chs:
                            cs_ = slice(ch * 128, (ch + 1) * 128)
                            if sweep == 0:
                                sT = sTr.next()
                                for kc in range(2):
                                    k.mm(sT.ap[:, 0:128], kt.ap[:, kc, cs_], qt.ap[:, kc, cs_], start=(kc == 0), stop=(kc == 1), R=[kt.b, qt.b], W=[sT.b])
                                PT = ptr_.next()
                                k.tt("dve", PT.ap, sT.ap[:, 0:128], S("r_mask", hh * 128, 128), ALU.mult, R=[sT.b, smt.b], W=[PT.b])
                            for dvc in range(4):
                                if sweep == 0:
                                    k.mm(o_ps[dvc].ap[:, cs_], vt_.ap[:, ch, dvc * 128:(dvc + 1) * 128], PT.ap, start=True, stop=False,
                                         R=[vt_.b, PT.b], W=[o_ps[dvc].b])
                                for kc in range(2):
                                    k.mm(o_ps[dvc].ap[:, cs_], Sb.ap[:, kc, dvc * 128:(dvc + 1) * 128], qd.ap[:, kc, cs_],
                                         start=(sweep == 1 and kc == 0), stop=(kc == 1), R=[Sb.b, qd.b], W=[o_ps[dvc].b])
                            for kc in range(2):
                                k.mm(KV[kc].ap, kd.ap[:, ch, kc * 128:(kc + 1) * 128], vt_.ap[:, ch, :], R=[kd.b, vt_.b], W=[KV[kc].b])
                                k.stt("dve", St.ap[:, kc, :], St.ap[:, kc, :], S("r_cd", hh, 1), KV[kc].ap, ALU.mult, ALU.add,
                                      R=[St.b, KV[kc].b, smt.b], W=[St.b])
                                k.copy("act", Sb.ap[:, kc, :], St.ap[:, kc, :], R=[St.b], W=[Sb.b])
                        ot = otr.next()
                        if sweep == 0:
                            for dvc in range(4):
                                k.copy("act", ot.ap[:, dvc, 0:n], o_ps[dvc].ap[:, 0:n], R=[o_ps[dvc].b], W=[ot.b])
                            k.dma("pool", osl("ro", t0, n), ot.ap[:, :, 0:n], reads=[ot.b])
                        else:
                            yr = yrr.next()
                            for dvc in range(4):
                                k.tt("dve", ot.ap[:, dvc, 0:n], o1.ap[:, dvc, 0:n], o_ps[dvc].ap[:, 0:n], ALU.add, R=[o1.b, o_ps[dvc].b], W=[ot.b])
                                q_ = sq.next()
                                k.act(q_.ap[:, 0:n], ot.ap[:, dvc, 0:n], AF.Square, R=[ot.b], W=[q_.b])
                                k.mm(ps[4].ap[:, 0:n], ones_f.ap, q_.ap[:, 0:n], start=(dvc == 0), stop=(dvc == 3), R=[q_.b, ones_f.b], W=[ps[4].b])
                            k.rstd(rs.ap[:, 0:n], rs.b, ps[4].ap[:, 0:n], ps[4].b, 1.0 / 512)
                            for dvc in range(4):
                                k.tt("dve", ot.ap[:, dvc, 0:n], ot.ap[:, dvc, 0:n], rs.ap[:, 0:n], ALU.mult, R=[ot.b, rs.b], W=[ot.b])
                                k.tt("dve", yr.ap[:, dvc, 0:n], ot.ap[:, dvc, 0:n], sg.ap[:, dvc, 0:n], ALU.mult, R=[ot.b, sg.b], W=[yr.b])
                            k.dma("pool", osl("ryr", t0, n), yr.ap[:, :, 0:n], reads=[yr.b])
                k.barrier()
                if rstop <= 3 + sweep:
                    return True
            outproj_phase(l, dr["ryr"], "rwo", 128, 16)
            return done()

        PH["ret"] = ret_phase

        def dif_phase(l):
            dstop = int(os.environ.get("DIF_STOP", "99"))
            gb = GB("dwq")
            bl = blocks(512)
            fm = lambda nm, t0, n: dr[nm][:, :, t0:t0 + n].rearrange("c p t -> p c t")

            def wload(name, ng, gw):
                wt = k.sb("w_" + name, [ng, 8, gw], BF16)
                k.dma("sp", wt.ap, wbf[name].rearrange("(g p) (kc c) -> p g kc c", p=128, kc=8), reads=[gb], writes=[wt.b])
                return wt

            for which in ("q", "k"):
                w1 = wload("dw" + which, 8, 128)
                w2 = wload("dw" + which + "s", 8, 128)
                if which == "q":
                    wv = wload("dwv", 2, 512)
                    vt = k.sb("vt", [4, 1024], BF16)
                xt = Rot([k.sb(f"x{i}", [8, 512]) for i in range(2)])
                hr = Rot([k.sb(f"h{i}", [8, 512], BF16) for i in range(2)])
                sq = Rot([k.sb(f"sq{i}", [512]) for i in range(2)])
                tmp = Rot([k.sb(f"tmp{i}", [512]) for i in range(2)])
                rs = k.sb("rs", [512])
                csr = Rot([k.sb(f"cos{i}", [512]) for i in range(2)])
                snr = Rot([k.sb(f"sin{i}", [512]) for i in range(2)])
                t1r = Rot([k.sb(f"t1{i}", [512]) for i in range(3)])
                t2r = Rot([k.sb(f"t2{i}", [512]) for i in range(3)])
                qbr = Rot([k.sb(f"qb{i}", [8, 512], BF16) for i in range(2)])
                pp = Rot([ps[i] for i in range(1, 8)])
                ev = 0
                for (t0, n, kind) in bl:
                    x, h = xt.next(), hr.next()
                    load_norm(dr["xs"], t0, n, kind, l, 1, x, h, sq, rs, tmp, ps[0])
                    qb = qbr.next()
                    if kind == 0:
                        cos, sin = csr.next(), snr.next()
                        k.dma("sp", cos.ap[:, 0:n], dr["d_cos"][:, t0 - CTX:t0 - CTX + n], writes=[cos.b])
                        k.dma("sp", sin.ap[:, 0:n], dr["d_sin"][:, t0 - CTX:t0 - CTX + n], writes=[sin.b])
                    for oc in range(8):
                        p1 = pp.next()
                        for kc in range(8):
                            k.mm(p1.ap[:, 0:n], w1.ap[:, oc, kc, :], h.ap[:, kc, 0:n], start=(kc == 0), stop=(kc == 7), R=[w1.b, h.b], W=[p1.b])
                        if kind == 1:
                            k.copy("act", qb.ap[:, oc, 0:n], p1.ap[:, 0:n], R=[p1.b], W=[qb.b])
                            continue
                        p2 = pp.next()
                        for kc in range(8):
                            k.mm(p2.ap[:, 0:n], w2.ap[:, oc, kc, :], h.ap[:, kc, 0:n], start=(kc == 0), stop=(kc == 7), R=[w2.b, h.b], W=[p2.b])
                        t1, t2 = t1r.next(), t2r.next()
                        k.tt("dve", t1.ap[:, 0:n], p1.ap[:, 0:n], cos.ap[:, 0:n], ALU.mult, R=[p1.b, cos.b], W=[t1.b])
                        k.tt("dve", t2.ap[:, 0:n], p2.ap[:, 0:n], sin.ap[:, 0:n], ALU.mult, R=[p2.b, sin.b], W=[t2.b])
                        k.tt("dve", qb.ap[:, oc, 0:n], t1.ap[:, 0:n], t2.ap[:, 0:n], ALU.add, R=[t1.b, t2.b], W=[qb.b])
                    k.dma("pool", fm("d" + which, t0, n), qb.ap[:, :, 0:n], reads=[qb.b])
                    if which == "q":
                        for ts_ in range(n // 128):
                            for vg in range(2):
                                p = pp.next()
                                for kc in range(8):
                                    k.mm(p.ap, h.ap[:, kc, ts_ * 128:(ts_ + 1) * 128], wv.ap[:, vg, kc, :], start=(kc == 0), stop=(kc == 7),
                                         R=[h.b, wv.b], W=[p.b])
                                k.copy("act" if ev % 2 == 0 else "dve", vt.ap[:, ts_, vg * 512:(vg + 1) * 512], p.ap, R=[p.b], W=[vt.b])
                                ev += 1
                        k.dma("pool", dr["dv"][t0:t0 + n, :].rearrange("(n p) d -> p n d", p=128), vt.ap[:, 0:n // 128, :], reads=[vt.b])
                k.barrier()
            if dstop <= 0:
                return True
            NKT = TT // 128
            Khr = Rot([k.sb(f"Kh{i}", [TT], BF16) for i in range(2)])
            var = [k.sb(f"vaug{i}", [NKT, 129], BF16) for i in range(2)]
            for v_ in var:
                k.memset("dve", v_.ap, 1.0, W=[v_.b])
            qtr = Rot([k.sb(f"qt{i}", [512], BF16) for i in range(2)])
            p1r = Rot([k.sb(f"P1{i}", [512], BF16) for i in range(3)])
            p2r = Rot([k.sb(f"P2{i}", [512], BF16) for i in range(3)])
            rlr = Rot([k.sb(f"rl{i}", [8]) for i in range(4)])
            odr = Rot([k.sb(f"od{i}", [128]) for i in range(2)])
            jkr = Rot([k.sb(f"jk{i}", [128]) for i in range(2)])
            ytr = Rot([k.sb(f"yt{i}", [128], BF16) for i in range(2)])
            yfr = Rot([k.sb(f"yf{i}", [512], BF16) for i in range(2)])
            S1r = Rot([ps[0], ps[1]])
            S2r = Rot([ps[2], ps[3]])
            OB = [ps[4], ps[5], ps[6]]
            tp = ps[7]
            tpb = psbf[tp]
            for hh in range(8):
                Kh, vaug = Khr.next(), var[hh % 2]
                k.dma("sp", Kh.ap, dr["dk"][hh], writes=[Kh.b])
                k.dma("sp", vaug.ap[:, :, 0:128], dr["dv"][:, hh * 128:(hh + 1) * 128].rearrange("(n p) d -> p n d", p=128), writes=[vaug.b])
                for (t0, n, kind) in bl:
                    qt = qtr.next()
                    k.dma("sp", qt.ap[:, 0:n], dr["dq"][hh, :, t0:t0 + n], writes=[qt.b])
                    nkt = (CTX // 128) if kind == 1 else NKT
                    nqs = n // 128
                    for bank in OB:
                        k.mm(bank.ap, zeros_b.ap[:, 0:128], zeros_b.ap, start=True, stop=False, R=[zeros_b.b], W=[bank.b], skip_group_check=True)
                    def scores(kt_):
                        a_, b_ = S1r.next(), S2r.next()
                        k.mm(a_.ap[:, 0:n], Kh.ap[0:64, kt_ * 128:(kt_ + 1) * 128], qt.ap[0:64, 0:n], R=[Kh.b, qt.b], W=[a_.b])
                        k.mm(b_.ap[:, 0:n], Kh.ap[64:128, kt_ * 128:(kt_ + 1) * 128], qt.ap[64:128, 0:n], R=[Kh.b, qt.b], W=[b_.b])
                        return a_, b_

                    nxt = scores(0)
                    for kt in range(nkt):
                        s1, s2 = nxt
                        if kt + 1 < nkt:
                            nxt = scores(kt + 1)
                        P1, P2 = p1r.next(), p2r.next()
                        k.act(P1.ap[:, 0:n], s1.ap[:, 0:n], AF.Exp, scale=0.125, R=[s1.b], W=[P1.b])
                        k.act(P2.ap[:, 0:n], s2.ap[:, 0:n], AF.Exp, scale=0.125, R=[s2.b], W=[P2.b])
                        for qs in range(nqs):
                            for m, P in ((0, P1), (1, P2)):
                                ti = m * 4 + qs
                                bank, c0 = OB[ti // 3], (ti % 3) * 129
                                k.mm(bank.ap[:, c0:c0 + 129], P.ap[:, qs * 128:(qs + 1) * 128], vaug.ap[:, kt, :], start=False,
                                     stop=(kt == nkt - 1), R=[P.b, vaug.b], W=[bank.b], skip_group_check=True)
                    for qs in range(nqs):
                        b1, c1 = OB[qs // 3], (qs % 3) * 129
                        b2, c2 = OB[(4 + qs) // 3], ((4 + qs) % 3) * 129
                        rl, od, jk, yt = rlr.next(), odr.next(), jkr.next(), ytr.next()
                        k.op("dve", lambda e, rl=rl, b1=b1, c1=c1: e.reciprocal(out=rl.ap[:, 0:1], in_=b1.ap[:, c1 + 128:c1 + 129]), [b1.b], [rl.b])
                        k.op("dve", lambda e, rl=rl, b2=b2, c2=c2: e.reciprocal(out=rl.ap[:, 1:2], in_=b2.ap[:, c2 + 128:c2 + 129]), [b2.b, rl.b], [rl.b])
                        k.tt("dve", rl.ap[:, 2:3], rl.ap[:, 1:2], lamt.ap[:, 3:4], ALU.mult, R=[rl.b, lamt.b], W=[rl.b])
                        k.ts("dve", od.ap, b1.ap[:, c1:c1 + 128], rl.ap[:, 0:1], None, ALU.mult, R=[b1.b, rl.b], W=[od.b])
                        k.stt("dve", od.ap, b2.ap[:, c2:c2 + 128], rl.ap[:, 2:3], od.ap, ALU.mult, ALU.add, R=[b2.b, rl.b, od.b], W=[od.b])
                        k.act(jk.ap, od.ap, AF.Square, accum_out=rl.ap[:, 3:4], R=[od.b, rl.b], W=[jk.b, rl.b])
                        k.act(rl.ap[:, 4:5], rl.ap[:, 3:4], AF.Sqrt, bias=EPS, scale=1.0 / 128, R=[rl.b], W=[rl.b])
                        k.op("dve", lambda e, rl=rl: e.reciprocal(out=rl.ap[:, 5:6], in_=rl.ap[:, 4:5]), [rl.b], [rl.b])
                        k.stt("dve", yt.ap, od.ap, rl.ap[:, 5:6], gsub.ap, ALU.mult, ALU.mult, R=[od.b, rl.b, gsub.b], W=[yt.b])
                        k.tr(tpb[:, qs * 128:(qs + 1) * 128], yt.ap, ident_b.ap, R=[yt.b, ident_b.b], W=[tp.b])
                    yf = yfr.next()
                    k.copy("act", yf.ap[:, 0:n], tpb[:, 0:n], R=[tp.b], W=[yf.b])
                    k.dma("pool", dr["dya"][hh, :, t0:t0 + n], yf.ap[:, 0:n], reads=[yf.b])
            k.barrier()
            if dstop <= 1:
                return True
            outproj_phase(l, dr["dya"], "dwo", 128, 8)
            return done()

        PH["dif"] = dif_phase

        def final_phase():
            xt = Rot([k.sb(f"x{i}", [8, 1024]) for i in range(2)])
            sq = Rot([k.sb(f"sq{i}", [512]) for i in range(2)])
            rs = k.sb("rs", [1024])
            for i in range(SEQ // 1024):
                t0 = CTX + 1024 * i
                x = xt.next()
                k.dma("sp", x.ap, dr["xs"][:, t0:t0 + 1024].rearrange("(c p) t -> p c t", p=128), writes=[x.b])
                for sub in range(0, 1024, 512):
                    for c in range(8):
                        q = sq.next()
                        k.act(q.ap, x.ap[:, c, sub:sub + 512], AF.Square, R=[x.b], W=[q.b])
                        k.mm(ps[0].ap, ones_f.ap, q.ap, start=(c == 0), stop=(c == 7), R=[q.b, ones_f.b], W=[ps[0].b])
                    k.rstd(rs.ap[:, sub:sub + 512], rs.b, ps[0].ap, ps[0].b, 1.0 / D)
                    for c in range(8):
                        k.stt("dve", x.ap[:, c, sub:sub + 512], x.ap[:, c, sub:sub + 512], S("final_g", c, 1), rs.ap[:, sub:sub + 512],
                              ALU.mult, ALU.mult, R=[x.b, rs.b, smt.b], W=[x.b])
                k.dma("pool", dr["out"][:, 1024 * i:1024 * (i + 1)].rearrange("(c p) t -> p c t", p=128), x.ap, reads=[x.b])

        def program():
            if SIM_ONLY in ("rk",):
                ret_phase(1)
                return
            for l in range(DEPTH):
                ffn_phase(l, 0)
                if done():
                    return
                if l in MIXERS:
                    if MIXERS[l](l):
                        return
                ffn_phase(l, 1)
                if done():
                    return
            final_phase()

        MIXERS = {}
        if "lru" in PH:
            MIXERS[0] = PH["lru"]
            MIXERS[3] = PH["lru"]
        if "ret" in PH:
            MIXERS[1] = PH["ret"]
        if "dif" in PH:
            MIXERS[2] = PH["dif"]
        program()
        k.barrier(final=True)
        k.emit(esems, dsems)
    return nc


NONW = ("smalls", "mod_w", "r_cos", "r_sin", "d_cos", "d_sin", "rqtab")


def kernel(**inputs):
    inp = {k_: np.asarray(v) for k_, v in inputs.items()}
    sh, soff = prep_static(inp)
    wshapes = {k_: tuple(v.shape) for k_, v in sh.items()
               if k_ not in NONW}
    ns = sh["smalls"].shape[1]
    nc = build_program(soff, ns, wshapes)
    in_maps = []
    for b in range(NCORES):
        m = dict(sh)
        m["xin"] = np.ascontiguousarray(np.concatenate([inp["ctx"][b].T, inp["x"][b].T], axis=1), dtype=np.float32)
        m["cond"] = col128(inp["c"][b])
        in_maps.append(m)
    res = run_bass_kernel_spmd(nc, in_maps, core_ids=list(range(NCORES)))
    out = np.stack([np.ascontiguousarray(res.results[b]["out"].T) for b in range(NCORES)], axis=0)
    return out.astype(np.float32)
```
